# Optimizing a Trainium2 kernel written in Bass

```python
import jax, jax.numpy as jnp
from jax import lax
import numpy as np

D_MODEL = 1024
BATCH = 32
SEQ = 256
DEPTH = 2
DEC_BATCH = 4
DEC_SEQ = 2048
PAST_LEN = 256

GRID_W = 64
BLOCK = 128
WINDOW = 128
ROPE_BASE = 10000.0
NORM_EPS = 1e-6
NEG_INF = -1e30

N_BRANCH = 4
BRANCH_W = D_MODEL // N_BRANCH
HEAD_DIM = 64

MLA_HEADS = BRANCH_W // HEAD_DIM
MLA_NOPE = HEAD_DIM
MLA_ROPE = HEAD_DIM // 2
MLA_V = HEAD_DIM
MLA_Q_RANK = BRANCH_W
MLA_KV_RANK = BRANCH_W // 2
MLA_SCALE = (MLA_NOPE + MLA_ROPE) ** -0.5

RET_HEADS = BRANCH_W // HEAD_DIM
RET_DK = HEAD_DIM
RET_DV = HEAD_DIM

WIN_HEADS = BRANCH_W // HEAD_DIM
WIN_KV_HEADS = WIN_HEADS // 2
WIN_GROUP = WIN_HEADS // WIN_KV_HEADS

GQA_HEADS = BRANCH_W // HEAD_DIM
GQA_KV_HEADS = GQA_HEADS // 2
GQA_GROUP = GQA_HEADS // GQA_KV_HEADS

ATT_SCALE = HEAD_DIM ** -0.5
D_FF = 4 * D_MODEL
ALPHA = (2.0 * DEPTH) ** 0.25
BETA = (8.0 * DEPTH) ** -0.25

IN_SIZES = (MLA_Q_RANK, MLA_KV_RANK, MLA_ROPE,
            RET_HEADS * RET_DK, RET_HEADS * RET_DK, RET_HEADS * RET_DV, RET_HEADS * RET_DV,
            WIN_HEADS * HEAD_DIM, WIN_KV_HEADS * HEAD_DIM, WIN_KV_HEADS * HEAD_DIM,
            GQA_HEADS * HEAD_DIM, GQA_KV_HEADS * HEAD_DIM, GQA_KV_HEADS * HEAD_DIM,
            N_BRANCH * D_MODEL)
IN_DIM = sum(IN_SIZES)

kernel_name = 'hybrid_diffusion_parallel_mla_retention_window_qknorm'


def _rmsnorm(x, g):
    xf = x.astype(jnp.float32)
    y = xf * lax.rsqrt(jnp.mean(xf * xf, -1, keepdims=True) + NORM_EPS)
    return (y * g.astype(jnp.float32)).astype(x.dtype)


def _layernorm(x, g, b):
    xf = x.astype(jnp.float32)
    mu = jnp.mean(xf, -1, keepdims=True)
    var = jnp.mean(jnp.square(xf - mu), -1, keepdims=True)
    y = (xf - mu) * lax.rsqrt(var + NORM_EPS) * g.astype(jnp.float32) + b.astype(jnp.float32)
    return y.astype(x.dtype)


def _axial_rope(t, rot_dim):
    rows = t // GRID_W
    row = jnp.repeat(jnp.arange(rows, dtype=jnp.float32), GRID_W)
    col = (jnp.arange(t) % GRID_W).astype(jnp.float32)
    n_freq = rot_dim // 4
    inv = ROPE_BASE ** (-jnp.arange(n_freq, dtype=jnp.float32) / n_freq)
    ang = jnp.concatenate([row[:, None] * inv, col[:, None] * inv], axis=-1)
    return jnp.cos(ang), jnp.sin(ang)


def _apply_rope(x, cos, sin):
    half = x.shape[-1] // 2
    x1, x2 = x[..., :half], x[..., half:]
    c = cos[None, :, None, :].astype(x.dtype)
    s = sin[None, :, None, :].astype(x.dtype)
    return jnp.concatenate([x1 * c - x2 * s, x1 * s + x2 * c], axis=-1)


def _softmax(s, sink):
    if sink is None:
        return jax.nn.softmax(s, axis=-1)
    sink = sink.astype(jnp.float32)
    m = jnp.maximum(jnp.max(s, -1, keepdims=True), sink)
    e = jnp.exp(s - m)
    return e / (jnp.sum(e, -1, keepdims=True) + jnp.exp(sink - m))


def _dense_attention(q, k, v, sink=None):
    b, t, kh, g, dq = q.shape
    dv = v.shape[-1]
    nb = t // BLOCK
    qb = jnp.moveaxis(q.reshape(b, nb, BLOCK, kh, g, dq), 1, 0)
    sink_b = None if sink is None else sink[None, :, :, None, None]

    def one_block(qblk):
        s = jnp.einsum('bqkgd,bskd->bkgqs', qblk, k).astype(jnp.float32)
        p = _softmax(s, sink_b).astype(v.dtype)
        return jnp.einsum('bkgqs,bskd->bqkgd', p, v)

    o = lax.map(one_block, qb)
    return jnp.moveaxis(o, 0, 1).reshape(b, t, kh, g, dv)


def _banded_attention(q, k, v, k_ctx, v_ctx, sink):
    b, t, kh, g, d = q.shape
    nb = t // BLOCK
    pad = ((0, 0), (BLOCK, BLOCK), (0, 0), (0, 0))

    def band(a):
        ap = jnp.pad(a, pad).reshape(b, nb + 2, BLOCK, kh, a.shape[-1])
        return jnp.concatenate([ap[:, :-2], ap[:, 1:-1], ap[:, 2:]], axis=2)

    kb, vb = band(k), band(v)
    qb = q.reshape(b, nb, BLOCK, kh, g, d)
    qpos = jnp.arange(nb)[:, None] * BLOCK + jnp.arange(BLOCK)[None, :]
    kpos = jnp.arange(nb)[:, None] * BLOCK - BLOCK + jnp.arange(3 * BLOCK)[None, :]
    valid = ((jnp.abs(qpos[:, :, None] - kpos[:, None, :]) <= WINDOW)
             & (kpos[:, None, :] >= 0) & (kpos[:, None, :] < t))
    s_loc = jnp.einsum('bnqkgd,bnskd->bnkgqs', qb, kb).astype(jnp.float32)
    s_loc = jnp.where(valid[None, :, None, None], s_loc, NEG_INF)
    s_ctx = jnp.einsum('bnqkgd,bskd->bnkgqs', qb, k_ctx).astype(jnp.float32)
    p = _softmax(jnp.concatenate([s_loc, s_ctx], -1), sink[None, None, :, :, None, None]).astype(v.dtype)
    nl = 3 * BLOCK
    o = (jnp.einsum('bnkgqs,bnskd->bnqkgd', p[..., :nl], vb)
         + jnp.einsum('bnkgqs,bskd->bnqkgd', p[..., nl:], v_ctx))
    return o.reshape(b, t, kh, g, v.shape[-1])


def _retention_dir(q, k, v, log_gamma, s0, strict):
    b, t, h, dk = q.shape
    dv = v.shape[-1]
    nc = t // BLOCK
    lg = log_gamma.astype(jnp.float32)
    idx = jnp.arange(BLOCK, dtype=jnp.float32)
    diff = idx[:, None] - idx[None, :]
    mask = (diff > 0) if strict else (diff >= 0)
    dmat = jnp.where(mask[None], jnp.exp(jnp.maximum(diff, 0.0)[None] * lg[:, None, None]), 0.0)
    q_dec = jnp.exp((idx[:, None] + 1.0) * lg[None, :])
    k_dec = jnp.exp((BLOCK - 1.0 - idx)[:, None] * lg[None, :])
    c_dec = jnp.exp(BLOCK * lg)

    def chunks(a):
        return jnp.moveaxis(a.astype(jnp.float32).reshape(b, nc, BLOCK, h, a.shape[-1]), 1, 0)

    def step(state, inp):
        qc, kc, vc = inp
        att = jnp.einsum('bihd,bjhd->bhij', qc, kc) * dmat
        intra = jnp.einsum('bhij,bjhe->bihe', att, vc)
        inter = jnp.einsum('bihd,bhde->bihe', qc, state) * q_dec[None, :, :, None]
        state = (state * c_dec[None, :, None, None]
                 + jnp.einsum('bjhd,bjhe->bhde', kc * k_dec[None, :, :, None], vc))
        return state, intra + inter

    s_fin, o = lax.scan(step, s0.astype(jnp.float32), (chunks(q), chunks(k), chunks(v)))
    return jnp.moveaxis(o, 0, 1).reshape(b, t, h, dv), s_fin


def _bi_retention(q, k, v, lg_f, lg_b, s0_f, s0_b):
    o_f, s_f = _retention_dir(q, k, v, lg_f, s0_f, False)
    o_b, s_b = _retention_dir(q[:, ::-1], k[:, ::-1], v[:, ::-1], lg_b, s0_b, True)
    return o_f + o_b[:, ::-1], s_f, s_b


def _retention_out(o, gate, gain):
    mu = jnp.mean(o, -1, keepdims=True)
    var = jnp.mean(jnp.square(o - mu), -1, keepdims=True)
    y = ((o - mu) * lax.rsqrt(var + NORM_EPS)).reshape(o.shape[0], o.shape[1], -1) * gain.astype(jnp.float32)
    return jax.nn.silu(gate) * y.astype(gate.dtype)


def _project(h, lp):
    b, t, _ = h.shape
    split_at = np.cumsum(IN_SIZES)[:-1].tolist()
    (q_lat, kv_lat, k_pe, rq, rk, rv, rg, wq, wk, wv, gq, gk, gv, gates) = jnp.split(h @ lp['w_in'], split_at, axis=-1)
    qa = (_rmsnorm(q_lat, lp['mla_q_norm']) @ lp['mla_w_uq']).reshape(b, t, MLA_HEADS, MLA_NOPE + MLA_ROPE)
    return dict(
        a_q_nope=qa[..., :MLA_NOPE], a_q_pe=qa[..., MLA_NOPE:],
        a_ckv=_rmsnorm(kv_lat, lp['mla_kv_norm']), a_k_pe=k_pe,
        b_q=rq.reshape(b, t, RET_HEADS, RET_DK),
        b_k=rk.reshape(b, t, RET_HEADS, RET_DK) * (RET_DK ** -0.5),
        b_v=rv.reshape(b, t, RET_HEADS, RET_DV), b_g=rg,
        c_q=wq.reshape(b, t, WIN_HEADS, HEAD_DIM),
        c_k=wk.reshape(b, t, WIN_KV_HEADS, HEAD_DIM),
        c_v=wv.reshape(b, t, WIN_KV_HEADS, HEAD_DIM),
        d_q=_rmsnorm(gq.reshape(b, t, GQA_HEADS, HEAD_DIM), lp['gqa_q_norm']),
        d_k=_rmsnorm(gk.reshape(b, t, GQA_KV_HEADS, HEAD_DIM), lp['gqa_k_norm']),
        d_v=gv.reshape(b, t, GQA_KV_HEADS, HEAD_DIM),
        gates=gates)


def _mla_keys(ckv, k_pe, lp):
    b, s, _ = ckv.shape
    k_nope = (ckv @ lp['mla_w_uk']).reshape(b, s, MLA_HEADS, MLA_NOPE)
    v = (ckv @ lp['mla_w_uv']).reshape(b, s, MLA_HEADS, MLA_V)
    k = jnp.concatenate([k_nope, jnp.broadcast_to(k_pe[:, :, None, :], (b, s, MLA_HEADS, MLA_ROPE))], -1)
    return k, v


def _merge(outs, gates, lp):
    terms = [jax.nn.sigmoid(gates[..., i * D_MODEL:(i + 1) * D_MODEL]) * (o @ lp['w_branch'][i])
             for i, o in enumerate(outs)]
    return (terms[0] + terms[1] + terms[2] + terms[3]) @ lp['w_o']


def _context_mixer(h, lp):
    p = _project(h, lp)
    b, t, _ = h.shape
    ka, va = _mla_keys(p['a_ckv'], p['a_k_pe'], lp)
    qa = jnp.concatenate([p['a_q_nope'], p['a_q_pe']], -1) * MLA_SCALE
    o_a = _dense_attention(qa[:, :, :, None, :], ka, va)
    zeros = jnp.zeros((b, RET_HEADS, RET_DK, RET_DV), jnp.float32)
    o_b, s_f, s_b = _bi_retention(p['b_q'], p['b_k'], p['b_v'], jax.nn.log_sigmoid(lp['ret_decay_fwd']),
                                  jax.nn.log_sigmoid(lp['ret_decay_bwd']), zeros, zeros)
    o_b = _retention_out(o_b, p['b_g'], lp['ret_gn_gain'])
    sink = lp['win_sink'].reshape(WIN_KV_HEADS, WIN_GROUP)
    qc = p['c_q'].reshape(b, t, WIN_KV_HEADS, WIN_GROUP, HEAD_DIM) * ATT_SCALE
    o_c = _dense_attention(qc, p['c_k'], p['c_v'], sink)
    qd = p['d_q'].reshape(b, t, GQA_KV_HEADS, GQA_GROUP, HEAD_DIM) * ATT_SCALE
    o_d = _dense_attention(qd, p['d_k'], p['d_v'])
    y = _merge([o_a.reshape(b, t, -1), o_b, o_c.reshape(b, t, -1), o_d.reshape(b, t, -1)], p['gates'], lp)
    ctx_state = (p['a_ckv'], p['a_k_pe'], p['c_k'], p['c_v'], p['d_k'], p['d_v'],
                 s_f.astype(h.dtype), s_b.astype(h.dtype))
    return y, ctx_state


def _latent_mixer(h, lp, ckv_c, kpe_c, kc_c, vc_c, kd_c, vd_c, sf_c, sb_c):
    p = _project(h, lp)
    b, t, _ = h.shape
    cos_a, sin_a = _axial_rope(t, MLA_ROPE)
    cos_h, sin_h = _axial_rope(t, HEAD_DIM)
    q_pe = _apply_rope(p['a_q_pe'], cos_a, sin_a)
    k_pe = _apply_rope(p['a_k_pe'][:, :, None, :], cos_a, sin_a)[:, :, 0]
    ka, va = _mla_keys(jnp.concatenate([p['a_ckv'], ckv_c], 1), jnp.concatenate([k_pe, kpe_c], 1), lp)
    qa = jnp.concatenate([p['a_q_nope'], q_pe], -1) * MLA_SCALE
    o_a = _dense_attention(qa[:, :, :, None, :], ka, va)
    o_b, _, _ = _bi_retention(p['b_q'], p['b_k'], p['b_v'], jax.nn.log_sigmoid(lp['ret_decay_fwd']),
                              jax.nn.log_sigmoid(lp['ret_decay_bwd']), sf_c, sb_c)
    o_b = _retention_out(o_b, p['b_g'], lp['ret_gn_gain'])
    sink = lp['win_sink'].reshape(WIN_KV_HEADS, WIN_GROUP)
    qc = _apply_rope(p['c_q'], cos_h, sin_h).reshape(b, t, WIN_KV_HEADS, WIN_GROUP, HEAD_DIM) * ATT_SCALE
    kc = _apply_rope(p['c_k'], cos_h, sin_h)
    o_c = _banded_attention(qc, kc, p['c_v'], kc_c, vc_c, sink)
    qd = _apply_rope(p['d_q'], cos_h, sin_h).reshape(b, t, GQA_KV_HEADS, GQA_GROUP, HEAD_DIM) * ATT_SCALE
    kd = jnp.concatenate([_apply_rope(p['d_k'], cos_h, sin_h), kd_c], 1)
    vd = jnp.concatenate([p['d_v'], vd_c], 1)
    o_d = _dense_attention(qd, kd, vd)
    return _merge([o_a.reshape(b, t, -1), o_b, o_c.reshape(b, t, -1), o_d.reshape(b, t, -1)], p['gates'], lp)


def _layer(x, cond, lp, mixer_fn):
    mod = jax.nn.silu(cond) @ lp['w_ada'] + lp['b_ada']
    sh1, sc1, g1, sh2, sc2, g2 = jnp.split(mod[:, None, :], 6, axis=-1)
    y, extra = mixer_fn(x * (1 + sc1) + sh1)
    x = _layernorm(ALPHA * x + g1 * y, lp['ln1_g'], lp['ln1_b'])
    h = x * (1 + sc2) + sh2
    f = jnp.square(jax.nn.relu(h @ lp['w_up'])) @ lp['w_down']
    x = _layernorm(ALPHA * x + g2 * f, lp['ln2_g'], lp['ln2_b'])
    return x, extra


def setup_inputs(seed: int = 0) -> dict:
    key = jax.random.key(seed)
    ks = iter(jax.random.split(key, 40))

    def nrm(shape, scale):
        return scale * jax.random.normal(next(ks), shape, jnp.float32)

    base_logit = jnp.log(2.0 ** (5.0 + jnp.arange(RET_HEADS, dtype=jnp.float32)) - 1.0)
    return {
        'x_prompt': nrm((BATCH, SEQ, D_MODEL), 1.0),
        'x_sample': nrm((DEC_BATCH, DEC_SEQ, D_MODEL), 1.0),
        'cache_mla_ckv': nrm((DEC_BATCH, DEPTH, PAST_LEN, MLA_KV_RANK), 1.0),
        'cache_mla_kpe': nrm((DEC_BATCH, DEPTH, PAST_LEN, MLA_ROPE), 1.0),
        'cache_win_k': nrm((DEC_BATCH, DEPTH, PAST_LEN, WIN_KV_HEADS, HEAD_DIM), 1.0),
        'cache_win_v': nrm((DEC_BATCH, DEPTH, PAST_LEN, WIN_KV_HEADS, HEAD_DIM), 1.0),
        'cache_gqa_k': nrm((DEC_BATCH, DEPTH, PAST_LEN, GQA_KV_HEADS, HEAD_DIM), 1.0),
        'cache_gqa_v': nrm((DEC_BATCH, DEPTH, PAST_LEN, GQA_KV_HEADS, HEAD_DIM), 1.0),
        'state_ret_fwd': nrm((DEC_BATCH, DEPTH, RET_HEADS, RET_DK, RET_DV), 0.5),
        'state_ret_bwd': nrm((DEC_BATCH, DEPTH, RET_HEADS, RET_DK, RET_DV), 0.5),
        'c': nrm((DEC_BATCH, D_MODEL), 1.0),
        'c_ctx': nrm((D_MODEL,), 1.0),
        'w_ada': nrm((DEPTH, D_MODEL, 6 * D_MODEL), 0.5 * D_MODEL ** -0.5),
        'b_ada': nrm((DEPTH, 6 * D_MODEL), 0.1),
        'w_in': nrm((DEPTH, D_MODEL, IN_DIM), D_MODEL ** -0.5),
        'mla_q_norm': 1.0 + nrm((DEPTH, MLA_Q_RANK), 0.02),
        'mla_w_uq': nrm((DEPTH, MLA_Q_RANK, MLA_HEADS * (MLA_NOPE + MLA_ROPE)), MLA_Q_RANK ** -0.5),
        'mla_kv_norm': 1.0 + nrm((DEPTH, MLA_KV_RANK), 0.02),
        'mla_w_uk': nrm((DEPTH, MLA_KV_RANK, MLA_HEADS * MLA_NOPE), MLA_KV_RANK ** -0.5),
        'mla_w_uv': nrm((DEPTH, MLA_KV_RANK, MLA_HEADS * MLA_V), MLA_KV_RANK ** -0.5),
        'ret_decay_fwd': base_logit + nrm((DEPTH, RET_HEADS), 0.1),
        'ret_decay_bwd': base_logit + nrm((DEPTH, RET_HEADS), 0.1),
        'ret_gn_gain': 1.0 + nrm((DEPTH, RET_HEADS * RET_DV), 0.02),
        'win_sink': nrm((DEPTH, WIN_HEADS), 0.5),
        'gqa_q_norm': 1.0 + nrm((DEPTH, HEAD_DIM), 0.02),
        'gqa_k_norm': 1.0 + nrm((DEPTH, HEAD_DIM), 0.02),
        'w_branch': nrm((DEPTH, N_BRANCH, BRANCH_W, D_MODEL), BETA * BRANCH_W ** -0.5),
        'w_o': nrm((DEPTH, D_MODEL, D_MODEL), BETA * D_MODEL ** -0.5),
        'ln1_g': 1.0 + nrm((DEPTH, D_MODEL), 0.02),
        'ln1_b': nrm((DEPTH, D_MODEL), 0.02),
        'w_up': nrm((DEPTH, D_MODEL, D_FF), D_MODEL ** -0.5),
        'w_down': nrm((DEPTH, D_FF, D_MODEL), BETA * D_FF ** -0.5),
        'ln2_g': 1.0 + nrm((DEPTH, D_MODEL), 0.02),
        'ln2_b': nrm((DEPTH, D_MODEL), 0.02),
    }


def reference(x_prompt, x_sample, cache_mla_ckv, cache_mla_kpe, cache_win_k, cache_win_v, cache_gqa_k,
              cache_gqa_v, state_ret_fwd, state_ret_bwd, c, c_ctx, w_ada, b_ada, w_in, mla_q_norm, mla_w_uq,
              mla_kv_norm, mla_w_uk, mla_w_uv, ret_decay_fwd, ret_decay_bwd, ret_gn_gain, win_sink, gqa_q_norm,
              gqa_k_norm, w_branch, w_o, ln1_g, ln1_b, w_up, w_down, ln2_g, ln2_b):
    layers = [dict(w_ada=w_ada[l], b_ada=b_ada[l], w_in=w_in[l], mla_q_norm=mla_q_norm[l], mla_w_uq=mla_w_uq[l],
                   mla_kv_norm=mla_kv_norm[l], mla_w_uk=mla_w_uk[l], mla_w_uv=mla_w_uv[l],
                   ret_decay_fwd=ret_decay_fwd[l], ret_decay_bwd=ret_decay_bwd[l], ret_gn_gain=ret_gn_gain[l],
                   win_sink=win_sink[l], gqa_q_norm=gqa_q_norm[l], gqa_k_norm=gqa_k_norm[l],
                   w_branch=w_branch[l], w_o=w_o[l], ln1_g=ln1_g[l], ln1_b=ln1_b[l], w_up=w_up[l],
                   w_down=w_down[l], ln2_g=ln2_g[l], ln2_b=ln2_b[l]) for l in range(DEPTH)]

    xp = x_prompt
    ctx_states = []
    for l in range(DEPTH):
        lp = layers[l]
        xp, st = _layer(xp, c_ctx[None, :], lp, lambda hh, lp=lp: _context_mixer(hh, lp))
        ctx_states.append(st)
    y_prompt = xp
    new_mla_ckv = jnp.stack([s[0] for s in ctx_states], axis=1)
    new_mla_kpe = jnp.stack([s[1] for s in ctx_states], axis=1)
    new_win_k = jnp.stack([s[2] for s in ctx_states], axis=1)
    new_win_v = jnp.stack([s[3] for s in ctx_states], axis=1)
    new_gqa_k = jnp.stack([s[4] for s in ctx_states], axis=1)
    new_gqa_v = jnp.stack([s[5] for s in ctx_states], axis=1)
    new_ret_fwd = jnp.stack([s[6] for s in ctx_states], axis=1)
    new_ret_bwd = jnp.stack([s[7] for s in ctx_states], axis=1)

    xs = x_sample
    for l in range(DEPTH):
        lp = layers[l]

        def mix(hh, lp=lp, l=l):
            return _latent_mixer(hh, lp, cache_mla_ckv[:, l], cache_mla_kpe[:, l], cache_win_k[:, l],
                                 cache_win_v[:, l], cache_gqa_k[:, l], cache_gqa_v[:, l],
                                 state_ret_fwd[:, l], state_ret_bwd[:, l]), None

        xs, _ = _layer(xs, c, lp, mix)
    y_sample = xs

    return (y_prompt, y_sample, new_mla_ckv, new_mla_kpe, new_win_k, new_win_v, new_gqa_k, new_gqa_v,
            new_ret_fwd, new_ret_bwd)
```

```python
import contextlib
import itertools
import numpy as np
import ml_dtypes
import concourse.bass as bass
import concourse.mybir as mybir
from concourse.bass_utils import run_bass_kernel_spmd

F32 = mybir.dt.float32
BF16 = mybir.dt.bfloat16
AF = mybir.ActivationFunctionType
ALU = mybir.AluOpType
AX = mybir.AxisListType

T = 2048
NB = 16
BIG = 100.0
EPS = 1e-6
ALPHA = 4.0 ** 0.25
MLA_SCALE = 96.0 ** -0.5
ATT_SCALE = 0.125
SAME_ENGINE_SYNC = True
STAGE = None
DBG = {}


class K:
    def __init__(self, nc, es, n_dma_sems=8):
        self.nc = nc
        self.engs = {"pe": nc.tensor, "act": nc.scalar, "dve": nc.vector, "pool": nc.gpsimd, "sp": nc.sync}
        self.sem = {}
        self.cnt = {}
        for e in ("pe", "act", "dve", "pool"):
            self.sem[e] = es.enter_context(nc.semaphore("s_" + e))
            self.cnt[e] = 0
        self.dsem, self.dval, self.dnext = {}, {}, {}
        for q, ns in (("sp", n_dma_sems), ("pool", n_dma_sems), ("poolx", 24)):
            self.dsem[q] = [es.enter_context(nc.semaphore("d_%s%d" % (q, i))) for i in range(ns)]
            self.dval[q] = [0] * ns
            self.dnext[q] = 0
        self.engs["poolx"] = nc.gpsimd
        self.semobjs = {}
        for e, s in self.sem.items():
            self.semobjs[("e", e)] = s
        for q in self.dsem:
            for i, s in enumerate(self.dsem[q]):
                self.semobjs[("d", q, i)] = s
        self.waited = {e: {} for e in self.engs if e != "poolx"}
        self.waited["poolx"] = self.waited["pool"]
        self.lastw = {}
        self.reads = {}
        self.nwaits = 0
        self.ninst = {e: 0 for e in self.engs}
        self.ninst["poolx"] = 0

    def _deps(self, reads, writes, eng=None):
        deps = {}

        def add(ev):
            if ev is None:
                return
            sid, v = ev
            if deps.get(sid, 0) < v:
                deps[sid] = v
        for key in reads:
            add(self.lastw.get(key))
            if isinstance(key, str) and key[:2] in ("pf", "pb"):
                for sid, v in self.reads.get(key, {}).items():
                    if sid != ("e", eng):
                        add((sid, v))
        for key in writes:
            add(self.lastw.get(key))
            for sid, v in self.reads.get(key, {}).items():
                add((sid, v))
        return deps

    def _emit_waits(self, eng, deps):
        w = self.waited[eng]
        for sid, v in deps.items():
            if sid == ("e", eng) and (eng == "pe" or not SAME_ENGINE_SYNC):
                continue
            if sid == ("e", "pe"):
                assert self.cnt["pe"] >= v, "dependency on PE instruction without inc"
            if w.get(sid, 0) >= v:
                continue
            self.engs[eng].wait_ge(self.semobjs[sid], v)
            w[sid] = v
            self.nwaits += 1

    def _record(self, ev, reads, writes):
        sid, v = ev
        for key in reads:
            d = self.reads.setdefault(key, {})
            if d.get(sid, 0) < v:
                d[sid] = v
        for key in writes:
            self.lastw[key] = ev
            self.reads[key] = {}

    def op(self, eng, fn, reads=(), writes=(), inc=True):
        self._emit_waits(eng, self._deps(reads, writes, eng))
        ins = fn()
        self.ninst[eng] += 1
        if eng == "pe" and not inc:
            ev = (("e", "pe"), self.cnt["pe"] + 1)
        else:
            ins.then_inc(self.sem[eng], 1)
            self.cnt[eng] += 1
            ev = (("e", eng), self.cnt[eng])
        self._record(ev, reads, writes)
        return ins

    def dma(self, q, out, in_, reads=(), writes=()):
        i = self.dnext[q]
        self.dnext[q] = (i + 1) % len(self.dsem[q])
        sid = ("d", q, i)
        deps = self._deps(reads, writes, q)
        if self.dval[q][i] > 0:
            deps[sid] = max(deps.get(sid, 0), self.dval[q][i])
        self._emit_waits(q, deps)
        ins = self.engs[q].dma_start(out=out, in_=in_)
        self.dval[q][i] += 16
        ins.then_inc(self.dsem[q][i], 16)
        self._record((sid, self.dval[q][i]), reads, writes)
        self.ninst[q] += 1
        return ins

    def barrier(self, final=False):
        evs = {}
        for e in ("pe", "act", "dve", "pool"):
            if self.cnt[e] > 0:
                evs[("e", e)] = self.cnt[e]
        for q in self.dsem:
            if q == "poolx" and not final:
                continue
            for i in range(len(self.dsem[q])):
                if self.dval[q][i] > 0:
                    evs[("d", q, i)] = self.dval[q][i]
        for eng in ("pe", "act", "dve", "pool", "sp"):
            w = self.waited[eng]
            for sid, v in evs.items():
                if sid == ("e", eng):
                    continue
                if w.get(sid, 0) >= v:
                    continue
                self.engs[eng].wait_ge(self.semobjs[sid], v)
                w[sid] = v
                self.nwaits += 1
        self.lastw = {kk: v for kk, v in self.lastw.items() if isinstance(kk, tuple) and kk[0] == "wb"}
        self.reads = {}

    def finish(self):
        self.barrier(final=True)


W_NAMES = ['w_ada', 'b_ada', 'w_in', 'mla_q_norm', 'mla_w_uq', 'mla_kv_norm', 'mla_w_uk', 'mla_w_uv',
           'ret_decay_fwd', 'ret_decay_bwd', 'ret_gn_gain', 'win_sink', 'gqa_q_norm', 'gqa_k_norm',
           'w_branch', 'w_o', 'ln1_g', 'ln1_b', 'w_up', 'w_down', 'ln2_g', 'ln2_b']
W_SHAPES = dict(w_ada=(2, 1024, 6144), b_ada=(2, 6144), w_in=(2, 1024, 6560), mla_q_norm=(2, 256),
                mla_w_uq=(2, 256, 384), mla_kv_norm=(2, 128), mla_w_uk=(2, 128, 256), mla_w_uv=(2, 128, 256),
                ret_decay_fwd=(2, 4), ret_decay_bwd=(2, 4), ret_gn_gain=(2, 256), win_sink=(2, 4),
                gqa_q_norm=(2, 64), gqa_k_norm=(2, 64), w_branch=(2, 4, 256, 1024), w_o=(2, 1024, 1024),
                ln1_g=(2, 1024), ln1_b=(2, 1024), w_up=(2, 1024, 4096), w_down=(2, 4096, 1024),
                ln2_g=(2, 1024), ln2_b=(2, 1024))
IN_SPECS = [("xin", (2048, 1024), F32), ("cond", (1024,), F32), ("ropetab", (2048, 192), F32),
            ("indq", (8, 2048), BF16), ("indk", (8, 2304), BF16), ("indkc", (8, 2560), BF16),
            ("bandm", (2, 128, 256), BF16), ("keepfb", (32,), F32), ("rconst", (128, 770), F32),
            ("c_ckv", (2, 256, 128), F32), ("c_kpe", (2, 256, 32), F32), ("c_wk", (2, 256, 128), F32),
            ("c_wv", (2, 256, 128), F32), ("c_gk", (2, 256, 128), F32), ("c_gv", (2, 256, 128), F32),
            ("sf0", (2, 4, 64, 64), F32), ("sb0", (2, 4, 64, 64), F32)]
OUT_SPECS = [("y", (2048, 1024)), ("o_ckv", (2, 2048, 128)), ("o_kpe", (2, 2048, 32)), ("o_wk", (2, 2048, 128)),
             ("o_wv", (2, 2048, 128)), ("o_gk", (2, 2048, 128)), ("o_gv", (2, 2048, 128)),
             ("o_rf", (2, 8, 4, 64, 64)), ("o_rb", (2, 8, 4, 64, 64))]


def build():
    nc = bass.Bass("TRN2", target_bir_lowering=False)
    I = {}
    for name, shape, dt in IN_SPECS:
        I[name] = nc.dram_tensor(name, list(shape), dt, kind="ExternalInput").ap()
    for name in W_NAMES:
        I[name] = nc.dram_tensor(name, list(W_SHAPES[name]), F32, kind="ExternalInput").ap()
    O = {}
    for name, shape in OUT_SPECS:
        O[name] = nc.dram_tensor(name, list(shape), F32, kind="ExternalOutput").ap()
    dbg_out = {}
    WB = {}
    for name in ("w_in", "w_branch", "w_o", "w_up", "w_down"):
        WB[name] = nc.dram_tensor("wb_" + name, list(W_SHAPES[name]), BF16, kind="Internal").ap()

    with contextlib.ExitStack() as es:
        k = K(nc, es)

        sbn = [0]

        def sb(scope, name, shape, dt=F32):
            sbn[0] += 1
            return scope.enter_context(nc.sbuf_tensor("sb%d_%s" % (sbn[0], name), list(shape), dt))

        def MM(out, lhsT, rhs, start=True, stop=True, r=(), w=(), inc=False):
            return k.op("pe", lambda: nc.tensor.matmul(out, lhsT=lhsT, rhs=rhs, start=start, stop=stop),
                        reads=r, writes=w, inc=inc)

        def TR(out, in_, ident, r=(), w=(), inc=True):
            return k.op("pe", lambda: nc.tensor.transpose(out, in_, ident), reads=r, writes=w, inc=inc)

        def ACT(out, in_, func, r=(), w=(), bias=None, scale=None, accum=None):
            kw = {}
            if bias is not None:
                kw["bias"] = bias
                r = list(r) + ["cst"]
            if scale is not None:
                kw["scale"] = scale
            if accum is not None:
                kw["accum_out"] = accum
            return k.op("act", lambda: nc.scalar.activation(out=out, in_=in_, func=func, **kw), reads=r, writes=w)

        def E(eng):
            return {"dve": nc.vector, "pool": nc.gpsimd}[eng]

        def TT(eng, out, in0, in1, op, r=(), w=()):
            return k.op(eng, lambda: E(eng).tensor_tensor(out=out, in0=in0, in1=in1, op=op), reads=r, writes=w)

        def TS(eng, out, in0, s1, s2, op0, op1=None, r=(), w=()):
            if op1 is None:
                return k.op(eng, lambda: E(eng).tensor_scalar(out=out, in0=in0, scalar1=s1, scalar2=None, op0=op0),
                            reads=r, writes=w)
            return k.op(eng, lambda: E(eng).tensor_scalar(out=out, in0=in0, scalar1=s1, scalar2=s2, op0=op0, op1=op1),
                        reads=r, writes=w)

        def STT(eng, out, in0, scalar, in1, op0, op1, r=(), w=()):
            return k.op(eng, lambda: E(eng).scalar_tensor_tensor(out=out, in0=in0, scalar=scalar, in1=in1,
                                                                 op0=op0, op1=op1), reads=r, writes=w)

        def CP(eng, out, in_, r=(), w=()):
            if eng == "act":
                return k.op("act", lambda: nc.scalar.copy(out=out, in_=in_), reads=r, writes=w)
            return k.op(eng, lambda: E(eng).tensor_copy(out=out, in_=in_), reads=r, writes=w)

        def RED(out, in_, r=(), w=()):
            return k.op("dve", lambda: nc.vector.tensor_reduce(out=out, in_=in_, axis=AX.X, op=ALU.add),
                        reads=r, writes=w)

        def RCP(out, in_, r=(), w=()):
            return k.op("dve", lambda: nc.vector.reciprocal(out=out, in_=in_), reads=r, writes=w)

        def MSET(eng, ap, val, w=()):
            return k.op(eng, lambda: E(eng).memset(ap, val), writes=w)

        def DMA(q, out, in_, r=(), w=()):
            return k.dma(q, out, in_, reads=r, writes=w)

        def interleave(gens):
            gens = list(gens)
            while gens:
                for g_ in list(gens):
                    try:
                        next(g_)
                    except StopIteration:
                        gens.remove(g_)

        def interleave2(main, side, ratio):
            main_done = side is None
            side_done = side is None
            main_done = False
            while not main_done:
                try:
                    next(main)
                except StopIteration:
                    main_done = True
                if not side_done:
                    for _ in range(ratio):
                        try:
                            next(side)
                        except StopIteration:
                            side_done = True
                            break
            if not side_done:
                for _ in side:
                    pass

        def dump(name, ap, shape, dt=F32, r=()):
            d = nc.dram_tensor("dbg_" + name, list(shape), dt, kind="ExternalOutput").ap()
            dbg_out[name] = d
            DMA("sp", d, ap, r=r)

        pf = [es.enter_context(nc.psum_tensor("pf%d" % i, [128, 512], F32)) for i in range(6)]
        pb = [es.enter_context(nc.psum_tensor("pb%d" % i, [128, 1024], BF16)) for i in range(2)]

        P = es
        xT = sb(P, "xT", [128, 8, T], F32)
        hT = sb(P, "hT", [128, 8, T], BF16)
        ident_bf = sb(P, "ident_bf", [128, 128], BF16)
        ident_f = sb(P, "ident_f", [128, 128], F32)
        ones_div = sb(P, "ones_div", [128, 128], BF16)
        cst = sb(P, "cst", [128, 4], F32)
        modv = sb(P, "modv", [128, 2, 6, 8], F32)
        lnp = sb(P, "lnp", [128, 4, 2, 8], F32)
        der = sb(P, "der", [128, 2, 8, 8], F32)

        MSET("pool", ident_bf[:], 1.0, w=["ident_bf"])
        k.op("pool", lambda: nc.gpsimd.affine_select(out=ident_bf[:], in_=ident_bf[:], pattern=[[-1, 128]],
                                                     compare_op=ALU.is_equal, fill=0.0, base=0, channel_multiplier=1),
             reads=["ident_bf"], writes=["ident_bf"])
        MSET("pool", ident_f[:], 1.0, w=["ident_f"])
        k.op("pool", lambda: nc.gpsimd.affine_select(out=ident_f[:], in_=ident_f[:], pattern=[[-1, 128]],
                                                     compare_op=ALU.is_equal, fill=0.0, base=0, channel_multiplier=1),
             reads=["ident_f"], writes=["ident_f"])
        MSET("dve", ones_div[:], 1.0 / 1024.0, w=["ones_div"])
        MSET("dve", cst[:, 0:1], -BIG, w=["cst"])
        MSET("dve", cst[:, 1:2], EPS, w=["cst"])
        MSET("dve", cst[:, 2:3], 1.0, w=["cst"])
        MSET("dve", cst[:, 3:4], 0.0, w=["cst"])
        NEGBIG = cst[:, 0:1]
        EPSC = cst[:, 1:2]
        ONEC = cst[:, 2:3]
        esraw = sb(P, "esraw", [128, 8], F32)
        esink = sb(P, "esink", [128, 8], F32)
        DMA("sp", esraw[:], I["win_sink"].rearrange("a b -> (a b)").partition_broadcast(128), w=["esraw"])
        decraw = sb(P, "decraw", [128, 16], F32)
        lgall = sb(P, "lgall", [128, 16], F32)
        DMA("sp", decraw[:, 0:8], I["ret_decay_fwd"].rearrange("a b -> (a b)").partition_broadcast(128), w=["decraw"])
        DMA("sp", decraw[:, 8:16], I["ret_decay_bwd"].rearrange("a b -> (a b)").partition_broadcast(128), w=["decraw"])

        condT = sb(P, "condT", [128, 8], F32)
        ctmp = sb(P, "ctmp", [128, 8], F32)
        condS = sb(P, "condS", [128, 8], BF16)
        badaT = sb(P, "badaT", [128, 96], F32)
        SSTG = contextlib.ExitStack()
        stg1 = sb(SSTG, "stg1", [128, 128], F32)
        stg2 = sb(SSTG, "stg2", [128, 128], F32)
        MSET("dve", stg1[:], 0.0, w=["stg1"])
        MSET("dve", stg2[:], 0.0, w=["stg2"])
        for j, nm in enumerate(["ln1_g", "ln1_b", "ln2_g", "ln2_b"]):
            DMA("sp", stg1[j * 16:(j + 1) * 16, :], I[nm].rearrange("l (c p) -> (l c) p", p=128), w=["stg1"])
        DMA("sp", stg1[64:72, :], I["cond"].rearrange("(c p) -> c p", p=128), w=["stg1"])
        DMA("sp", stg2[0:96, :], I["b_ada"].rearrange("l (j p) -> (l j) p", p=128), w=["stg2"])
        TR(pf[1][:, 0:128], stg1[:], ident_f[:], r=["stg1", "ident_f"], w=["pf1"])
        CP("dve", lnp[:].rearrange("p j l c -> p (j l c)"), pf[1][:, 0:64], r=["pf1"], w=["lnp"])
        CP("dve", condT[:], pf[1][:, 64:72], r=["pf1"], w=["condT"])
        TR(pf[1][:, 128:256], stg2[:], ident_f[:], r=["stg2", "ident_f"], w=["pf1"])
        CP("dve", badaT[:], pf[1][:, 128:224], r=["pf1"], w=["badaT"])
        k.barrier()
        SSTG.close()
        ACT(ctmp[:], condT[:], AF.Exp, r=["condT"], w=["ctmp"], scale=-1.0)
        ACT(esink[:], esraw[:], AF.Exp, r=["esraw"], w=["esink"])
        ACT(lgall[:], decraw[:], AF.Exp, r=["decraw"], w=["lgall"], scale=-1.0)
        ACT(lgall[:], lgall[:], AF.Ln, r=["lgall"], w=["lgall"], bias=ONEC)
        TS("dve", lgall[:], lgall[:], -1.0, None, ALU.mult, r=["lgall"], w=["lgall"])
        TS("dve", ctmp[:], ctmp[:], 1.0, None, ALU.add, r=["ctmp"], w=["ctmp"])
        RCP(ctmp[:], ctmp[:], r=["ctmp"], w=["ctmp"])
        condSf = sb(P, "condSf", [128, 8], F32)
        TT("dve", condSf[:], condT[:], ctmp[:], ALU.mult, r=["condT", "ctmp"], w=["condS"])

        WBK = {}

        def precast(l_, names):
            for nm_ in names:
                keys = []
                if nm_ == "w_branch":
                    pieces = [(WB[nm_][l_].rearrange("i k n -> (i k) n"), I[nm_][l_].rearrange("i k n -> (i k) n"))]
                elif nm_ in ("w_in_qkv", "w_in_g"):
                    c0, c1 = (0, 2464) if nm_ == "w_in_qkv" else (2464, 6560)
                    pieces = [(WB["w_in"][l_][i_ * 512:(i_ + 1) * 512, c0:c1], I["w_in"][l_][i_ * 512:(i_ + 1) * 512, c0:c1])
                              for i_ in range(2)]
                else:
                    rows = W_SHAPES[nm_][1]
                    npc = 4 if nm_ == "w_down" else 2
                    step = rows // npc
                    pieces = [(WB[nm_][l_][i_ * step:(i_ + 1) * step, :], I[nm_][l_][i_ * step:(i_ + 1) * step, :])
                              for i_ in range(npc)]
                for i_, (dst_, src_) in enumerate(pieces):
                    key = ("wb", nm_, l_, i_)
                    keys.append(key)
                    DMA("poolx", dst_, src_, w=[key])
                WBK[(nm_, l_)] = keys

        def mod_gen(l, wada, bank):
            pm = pf[bank]
            pmk = "pf%d" % bank
            SW_ = wada[0].shape[2]
            for j in range(6144 // SW_):
                wt = wada[j % 2]
                wk = "wada%d" % (j % 2)
                DMA("sp", wt[:], I["w_ada"][l][:, j * SW_:(j + 1) * SW_].rearrange("(c p) n -> p c n", p=128), w=[wk])
                yield
                for ft in range(SW_ // 128):
                    col = j * (SW_ // 128) + ft
                    for c in range(8):
                        MM(pm[:, col:col + 1], wt[:, c, ft * 128:(ft + 1) * 128], condSf[:, c:c + 1],
                           start=(c == 0), stop=(c == 7), r=[wk, "condS"], w=[pmk], inc=(c == 7))
                    yield
            TT("dve", modv[:, l].rearrange("p j c -> p (j c)"), pm[:, 0:48], badaT[:, l * 48:(l + 1) * 48], ALU.add,
               r=[pmk, "badaT"], w=["modv"])
            TS("dve", der[:, l, 0, :], modv[:, l, 1, :], 1.0, None, ALU.add, r=["modv"], w=["der"])
            TS("dve", der[:, l, 1, :], modv[:, l, 4, :], 1.0, None, ALU.add, r=["modv"], w=["der"])
            yield
            TT("dve", der[:, l, 2, :], lnp[:, 0, l, :], der[:, l, 1, :], ALU.mult, r=["lnp", "der"], w=["der"])
            TT("dve", der[:, l, 3, :], lnp[:, 1, l, :], der[:, l, 1, :], ALU.mult, r=["lnp", "der"], w=["der"])
            TT("dve", der[:, l, 3, :], der[:, l, 3, :], modv[:, l, 3, :], ALU.add, r=["modv", "der"], w=["der"])
            yield
            TS("dve", der[:, l, 6, :], modv[:, l, 2, :], 0.5, None, ALU.mult, r=["modv"], w=["der"])
            CP("dve", der[:, l, 7, :], modv[:, l, 5, :], r=["modv"], w=["der"])
            if l == 1:
                TT("dve", der[:, 0, 4, :], lnp[:, 2, 0, :], der[:, 1, 0, :], ALU.mult, r=["lnp", "der"], w=["der"])
                TT("dve", der[:, 0, 5, :], lnp[:, 3, 0, :], der[:, 1, 0, :], ALU.mult, r=["lnp", "der"], w=["der"])
                TT("dve", der[:, 0, 5, :], der[:, 0, 5, :], modv[:, 1, 0, :], ALU.add, r=["modv", "der"], w=["der"])
            yield

        def x_gen(xs):
            for b in range(NB):
                xk = "xs%d" % (b % 2)
                DMA("sp", xs[b % 2][:], I["xin"][b * 128:(b + 1) * 128, :], w=[xk])
                for q in range(2):
                    bank = (2 * b + q) % 4
                    for cc in range(4):
                        c = 4 * q + cc
                        TR(pf[bank][:, cc * 128:(cc + 1) * 128], xs[b % 2][:, c * 128:(c + 1) * 128], ident_f[:],
                           r=[xk, "ident_f"], w=["pf%d" % bank], inc=(cc == 3))
                    CP("act" if q == 0 else "dve", xT[:, 4 * q:4 * q + 4, b * 128:(b + 1) * 128],
                       pf[bank][:].rearrange("p (c t) -> p c t", c=4), r=["pf%d" % bank], w=[("xT", b // 4)])
                    yield

        precast(0, ["w_in_qkv"])
        with contextlib.ExitStack() as S1:
            xs = [sb(S1, "xs%d" % i, [128, 1024], F32) for i in range(2)]
            wada0 = [sb(S1, "wadaS%d" % i, [128, 8, 512], F32) for i in range(2)]
            interleave([x_gen(xs), mod_gen(0, wada0, 4)])
            k.barrier()
        for g in range(4):
            for c in range(8):
                TS("dve" if c % 2 == 0 else "pool", hT[:, c, g * 512:(g + 1) * 512], xT[:, c, g * 512:(g + 1) * 512],
                   der[:, 0, 0, c:c + 1], modv[:, 0, 0, c:c + 1], ALU.mult, ALU.add,
                   r=[("xT", g), "der", "modv"], w=[("hT", g)])
        k.barrier()

        if STAGE == "x":
            dump("xT", xT[:], [128, 8, T], r=[("xT", g) for g in range(4)])
            dump("hT", hT[:], [128, 8, T], BF16, r=[("hT", g) for g in range(4)])
            dump("modv", modv[:], [128, 2, 6, 8], r=["modv"])
            k.finish()
            return nc, dbg_out

        def lastsl(X, a, b):
            if len(X.shape) == 3:
                return X[:, :, a:b]
            return X[:, :, :, a:b]

        def bcl(t_, X):
            ap = t_
            for _ in range(len(X.shape) - 2):
                ap = ap.unsqueeze(1)
            return ap.to_broadcast(list(X.shape[:-1]) + [t_.shape[-1]])

        def rope(eng, out, X, cc, ss, tmpA, tmpB, H, half, r, w, ktmp):
            D2 = 2 * half
            TT(eng, tmpA, X, bcl(cc, X), ALU.mult, r=r, w=[ktmp + "A"])
            TT(eng, lastsl(tmpB, 0, half), lastsl(X, half, D2), bcl(ss[:, 0:half], X), ALU.mult, r=r, w=[ktmp + "B"])
            TT(eng, lastsl(tmpB, half, D2), lastsl(X, 0, half), bcl(ss[:, half:D2], X), ALU.mult, r=r, w=[ktmp + "B"])
            TT(eng, out, tmpA, tmpB, ALU.add, r=[ktmp + "A", ktmp + "B"], w=w)

        def branch_A(l, OT):
            with contextlib.ExitStack() as S:
                KTA = sb(S, "KTA", [128, 4, 2304], BF16)
                VA = sb(S, "VA", [128, 18, 2, 192], BF16)
                QTA = [sb(S, "QTA%d" % i, [128, 4, 512], BF16) for i in range(2)]
                WinKV = sb(S, "WinKV", [128, 8, 160], BF16)
                WinQ = sb(S, "WinQ", [128, 8, 256], BF16)
                Wuq32 = sb(S, "Wuq32", [128, 2, 384], F32)
                Wuq = sb(S, "Wuq", [128, 2, 384], BF16)
                gq = sb(S, "gq", [128, 2], F32)
                Wukv = sb(S, "Wukv", [128, 512], BF16)
                gkv = sb(S, "gkv", [128, 128], F32)
                rt = [sb(S, "rtA%d" % i, [128, 192], F32) for i in range(2)]
                ssq = [sb(S, "ssqA%d" % i, [128, 1], F32) for i in range(2)]
                rs = [sb(S, "rsA%d" % i, [128, 2], F32) for i in range(2)]
                junk = [sb(S, "junkA%d" % i, [128, 256], BF16) for i in range(2)]
                ckv32 = [sb(S, "ckv32_%d" % i, [128, 128], F32) for i in range(2)]
                ckvb = [sb(S, "ckvb%d" % i, [128, 128], BF16) for i in range(2)]
                ckvT = [sb(S, "ckvT%d" % i, [128, 128], BF16) for i in range(2)]
                kpe32 = [sb(S, "kpe32_%d" % i, [128, 32], F32) for i in range(2)]
                kper = [sb(S, "kper%d" % i, [128, 32], F32) for i in range(2)]
                tmpA = [sb(S, "tmpAA%d" % i, [128, 4, 32], F32) for i in range(2)]
                tmpB = [sb(S, "tmpBA%d" % i, [128, 4, 32], F32) for i in range(2)]
                KA = [sb(S, "KA%d" % i, [128, 4, 96], BF16) for i in range(2)]
                qlb = [sb(S, "qlb%d" % i, [128, 256], BF16) for i in range(2)]
                qlT = [sb(S, "qlT%d" % i, [128, 2, 128], BF16) for i in range(2)]
                qs = [sb(S, "qs%d" % i, [128, 4, 96], F32) for i in range(2)]
                QA = [sb(S, "QA%d" % i, [128, 4, 96], BF16) for i in range(2)]
                PT = [sb(S, "PTA%d" % i, [128, 512], BF16) for i in range(3)]
                rcp = sb(S, "rcpA", [128, 512], F32)

                DMA("sp", WinKV[:], WB["w_in"][l][:, 256:416].rearrange("(c p) n -> p c n", p=128), r=WBK[("w_in_qkv", l)], w=["WinKV"])
                DMA("sp", WinQ[:], WB["w_in"][l][:, 0:256].rearrange("(c p) n -> p c n", p=128), r=WBK[("w_in_qkv", l)], w=["WinQ"])
                DMA("sp", rcp[:, 0:256], I["mla_w_uk"][l], w=["rcpA"])
                DMA("sp", rcp[:, 256:512], I["mla_w_uv"][l], w=["rcpA"])
                CP("dve", Wukv[:], rcp[:], r=["rcpA"], w=["Wukv"])
                DMA("sp", Wuq32[:], I["mla_w_uq"][l].rearrange("(c p) n -> p c n", p=128), w=["Wuq32"])
                with nc.allow_non_contiguous_dma(reason="tiny parameter vectors"):
                    DMA("sp", gq[:], I["mla_q_norm"][l].rearrange("(c p) -> p c", p=128), w=["gq"])
                DMA("sp", gkv[:], I["mla_kv_norm"][l].partition_broadcast(128), w=["gkv"])
                for c in range(2):
                    TS("dve", Wuq[:, c, :], Wuq32[:, c, :], gq[:, c:c + 1], None, ALU.mult, r=["Wuq32", "gq"], w=["Wuq"])
                MSET("pool", VA[:], 1.0, w=["VA"])
                for h in range(4):
                    DMA("sp", KTA[96:104, h, :], I["indk"], w=["KTA"])

                def blockA1(b):
                    s = b % 2
                    p1k, pkvk = "pf%d" % s, "pf%d" % (2 + s)
                    P1, Pkv = pf[s], pf[2 + s]
                    if b < 16:
                        DMA("sp", rt[s][:], I["ropetab"][b * 128:(b + 1) * 128, :], w=["rtA%d" % s])
                        yield
                        for c in range(8):
                            MM(P1[:, 0:160], hT[:, c, b * 128:(b + 1) * 128], WinKV[:, c, :], start=(c == 0), stop=(c == 7),
                               r=[("hT", b // 4), "WinKV"], w=[p1k], inc=(c == 7))
                        MSET("dve", ssq[s][:], 0.0, w=["ssqA%d" % s])
                        yield
                        ACT(junk[s][:, 0:128], P1[:, 0:128], AF.Square, r=[p1k], w=["junkA%d" % s, "ssqA%d" % s],
                            accum=ssq[s][:, 0:1])
                        yield
                        ACT(rs[s][:, 0:1], ssq[s][:, 0:1], AF.Ln, r=["ssqA%d" % s], w=["rsA%d" % s], scale=1.0 / 128.0,
                            bias=EPSC)
                        yield
                        ACT(rs[s][:, 0:1], rs[s][:, 0:1], AF.Exp, r=["rsA%d" % s], w=["rsA%d" % s], scale=-0.5)
                        yield
                        STT("dve", ckv32[s][:], P1[:, 0:128], rs[s][:, 0:1], gkv[:], ALU.mult, ALU.mult,
                            r=[p1k, "rsA%d" % s, "gkv"], w=["ckv32_%d" % s])
                        yield
                        DMA("sp", O["o_ckv"][l, b * 128:(b + 1) * 128, :], ckv32[s][:], r=["ckv32_%d" % s])
                        yield
                        CP("act", kpe32[s][:], P1[:, 128:160], r=[p1k], w=["kpe32_%d" % s])
                        yield
                        DMA("sp", O["o_kpe"][l, b * 128:(b + 1) * 128, :], kpe32[s][:], r=["kpe32_%d" % s])
                        yield
                        rope("dve", kper[s][:].unsqueeze(1), kpe32[s][:].unsqueeze(1), rt[s][:, 0:32], rt[s][:, 32:64],
                             tmpA[s][:, 0:1, :], tmpB[s][:, 0:1, :], 1, 16,
                             r=["kpe32_%d" % s, "rtA%d" % s], w=["kper%d" % s], ktmp="tmpA%d" % s)
                        yield
                    else:
                        j = b - 16
                        DMA("sp", ckv32[s][:], I["c_ckv"][l, j * 128:(j + 1) * 128, :], w=["ckv32_%d" % s])
                        yield
                        DMA("sp", kper[s][:], I["c_kpe"][l, j * 128:(j + 1) * 128, :], w=["kper%d" % s])
                        yield
                    CP("dve", ckvb[s][:], ckv32[s][:], r=["ckv32_%d" % s], w=["ckvb%d" % s])
                    yield
                    TR(pb[0][:, s * 128:(s + 1) * 128], ckvb[s][:], ident_bf[:], r=["ckvb%d" % s, "ident_bf"], w=["pb0"])
                    yield
                    CP("act", ckvT[s][:], pb[0][:, s * 128:(s + 1) * 128], r=["pb0"], w=["ckvT%d" % s])
                    yield
                    MM(Pkv[:, 0:512], ckvT[s][:], Wukv[:], r=["ckvT%d" % s, "Wukv"], w=[pkvk], inc=True)
                    yield
                    CP("act", KA[s][:, :, 0:64], Pkv[:, 0:256].rearrange("p (h d) -> p h d", h=4), r=[pkvk], w=["KA%d" % s])
                    yield
                    CP("dve", KA[s][:, :, 64:96], kper[s][:].unsqueeze(1).to_broadcast([128, 4, 32]),
                       r=["kper%d" % s], w=["KA%d" % s])
                    yield
                    for h in range(4):
                        TR(pb[1][0:96, s * 512 + h * 128:s * 512 + (h + 1) * 128], KA[s][:, h, :], ident_bf[:], r=["KA%d" % s, "ident_bf"],
                           w=["pb1"], inc=(h == 3))
                        yield
                    CP("dve", KTA[0:96, :, b * 128:(b + 1) * 128], pb[1][0:96, s * 512:(s + 1) * 512].rearrange("p (h t) -> p h t", h=4),
                       r=["pb1"], w=["KTA"])
                    yield
                    vv = Pkv[:, 256:512].rearrange("p (a b d) -> p a b d", a=2, b=2)
                    CP("act", VA[:, b, :, 0:64], vv[:, :, 0, :], r=[pkvk], w=["VA"])
                    yield
                    CP("dve", VA[:, b, :, 128:192], vv[:, :, 1, :], r=[pkvk], w=["VA"])
                    yield

                for b0 in range(0, 18, 2):
                    interleave([blockA1(b0), blockA1(b0 + 1)])

                ipt_box = [0]

                def prologueA(qg):
                    sl = qg % 2
                    qk = "QTA%d" % sl
                    for h in range(4):
                        DMA("sp", QTA[sl][96:104, h, :], I["indq"][:, qg * 512:(qg + 1) * 512], w=[qk])
                        yield
                    for bb in range(4):
                        b = qg * 4 + bb
                        s = b % 2
                        P1, Pq = pf[0], pf[1]
                        DMA("sp", rt[s][:], I["ropetab"][b * 128:(b + 1) * 128, :], w=["rtA%d" % s])
                        yield
                        for c in range(8):
                            MM(P1[:, 0:256], hT[:, c, b * 128:(b + 1) * 128], WinQ[:, c, :], start=(c == 0), stop=(c == 7),
                               r=[("hT", b // 4), "WinQ"], w=["pf0"], inc=(c == 7))
                            yield
                        MSET("dve", ssq[s][:], 0.0, w=["ssqA%d" % s])
                        yield
                        ACT(junk[s][:], P1[:, 0:256], AF.Square, r=["pf0"], w=["junkA%d" % s, "ssqA%d" % s],
                            accum=ssq[s][:, 0:1])
                        yield
                        ACT(rs[s][:, 0:1], ssq[s][:, 0:1], AF.Ln, r=["ssqA%d" % s], w=["rsA%d" % s], scale=1.0 / 256.0,
                            bias=EPSC)
                        yield
                        ACT(rs[s][:, 0:1], rs[s][:, 0:1], AF.Exp, r=["rsA%d" % s], w=["rsA%d" % s], scale=-0.5)
                        yield
                        TS("dve", rs[s][:, 1:2], rs[s][:, 0:1], MLA_SCALE, None, ALU.mult, r=["rsA%d" % s], w=["rsA%d" % s])
                        yield
                        CP("dve", qlb[s][:], P1[:, 0:256], r=["pf0"], w=["qlb%d" % s])
                        yield
                        for kc in range(2):
                            TR(pb[0][:, 128 + kc * 128:256 + kc * 128], qlb[s][:, kc * 128:(kc + 1) * 128], ident_bf[:],
                               r=["qlb%d" % s, "ident_bf"], w=["pb0"], inc=(kc == 1))
                            yield
                        CP("act", qlT[s][:], pb[0][:, 128:384].rearrange("p (c t) -> p c t", c=2), r=["pb0"],
                           w=["qlT%d" % s])
                        yield
                        for kc in range(2):
                            MM(Pq[:, 0:384], qlT[s][:, kc, :], Wuq[:, kc, :], start=(kc == 0), stop=(kc == 1),
                               r=["qlT%d" % s, "Wuq"], w=["pf1"], inc=(kc == 1))
                            yield
                        ACT(qs[s][:], Pq[:, 0:384].rearrange("p (h d) -> p h d", h=4), AF.Copy, r=["pf1", "rsA%d" % s],
                            w=["qs%d" % s], scale=rs[s][:, 1:2])
                        yield
                        CP("dve", QA[s][:, :, 0:64], qs[s][:, :, 0:64], r=["qs%d" % s], w=["QA%d" % s])
                        yield
                        rope("dve", QA[s][:, :, 64:96], qs[s][:, :, 64:96], rt[s][:, 0:32], rt[s][:, 32:64],
                             tmpA[s][:], tmpB[s][:], 4, 16, r=["qs%d" % s, "rtA%d" % s], w=["QA%d" % s], ktmp="tmpA%d" % s)
                        yield
                        for h in range(4):
                            TR(pb[0][0:96, 512 + h * 128:640 + h * 128], QA[s][:, h, :], ident_bf[:],
                               r=["QA%d" % s, "ident_bf"], w=["pb0"], inc=(h == 3))
                            yield
                        CP("act", QTA[sl][0:96, :, bb * 128:(bb + 1) * 128],
                           pb[0][0:96, 512:1024].rearrange("p (h t) -> p h t", h=4), r=["pb0"], w=[qk])
                        yield

                def attentionA(qg):
                    sl = qg % 2
                    qk = "QTA%d" % sl
                    tl = [(h, kt) for h in range(4) for kt in range(18)]

                    sbank = [(pf[2], "pf2"), (pf[3], "pf3"), (pb[1][:, 0:1024].bitcast(F32), "pb1")]

                    def qk_A(i):
                        h, kt = tl[i]
                        MM(sbank[i % 3][0][:, 0:512], KTA[0:104, h, kt * 128:(kt + 1) * 128], QTA[sl][0:104, h, :],
                           r=["KTA", qk], w=[sbank[i % 3][1]], inc=True)
                    qk_A(0)
                    qk_A(1)
                    for i, (h, kt) in enumerate(tl):
                        if i + 2 < len(tl):
                            qk_A(i + 2)
                        Oacc = pf[4 + (h % 2)]
                        ok = "pf%d" % (4 + (h % 2))
                        pt = PT[ipt_box[0] % 3]
                        ptk = "PTA%d" % (ipt_box[0] % 3)
                        ipt_box[0] += 1
                        ACT(pt[:], sbank[i % 3][0][:, 0:512], AF.Exp, r=[sbank[i % 3][1]], w=[ptk], bias=NEGBIG)
                        MM(Oacc[:, 0:512], VA[:, kt, h // 2, (h % 2) * 64:(h % 2) * 64 + 128], pt[:],
                           start=(kt == 0), stop=(kt == 17), r=["VA", ptk], w=[ok], inc=True)
                        yield
                        if kt == 17:
                            po = (h % 2) * 64
                            pss = 64 - po
                            RCP(rcp[pss:pss + 64, :], Oacc[pss:pss + 64, 0:512], r=[ok], w=["rcpA"])
                            TT("dve", OT[po:po + 64, 0, h // 2, qg * 512:(qg + 1) * 512], Oacc[po:po + 64, 0:512],
                               rcp[pss:pss + 64, :], ALU.mult, r=[ok, "rcpA"], w=[("OT", 0, qg)])

                for _ in prologueA(0):
                    pass
                for qg in range(4):
                    interleave2(attentionA(qg), prologueA(qg + 1) if qg + 1 < 4 else None, 2)
                k.barrier()

        def branch_CD(l, OT, isD):
            nm = "D" if isD else "C"
            br = 3 if isD else 2
            NT = 18 if isD else 20
            kcol = 2208 if isD else 1696
            qcol = 1952 if isD else 1440
            o_k, o_v = (O["o_gk"], O["o_gv"]) if isD else (O["o_wk"], O["o_wv"])
            c_k, c_v = (I["c_gk"], I["c_gv"]) if isD else (I["c_wk"], I["c_wv"])
            with contextlib.ExitStack() as S:
                KT = sb(S, "KT" + nm, [128, 2, NT * 128], BF16)
                VV = sb(S, "VV" + nm, [128, NT, 192], BF16)
                QW = 256 if isD else 128
                QT = [sb(S, "QT%s%d" % (nm, i), [128, 2, 2, QW], BF16) for i in range(2)]
                WinKV = sb(S, "WinKV" + nm, [128, 8, 256], BF16)
                WinQ = sb(S, "WinQ" + nm, [128, 8, 256], BF16)
                gains = sb(S, "gains" + nm, [128, 2, 64], F32)
                rt = [sb(S, "rt%s%d" % (nm, i), [128, 192], F32) for i in range(2)]
                sq = [sb(S, "sq%s%d" % (nm, i), [128, 256], F32) for i in range(2)]
                ssq = [sb(S, "ssq%s%d" % (nm, i), [128, 4], F32) for i in range(2)]
                rs = [sb(S, "rs%s%d" % (nm, i), [128, 4], F32) for i in range(2)]
                x32 = [sb(S, "x32%s%d" % (nm, i), [128, 4, 64], F32) for i in range(2)]
                v32 = [sb(S, "v32%s%d" % (nm, i), [128, 128], F32) for i in range(2)]
                tA = [sb(S, "tA%s%d" % (nm, i), [128, 4, 64], F32) for i in range(2)]
                tB = [sb(S, "tB%s%d" % (nm, i), [128, 4, 64], F32) for i in range(2)]
                xb = [sb(S, "xb%s%d" % (nm, i), [128, 256], BF16) for i in range(2)]
                PT = [sb(S, "PT%s%d" % (nm, i), [128, 512], BF16) for i in range(3)]
                rcp = sb(S, "rcp" + nm, [128, 512], F32)
                bandm = sb(S, "bandm" + nm, [128, 2, 256], BF16)
                K_ = lambda base, i: "%s%s%d" % (base, nm, i)
                modside = None
                if isD and l == 0:
                    wada1 = [sb(S, "wadaD%d" % i, [128, 8, 256], F32) for i in range(2)]
                    modside = mod_gen(1, wada1, 1)

                DMA("sp", WinKV[:], WB["w_in"][l][:, kcol:kcol + 256].rearrange("(c p) n -> p c n", p=128), r=WBK[("w_in_qkv", l)], w=["WinKV"])
                DMA("sp", WinQ[:], WB["w_in"][l][:, qcol:qcol + 256].rearrange("(c p) n -> p c n", p=128), r=WBK[("w_in_qkv", l)], w=["WinQ"])
                if isD:
                    DMA("sp", gains[:, 0, :], I["gqa_q_norm"][l].partition_broadcast(128), w=["gains"])
                    DMA("sp", gains[:, 1, :], I["gqa_k_norm"][l].partition_broadcast(128), w=["gains"])
                    TS("dve", gains[:, 0, :], gains[:, 0, :], ATT_SCALE, None, ALU.mult, r=["gains"], w=["gains"])
                else:
                    if STAGE != "C1":
                        for d_ in range(2):
                            DMA("sp", bandm[:, d_, :], I["bandm"][d_], w=["bandm"])
                MSET("pool", VV[:], 1.0, w=["VV"])
                MSET("pool", KT[:], 0.0, w=["KT"])
                for kp in range(2):
                    DMA("sp", KT[64:72, kp, :], I["indk"] if isD else I["indkc"], r=[], w=["KT"])

                def blockCD1(b):
                    s = b % 2
                    tile_i = b if isD else (b + 1 if b < 16 else b + 2)
                    P1 = pf[s]
                    p1k = "pf%d" % s
                    if b < 16:
                        DMA("sp", rt[s][:], I["ropetab"][b * 128:(b + 1) * 128, :], w=[K_("rt", s)])
                        yield
                        for c in range(8):
                            MM(P1[:, 0:256], hT[:, c, b * 128:(b + 1) * 128], WinKV[:, c, :], start=(c == 0), stop=(c == 7),
                               r=[("hT", b // 4), "WinKV"], w=[p1k], inc=(c == 7))
                        kview = P1[:, 0:128].rearrange("p (h d) -> p h d", h=2)
                        if isD:
                            ACT(sq[s][:, 0:128], P1[:, 0:128], AF.Square, r=[p1k], w=[K_("sq", s)])
                            yield
                            RED(ssq[s][:, 0:2], sq[s][:, 0:128].rearrange("p (h d) -> p h d", h=2), r=[K_("sq", s)],
                                w=[K_("ssq", s)])
                            yield
                            ACT(rs[s][:, 0:2], ssq[s][:, 0:2], AF.Ln, r=[K_("ssq", s)], w=[K_("rs", s)], scale=1.0 / 64.0,
                                bias=EPSC)
                            yield
                            ACT(rs[s][:, 0:2], rs[s][:, 0:2], AF.Exp, r=[K_("rs", s)], w=[K_("rs", s)], scale=-0.5)
                            yield
                            TT("dve", x32[s][:, 0:2, :], kview, rs[s][:, 0:2].unsqueeze(2).to_broadcast([128, 2, 64]), ALU.mult,
                               r=[p1k, K_("rs", s)], w=[K_("x32", s)])
                            yield
                            TT("dve", x32[s][:, 0:2, :], x32[s][:, 0:2, :],
                               gains[:, 1, :].unsqueeze(1).to_broadcast([128, 2, 64]), ALU.mult,
                               r=[K_("x32", s), "gains"], w=[K_("x32", s)])
                            yield
                        else:
                            CP("dve", x32[s][:, 0:2, :], kview, r=[p1k], w=[K_("x32", s)])
                            yield
                        DMA("sp", o_k[l, b * 128:(b + 1) * 128, :], x32[s][:, 0:2, :].rearrange("p h d -> p (h d)"),
                            r=[K_("x32", s)])
                        yield
                        CP("act", v32[s][:], P1[:, 128:256], r=[p1k], w=[K_("v32", s)])
                        yield
                        DMA("sp", o_v[l, b * 128:(b + 1) * 128, :], v32[s][:], r=[K_("v32", s)])
                        yield
                        rope("dve", xb[s][:, 0:128].rearrange("p (h d) -> p h d", h=2), x32[s][:, 0:2, :],
                             rt[s][:, 64:128], rt[s][:, 128:192], tA[s][:, 0:2, :], tB[s][:, 0:2, :], 2, 32,
                             r=[K_("x32", s), K_("rt", s)], w=[K_("xb", s)], ktmp=K_("t", s))
                        yield
                    else:
                        j = b - 16
                        DMA("sp", x32[s][:, 0:2, :].rearrange("p h d -> p (h d)"), c_k[l, j * 128:(j + 1) * 128, :],
                            w=[K_("x32", s)])
                        yield
                        DMA("sp", v32[s][:], c_v[l, j * 128:(j + 1) * 128, :], w=[K_("v32", s)])
                        yield
                        CP("dve", xb[s][:, 0:128], x32[s][:, 0:2, :].rearrange("p h d -> p (h d)"), r=[K_("x32", s)],
                           w=[K_("xb", s)])
                        yield
                    TR(pb[1][:, s * 128:(s + 1) * 128], xb[s][:, 0:128], ident_bf[:], r=[K_("xb", s), "ident_bf"], w=["pb1"])
                    yield
                    CP("act", KT[0:64, 0, tile_i * 128:(tile_i + 1) * 128], pb[1][0:64, s * 128:(s + 1) * 128], r=["pb1"], w=["KT"])
                    yield
                    CP("dve", KT[0:64, 1, tile_i * 128:(tile_i + 1) * 128], pb[1][64:128, s * 128:(s + 1) * 128], r=["pb1"], w=["KT"])
                    yield
                    CP("act", VV[:, tile_i, 0:64], v32[s][:, 0:64], r=[K_("v32", s)], w=["VV"])
                    yield
                    CP("dve", VV[:, tile_i, 128:192], v32[s][:, 64:128], r=[K_("v32", s)], w=["VV"])
                    yield

                for b0 in range(0, 18, 2):
                    interleave([blockCD1(b0), blockCD1(b0 + 1)])

                ipt = 0
                NG = 8 if isD else 16
                if STAGE == "C1":
                    NG = 0
                if STAGE == "C2":
                    NG = 2
                BPG = 2 if isD else 1
                ipt_box = [0]

                def prologueCD(qg):
                    sl = qg % 2
                    qk = K_("QT", sl)
                    for kp in range(2):
                        for g in range(2):
                            DMA("sp", QT[sl][64:72, kp, g, :], I["indq"][:, qg * QW:(qg + 1) * QW], w=[qk])
                            yield
                    for bb in range(BPG):
                        b = qg * BPG + bb
                        s = b % 2
                        P1 = pf[0]
                        DMA("sp", rt[s][:], I["ropetab"][b * 128:(b + 1) * 128, :], w=[K_("rt", s)])
                        yield
                        for c in range(8):
                            MM(P1[:, 0:256], hT[:, c, b * 128:(b + 1) * 128], WinQ[:, c, :], start=(c == 0), stop=(c == 7),
                               r=[("hT", b // 4), "WinQ"], w=["pf0"], inc=(c == 7))
                            yield
                        qview = P1[:, 0:256].rearrange("p (h d) -> p h d", h=4)
                        if isD:
                            ACT(sq[s][:], P1[:, 0:256], AF.Square, r=["pf0"], w=[K_("sq", s)])
                            yield
                            RED(ssq[s][:], sq[s][:].rearrange("p (h d) -> p h d", h=4), r=[K_("sq", s)], w=[K_("ssq", s)])
                            yield
                            ACT(rs[s][:], ssq[s][:], AF.Ln, r=[K_("ssq", s)], w=[K_("rs", s)], scale=1.0 / 64.0, bias=EPSC)
                            yield
                            ACT(rs[s][:], rs[s][:], AF.Exp, r=[K_("rs", s)], w=[K_("rs", s)], scale=-0.5)
                            yield
                            TT("dve", x32[s][:], qview, rs[s][:].unsqueeze(2).to_broadcast([128, 4, 64]), ALU.mult,
                               r=["pf0", K_("rs", s)], w=[K_("x32", s)])
                            yield
                            TT("dve", x32[s][:], x32[s][:], gains[:, 0, :].unsqueeze(1).to_broadcast([128, 4, 64]), ALU.mult,
                               r=[K_("x32", s), "gains"], w=[K_("x32", s)])
                            yield
                        else:
                            ACT(x32[s][:], qview, AF.Copy, r=["pf0"], w=[K_("x32", s)], scale=ATT_SCALE)
                            yield
                        rope("dve", xb[s][:].rearrange("p (g k d) -> p k g d", g=2, k=2),
                             x32[s][:].rearrange("p (k g) d -> p k g d", k=2), rt[s][:, 64:128], rt[s][:, 128:192],
                             tA[s][:].rearrange("p (k g) d -> p k g d", k=2), tB[s][:].rearrange("p (k g) d -> p k g d", k=2),
                             4, 32, r=[K_("x32", s), K_("rt", s)], w=[K_("xb", s)], ktmp=K_("t", s))
                        yield
                        for g in range(2):
                            TR(pb[0][:, g * 128:(g + 1) * 128], xb[s][:, g * 128:(g + 1) * 128], ident_bf[:],
                               r=[K_("xb", s), "ident_bf"], w=["pb0"], inc=(g == 1))
                            yield
                        tv = pb[0][:, 0:256].rearrange("p (g t) -> p g t", g=2)
                        CP("act", QT[sl][0:64, 0, :, bb * 128:(bb + 1) * 128], tv[0:64, :, :], r=["pb0"], w=[qk])
                        yield
                        CP("dve", QT[sl][0:64, 1, :, bb * 128:(bb + 1) * 128], tv[64:128, :, :], r=["pb0"], w=[qk])
                        yield

                def attentionCD(qg):
                    sl = qg % 2
                    qk = K_("QT", sl)
                    NW = 2 * QW
                    if isD:
                        tiles = [(kt, None) for kt in range(18)]
                    else:
                        tiles = [(qg, 0), (qg + 1, None), (qg + 2, 1), (18, None), (19, None)]
                    tl = [(kp, ti) for kp in range(2) for ti in range(len(tiles))]

                    sbank = [(pf[2], "pf2"), (pf[3], "pf3"), (pb[1][:, 0:1024].bitcast(F32), "pb1")]

                    def qk_CD(i):
                        kp, ti = tl[i]
                        kt = tiles[ti][0]
                        MM(sbank[i % 3][0][:, 0:NW], KT[0:72, kp, kt * 128:(kt + 1) * 128], QT[sl][0:72, kp, :, :],
                           start=True, stop=True, r=["KT", qk], w=[sbank[i % 3][1]], inc=True)
                    qk_CD(0)
                    qk_CD(1)
                    for i, (kp, ti) in enumerate(tl):
                        kt, bm = tiles[ti]
                        if i + 2 < len(tl):
                            qk_CD(i + 2)
                        Oacc = pf[4 + kp]
                        ok = "pf%d" % (4 + kp)
                        pk = sbank[i % 3][1]
                        pt = PT[ipt_box[0] % 3]
                        ptk = K_("PT", ipt_box[0] % 3)
                        ipt_box[0] += 1
                        ACT(pt[:, 0:NW], sbank[i % 3][0][:, 0:NW], AF.Exp, r=[pk], w=[ptk], bias=NEGBIG)
                        if bm is not None:
                            TT("dve", pt[:, 0:NW], pt[:, 0:NW], bandm[:, bm, :], ALU.mult, r=[ptk, "bandm"], w=[ptk])
                        MM(Oacc[:, 0:NW], VV[:, kt, kp * 64:kp * 64 + 128], pt[:, 0:NW],
                           start=(ti == 0), stop=(ti == len(tiles) - 1), r=["VV", ptk], w=[ok], inc=True)
                        yield
                        if ti != len(tiles) - 1:
                            continue
                        po = kp * 64
                        pss = 64 - po
                        if isD:
                            RCP(rcp[pss:pss + 64, 0:NW], Oacc[pss:pss + 64, 0:NW], r=[ok], w=["rcp"])
                        else:
                            for g in range(2):
                                h = 2 * kp + g
                                ACT(rcp[pss:pss + 64, g * QW:(g + 1) * QW], Oacc[pss:pss + 64, g * QW:(g + 1) * QW], AF.Ln,
                                    r=[ok, "esink"], w=["rcp"], bias=esink[pss:pss + 64, 4 * l + h:4 * l + h + 1])
                            ACT(rcp[pss:pss + 64, 0:NW], rcp[pss:pss + 64, 0:NW], AF.Exp, r=["rcp"], w=["rcp"], scale=-1.0)
                        for g in range(2):
                            TT("dve", OT[g * 64:(g + 1) * 64, br, kp, qg * QW:(qg + 1) * QW],
                               Oacc[po:po + 64, g * QW:(g + 1) * QW], rcp[pss:pss + 64, g * QW:(g + 1) * QW], ALU.mult,
                               r=[ok, "rcp"], w=[("OT", br, (qg * QW) // 512)])

                if NG > 0:
                    for _ in prologueCD(0):
                        pass
                def chain(*gs):
                    for g_ in gs:
                        if g_ is not None:
                            for _ in g_:
                                yield

                for qg in range(NG):
                    side = prologueCD(qg + 1) if qg + 1 < NG else None
                    if modside is not None and qg >= 1:

                        side = chain(side, itertools.islice(modside, 14))
                    interleave2(attentionCD(qg), side, 2 if isD else 3)
                if modside is not None:
                    for _ in modside:
                        pass
                k.barrier()

        def branch_B(l, OT):
            with contextlib.ExitStack() as S:
                QTB = sb(S, "QTB", [128, 2, T], BF16)
                KTB = sb(S, "KTB", [128, 2, T], BF16)
                VB = sb(S, "VB", [128, 16, 256], BF16)
                SG = sb(S, "SG", [128, 16, 256], BF16)
                Ust = sb(S, "Ust", [128, 16, 2, 2, 64], F32)
                Dcomb = sb(S, "Dcomb", [128, 4, 128], F32)
                rc = sb(S, "rc", [128, 770], F32)
                DMA("sp", rc[:], I["rconst"], w=["rc"])
                keepfb = sb(S, "keepfb", [128, 32], F32)
                DMA("sp", keepfb[:], I["keepfb"].partition_broadcast(128), w=["keepfb"])
                qdec = sb(S, "qdec", [128, 2, 2, 128], F32)
                kdec = sb(S, "kdec", [128, 8], F32)
                cdec = sb(S, "cdec", [128, 4], F32)
                car = sb(S, "car", [128, 2, 2, 16], F32)
                lgp = sb(S, "lgp", [128, 4], F32)
                gnb = sb(S, "gnb", [128, 256], F32)
                Sst = [sb(S, "Sst%d" % i, [128, 2, 64], F32) for i in range(2)]
                tmpD = sb(S, "tmpD", [128, 128], F32)
                for h in range(4):
                    cf = l * 4 + h
                    cb = 8 + l * 4 + h
                    ACT(Dcomb[:, h, :], rc[:, 0:128], AF.Exp, r=["rc", "lgall"], w=["Dcomb"], scale=lgall[:, cf:cf + 1])
                    TT("dve", Dcomb[:, h, :], Dcomb[:, h, :], rc[:, 128:256], ALU.mult, r=["Dcomb", "rc"], w=["Dcomb"])
                    ACT(tmpD[:], rc[:, 256:384], AF.Exp, r=["rc", "lgall"], w=["tmpD"], scale=lgall[:, cb:cb + 1])
                    TT("dve", tmpD[:], tmpD[:], rc[:, 384:512], ALU.mult, r=["tmpD", "rc"], w=["tmpD"])
                    TT("dve", Dcomb[:, h, :], Dcomb[:, h, :], tmpD[:], ALU.add, r=["Dcomb", "tmpD"], w=["Dcomb"])
                for d in range(2):
                    for pr in range(2):
                        c0 = d * 8 + l * 4 + 2 * pr
                        CP("dve", lgp[0:64, d * 2 + pr:d * 2 + pr + 1], lgall[0:64, c0:c0 + 1], r=["lgall"], w=["lgp"])
                        CP("dve", lgp[64:128, d * 2 + pr:d * 2 + pr + 1], lgall[64:128, c0 + 1:c0 + 2], r=["lgall"], w=["lgp"])
                for pr in range(2):
                    ACT(qdec[:, 0, pr, :], rc[:, 512:640], AF.Exp, r=["rc", "lgp"], w=["qdec"], scale=lgp[:, pr:pr + 1])
                    ACT(qdec[:, 1, pr, :], rc[:, 640:768], AF.Exp, r=["rc", "lgp"], w=["qdec"], scale=lgp[:, 2 + pr:3 + pr])
                ACT(kdec[:, 0:4], lgall[:, l * 4:l * 4 + 4], AF.Exp, r=["rc", "lgall"], w=["kdec"], scale=rc[:, 768:769])
                ACT(kdec[:, 4:8], lgall[:, 8 + l * 4:12 + l * 4], AF.Exp, r=["rc", "lgall"], w=["kdec"], scale=rc[:, 769:770])
                TS("dve", kdec[:], kdec[:], 0.125, None, ALU.mult, r=["kdec"], w=["kdec"])
                ACT(cdec[:], lgp[:], AF.Exp, r=["lgp"], w=["cdec"], scale=128.0)
                for d in range(2):
                    for pr in range(2):
                        TS("dve", car[:, d, pr, :], keepfb[:, d * 16:(d + 1) * 16], cdec[:, d * 2 + pr:d * 2 + pr + 1], None,
                           ALU.mult, r=["keepfb", "cdec"], w=["car"])
                DMA("sp", gnb[:], I["ret_gn_gain"][l].partition_broadcast(128), w=["gnb"])
                TS("dve", gnb[:], gnb[:], 1.0, None, ALU.mult, r=["gnb"], w=["gnb"])

                with contextlib.ExitStack() as SW:
                    WinQK = sb(SW, "WinQK", [128, 8, 512], BF16)
                    DMA("sp", WinQK[:], WB["w_in"][l][:, 416:928].rearrange("(c p) n -> p c n", p=128), r=WBK[("w_in_qkv", l)], w=["WinQK"])
                    ii = 0
                    for g4 in range(4):
                        for ct in range(4):
                            Pq = pf[ii % 2]
                            pqk = "pf%d" % (ii % 2)
                            ii += 1
                            for c in range(8):
                                MM(Pq[:, 0:512], WinQK[:, c, ct * 128:(ct + 1) * 128], hT[:, c, g4 * 512:(g4 + 1) * 512],
                                   start=(c == 0), stop=(c == 7), r=["WinQK", ("hT", g4)], w=[pqk], inc=(c == 7))
                            if ct < 2:
                                CP("act", QTB[:, ct, g4 * 512:(g4 + 1) * 512], Pq[:, 0:512], r=[pqk], w=[("QTB", g4)])
                            else:
                                ACT(KTB[:, ct - 2, g4 * 512:(g4 + 1) * 512], Pq[:, 0:512], AF.Copy, r=[pqk], w=[("KTB", g4)],
                                    scale=0.125)
                    k.barrier()
                with contextlib.ExitStack() as SW:
                    WinKVG = sb(SW, "WinKVG", [128, 8, 768], BF16)
                    KdF = [sb(SW, "KdF%d" % i, [128, 256], BF16) for i in range(2)]
                    KdB = [sb(SW, "KdB%d" % i, [128, 256], BF16) for i in range(2)]
                    Eg = [sb(SW, "Eg%d" % i, [128, 256], F32) for i in range(2)]
                    DMA("sp", WinKVG[:], WB["w_in"][l][:, 672:1440].rearrange("(c p) n -> p c n", p=128), r=WBK[("w_in_qkv", l)], w=["WinKVG"])
                    def inprojB(b):
                        s = b % 2
                        for c in range(8):
                            MM(pf[s][:, 0:512], hT[:, c, b * 128:(b + 1) * 128], WinKVG[:, c, 0:512], start=(c == 0), stop=(c == 7),
                               r=["WinKVG", ("hT", b // 4)], w=["pf%d" % s], inc=(c == 7))
                        for c in range(8):
                            MM(pf[2 + s][:, 0:256], hT[:, c, b * 128:(b + 1) * 128], WinKVG[:, c, 512:768], start=(c == 0),
                               stop=(c == 7), r=["WinKVG", ("hT", b // 4)], w=["pf%d" % (2 + s)], inc=(c == 7))
                    inprojB(0)
                    for b in range(NB):
                        s = b % 2
                        Pa, Pb, PU = pf[s], pf[2 + s], pf[4 + s]
                        puk = "pf%d" % (4 + s)
                        if b + 1 < NB:
                            inprojB(b + 1)
                        kv_ = Pa[:, 0:256].rearrange("p (h d) -> p h d", h=4)
                        TT("dve", KdF[s][:].rearrange("p (h d) -> p h d", h=4), kv_,
                           kdec[:, 0:4].unsqueeze(2).to_broadcast([128, 4, 64]), ALU.mult, r=["pf%d" % s, "kdec"], w=["KdF%d" % s])
                        TT("dve", KdB[s][:].rearrange("p (h d) -> p h d", h=4), kv_,
                           kdec[:, 4:8].unsqueeze(2).to_broadcast([128, 4, 64]), ALU.mult, r=["pf%d" % s, "kdec"], w=["KdB%d" % s])
                        CP("act", VB[:, b, :], Pa[:, 256:512], r=["pf%d" % s], w=[("VB", b)])
                        ACT(Eg[s][:], Pb[:, 0:256], AF.Exp, r=["pf%d" % (2 + s)], w=["Eg%d" % s], scale=-1.0)
                        ACT(Eg[s][:], Eg[s][:], AF.Ln, r=["Eg%d" % s], w=["Eg%d" % s], bias=ONEC)
                        ACT(Eg[s][:], Eg[s][:], AF.Exp, r=["Eg%d" % s], w=["Eg%d" % s], scale=-1.0)
                        TT("dve", SG[:, b, :], Pb[:, 0:256], Eg[s][:], ALU.mult, r=["pf%d" % (2 + s), "Eg%d" % s], w=[("SG", b)])
                        for d in range(2):
                            Kd = KdF[s] if d == 0 else KdB[s]
                            for pr in range(2):
                                j = d * 2 + pr
                                MM(PU[:, j * 128:(j + 1) * 128], Kd[:, pr * 128:(pr + 1) * 128], VB[:, b, pr * 128:(pr + 1) * 128],
                                   r=["KdF%d" % s, "KdB%d" % s, ("VB", b)], w=[puk], inc=(j == 3))
                        puv = PU[:, 0:512].rearrange("p (j e) -> p j e", j=4)
                        uv = Ust[:, b].rearrange("p d r e -> p (d r) e")
                        CP("act", uv[0:64, :, :], puv[0:64, :, 0:64], r=[puk], w=[("Ust", b)])
                        CP("dve", uv[64:128, :, :], puv[64:128, :, 64:128], r=[puk], w=[("Ust", b)])
                    k.barrier()

                Sbf = sb(S, "Sbf", [128, 2, 16, 2, 64], BF16)
                for d in range(2):
                    src = I["sf0"] if d == 0 else I["sb0"]
                    dst = O["o_rf"] if d == 0 else O["o_rb"]
                    cur = 0
                    for hp in range(2):
                        DMA("sp", Sst[cur][hp * 64:(hp + 1) * 64, :, :],
                            src[l].rearrange("(pr hp) dd e -> hp dd pr e", hp=2)[hp], w=["Sst%d" % cur])
                    order = range(16) if d == 0 else range(15, -1, -1)
                    for n in order:
                        nxt = 1 - cur
                        TS("dve", Sbf[:, d, n], Sst[cur][:], keepfb[:, d * 16 + n:d * 16 + n + 1], None, ALU.mult,
                           r=["Sst%d" % cur, "keepfb"], w=[("Sbf", n)])
                        for pr in range(2):
                            STT("dve", Sst[nxt][:, pr, :], Sst[cur][:, pr, :], car[:, d, pr, n:n + 1], Ust[:, n, d, pr, :],
                                ALU.mult, ALU.add, r=["Sst%d" % cur, "car", ("Ust", n)], w=["Sst%d" % nxt])
                        if (d == 0 and n % 2 == 1) or (d == 1 and n % 2 == 0):
                            for hp in range(2):
                                DMA("sp", dst[l, n // 2].rearrange("(pr hp) dd e -> hp dd pr e", hp=2)[hp],
                                    Sst[nxt][hp * 64:(hp + 1) * 64, :, :], r=["Sst%d" % nxt])
                        cur = nxt

                with contextlib.ExitStack() as S2:
                    AT = [sb(S2, "AT%d" % i, [128, 4, 128], BF16) for i in range(2)]
                    Qd = [sb(S2, "Qd%d" % i, [128, 2, 2, 128], BF16) for i in range(2)]
                    sqo = [sb(S2, "sqo%d" % i, [128, 256], F32) for i in range(2)]
                    st = [sb(S2, "st%d" % i, [128, 16], F32) for i in range(2)]
                    tt = [sb(S2, "tt%d" % i, [128, 4, 64], F32) for i in range(2)]
                    OB = [sb(S2, "OB%d" % i, [128, 256], BF16) for i in range(2)]
                    def attB(n):
                        s = n % 2
                        cs = slice(n * 128, (n + 1) * 128)
                        for h in range(4):
                            hp, pr = h % 2, h // 2
                            MM(pf[4 * s + hp][:, pr * 128:(pr + 1) * 128], KTB[hp * 64:(hp + 1) * 64, pr, cs],
                               QTB[hp * 64:(hp + 1) * 64, pr, cs],
                               r=[("KTB", n // 4), ("QTB", n // 4)], w=["pf%d" % (4 * s + hp)], inc=(h >= 2))
                    attB(0)
                    for n in range(NB):
                        s = n % 2
                        Po = pf[2 + s]
                        pok = "pf%d" % (2 + s)
                        cs = slice(n * 128, (n + 1) * 128)
                        if n + 1 < NB:
                            attB(n + 1)
                        for hp in range(2):
                            TT("dve", AT[s][:].rearrange("p (a b) i -> p a b i", b=2)[:, :, hp, :],
                               pf[4 * s + hp][:, 0:256].rearrange("p (a i) -> p a i", a=2),
                               Dcomb[:].rearrange("p (a b) i -> p a b i", b=2)[:, :, hp, :], ALU.mult,
                               r=["pf%d" % (4 * s + hp), "Dcomb"], w=["AT%d" % s])
                        for d in range(2):
                            TT("pool", Qd[s][:, d], QTB[:, :, cs], qdec[:, d], ALU.mult, r=[("QTB", n // 4), "qdec"],
                               w=["Qd%d" % s])
                        for h in range(4):
                            hp, pr = h % 2, h // 2
                            ps_ = slice(hp * 64, (hp + 1) * 64)
                            MM(Po[:, h * 64:(h + 1) * 64], AT[s][:, h, :], VB[:, n, h * 64:(h + 1) * 64], start=True, stop=False,
                               r=["AT%d" % s, ("VB", n)], w=[pok])
                            MM(Po[:, h * 64:(h + 1) * 64], Qd[s][ps_, 0, pr, :], Sbf[ps_, 0, n, pr, :], start=False, stop=False,
                               r=["Qd%d" % s, ("Sbf", n)], w=[pok])
                            MM(Po[:, h * 64:(h + 1) * 64], Qd[s][ps_, 1, pr, :], Sbf[ps_, 1, n, pr, :], start=False, stop=True,
                               r=["Qd%d" % s, ("Sbf", n)], w=[pok], inc=(h == 3))
                        pov = Po[:, 0:256].rearrange("p (h e) -> p h e", h=4)
                        stk = "st%d" % s
                        RED(st[s][:, 0:4], pov, r=[pok], w=[stk])
                        ACT(sqo[s][:], Po[:, 0:256], AF.Square, r=[pok], w=["sqo%d" % s])
                        RED(st[s][:, 4:8], sqo[s][:].rearrange("p (h e) -> p h e", h=4), r=["sqo%d" % s], w=[stk])
                        TS("dve", st[s][:, 0:4], st[s][:, 0:4], 1.0 / 64.0, None, ALU.mult, r=[stk], w=[stk])
                        TT("dve", st[s][:, 8:12], st[s][:, 0:4], st[s][:, 0:4], ALU.mult, r=[stk], w=[stk])
                        STT("dve", st[s][:, 12:16], st[s][:, 4:8], 1.0 / 64.0, st[s][:, 8:12], ALU.mult, ALU.subtract,
                            r=[stk], w=[stk])
                        ACT(st[s][:, 12:16], st[s][:, 12:16], AF.Ln, r=[stk], w=[stk], bias=EPSC)
                        ACT(st[s][:, 12:16], st[s][:, 12:16], AF.Exp, r=[stk], w=[stk], scale=-0.5)
                        TT("dve", tt[s][:], pov, st[s][:, 0:4].unsqueeze(2).to_broadcast([128, 4, 64]), ALU.subtract,
                           r=[pok, stk], w=["tt%d" % s])
                        TT("dve", tt[s][:], tt[s][:], st[s][:, 12:16].unsqueeze(2).to_broadcast([128, 4, 64]), ALU.mult,
                           r=["tt%d" % s, stk], w=["tt%d" % s])
                        TT("dve", tt[s][:], tt[s][:], gnb[:].rearrange("p (h e) -> p h e", h=4), ALU.mult,
                           r=["tt%d" % s, "gnb"], w=["tt%d" % s])
                        TT("dve", OB[s][:], tt[s][:].rearrange("p h e -> p (h e)"), SG[:, n, :], ALU.mult,
                           r=["tt%d" % s, ("SG", n)], w=["OB%d" % s])
                        for pr in range(2):
                            TR(pb[0][:, pr * 128:(pr + 1) * 128], OB[s][:, pr * 128:(pr + 1) * 128], ident_bf[:],
                               r=["OB%d" % s, "ident_bf"], w=["pb0"], inc=(pr == 1))
                        CP("act", OT[:, 1, :, cs], pb[0][:, 0:256].rearrange("p (r t) -> p r t", r=2), r=["pb0"],
                           w=[("OT", 1, n // 4)])
                    k.barrier()

        def ln_finalize_gen(l, which, tok, Pmean, Pex2, mk, ek, LT, write_h):
            (msq, kmsq), (rstd, krstd), (nmr, knmr), tt = LT
            gi, bi = (0, 1) if which == 1 else (2, 3)
            ACT(msq[:], Pmean[:, 0:512], AF.Square, r=[mk], w=[kmsq])
            yield
            TT("dve", rstd[:], Pex2[:, 0:512], msq[:], ALU.subtract, r=[ek, kmsq], w=[krstd])
            yield
            ACT(rstd[:], rstd[:], AF.Ln, r=[krstd], w=[krstd], bias=EPSC)
            yield
            ACT(rstd[:], rstd[:], AF.Exp, r=[krstd], w=[krstd], scale=-0.5)
            yield
            STT("dve", nmr[:], Pmean[:, 0:512], -1.0, rstd[:], ALU.mult, ALU.mult, r=[mk, krstd], w=[knmr])
            yield
            g4 = tok.start // 512
            for fo in range(8):
                t_, tk = tt[fo % 2]
                TT("dve", t_[:], xT[:, fo, tok], rstd[:], ALU.mult, r=[("xT", g4), krstd], w=[tk])
                yield
                TT("pool", t_[:], t_[:], nmr[:], ALU.add, r=[tk, knmr], w=[tk])
                yield
                ACT(xT[:, fo, tok], t_[:], AF.Identity, r=[tk, "lnp"], w=[("xT", g4)],
                    scale=lnp[:, gi, l, fo:fo + 1], bias=lnp[:, bi, l, fo:fo + 1])
                yield
                if write_h:
                    sa, ba = (2, 3) if which == 1 else (4, 5)
                    ACT(hT[:, fo, tok], t_[:], AF.Identity, r=[tk, "der"], w=[("hT", g4)],
                        scale=der[:, l, sa, fo:fo + 1], bias=der[:, l, ba, fo:fo + 1])
                    yield


        def ln_finalize(*a):
            for _ in ln_finalize_gen(*a):
                pass

        def residual_and_stats(l, fo, tok, Pz, pzk, gcol, Pmean, Pex2, mk, ek, gz, ub, usq, idx, ubkey="ub%d"):
            g4 = tok.start // 512
            s2 = idx % 2
            gzb, gzk = gz[s2]
            ACT(gzb[:], Pz[:, 0:512], AF.Copy, r=[pzk, "der"], w=[gzk], scale=der[:, l, gcol, fo:fo + 1])
            STT("dve", xT[:, fo, tok], xT[:, fo, tok], ALPHA, gzb[:], ALU.mult, ALU.add,
                r=[("xT", g4), gzk], w=[("xT", g4)])
            CP("pool", ub[s2][:], xT[:, fo, tok], r=[("xT", g4)], w=[ubkey % s2])
            ACT(usq[s2][:], xT[:, fo, tok], AF.Square, r=[("xT", g4)], w=["usq%d" % s2])

            def stats():
                MM(Pmean[:, 0:512], ones_div[:], ub[s2][:], start=(fo == 0), stop=(fo == 7), r=["ones_div", ubkey % s2],
                   w=[mk], inc=True)
                MM(Pex2[:, 0:512], ones_div[:], usq[s2][:], start=(fo == 0), stop=(fo == 7), r=["ones_div", "usq%d" % s2],
                   w=[ek], inc=True)
            return stats

        def merge(l, OT):
            with contextlib.ExitStack() as S:
                Wbr = sb(S, "Wbr", [128, 4, 2, 1024], BF16)
                Wo = sb(S, "Wo", [128, 8, 1024], BF16)
                Wg = [sb(S, "Wg%d" % i, [128, 4, 8, 128], BF16) for i in range(2)]
                mT = sb(S, "mT", [128, 8, 512], BF16)
                sig = [sb(S, "sig%d" % i, [128, 512], BF16) for i in range(2)]
                W5 = [sb(S, "w5_%d" % i, [128, 512], F32) for i in range(5)]
                t32 = W5[0:3]
                gz = [(W5[3], "w5_3"), (W5[4], "w5_4")]
                ub = sig
                usq = [sb(S, "usq%d" % i, [128, 512], BF16) for i in range(2)]
                L3 = [sb(S, "l3_%d" % i, [128, 512], F32) for i in range(3)]
                LT = ((L3[0], "l3_0"), (L3[1], "l3_1"), (L3[2], "l3_2"), [(W5[3], "w5_3"), (W5[4], "w5_4")])
                pend_ln = [None]
                ig_box = [0]
                sig3 = sig + [sb(S, "sig2", [128, 512], BF16)]
                gbank = [(pf[0], "pf0"), (pf[1], "pf1"), (pb[0][:, 0:1024].bitcast(F32), "pb0")]
                ybank = [(pf[2], "pf2"), (pf[3], "pf3"), (pb[1][:, 0:1024].bitcast(F32), "pb1")]
                for i in range(4):
                    DMA("sp", Wbr[:, i], WB["w_branch"][l][i].rearrange("(kc p) n -> p kc n", p=128), r=WBK[("w_branch", l)], w=["Wbr"])
                DMA("sp", Wo[:], WB["w_o"][l].rearrange("(c p) n -> p c n", p=128), r=WBK[("w_o", l)], w=["Wo"])
                iw_box = [0]
                for g4 in range(4):
                    tok = slice(g4 * 512, (g4 + 1) * 512)
                    def gates_gen():
                        for ft in range(8):
                            ws = iw_box[0] % 2
                            wk = "Wg%d" % ws
                            iw_box[0] += 1
                            for br in range(4):
                                c0 = 2464 + br * 1024 + ft * 128
                                DMA("sp", Wg[ws][:, br], WB["w_in"][l][:, c0:c0 + 128].rearrange("(c p) n -> p c n", p=128), r=WBK[("w_in_g", l)], w=[wk])
                            for br in range(4):
                                ig = ig_box[0]
                                ig_box[0] += 1
                                Pg, pgk = gbank[ig % 3]
                                Py, pyk = ybank[ig % 3]
                                sg_, sgk = sig3[ig % 3], "sig%d" % (ig % 3)
                                for c in range(8):
                                    MM(Pg[:, 0:512], Wg[ws][:, br, c, :], hT[:, c, tok], start=(c == 0), stop=(c == 7),
                                       r=[wk, ("hT", g4)], w=[pgk], inc=(c == 7))
                                ACT(sg_[:], Pg[:, 0:512], AF.Tanh, r=[pgk], w=[sgk], scale=0.5)
                                for kc in range(2):
                                    MM(Py[:, 0:512], Wbr[:, br, kc, ft * 128:(ft + 1) * 128], OT[:, br, kc, tok], start=(kc == 0),
                                       stop=(kc == 1), r=["Wbr", ("OT", br, g4)], w=[pyk], inc=(kc == 1))
                                dst = t32[0] if br == 0 else t32[1 + br % 2]
                                dk = "w5_0" if br == 0 else "w5_%d" % (1 + br % 2)
                                STT("dve", dst[:], sg_[:], 1.0, Py[:, 0:512], ALU.add, ALU.mult,
                                    r=[sgk, pyk], w=[dk])
                                if br in (1, 2):
                                    TT("pool", t32[0][:], t32[0][:], dst[:], ALU.add, r=["w5_0", dk], w=["w5_0"])
                                if br == 3:
                                    TT("pool", mT[:, ft, :], t32[0][:], dst[:], ALU.add, r=["w5_0", dk], w=["mT"])
                                yield
                    interleave2(gates_gen(), pend_ln[0], 1)
                    pend = []
                    for fo in range(8):
                        Pz = pf[fo % 2]
                        pzk = "pf%d" % (fo % 2)
                        for ft in range(8):
                            MM(Pz[:, 0:512], Wo[:, ft, fo * 128:(fo + 1) * 128], mT[:, ft, :], start=(ft == 0), stop=(ft == 7),
                               r=["Wo", "mT"], w=[pzk], inc=(ft == 7))
                        if fo >= 1:
                            pend.pop(0)()
                        pend.append(residual_and_stats(l, fo, tok, Pz, pzk, 6, pf[4], pf[5], "pf4", "pf5", gz, ub, usq, fo, "sig%d"))
                    while pend:
                        pend.pop(0)()
                    pend_ln[0] = ln_finalize_gen(l, 1, tok, pf[4], pf[5], "pf4", "pf5", LT, True)
                for _ in pend_ln[0]:
                    pass
                k.barrier()

        def ffn(l):
            with contextlib.ExitStack() as S:
                aT = sb(S, "aT", [128, 32, 1024], BF16)
                Wup = [sb(S, "Wup%d" % i, [128, 8, 256], BF16) for i in range(2)]
                Wdn = [sb(S, "Wdn%d" % i, [128, 32, 128], BF16) for i in range(2)]
                rl = [sb(S, "rl%d" % i, [128, 512], BF16) for i in range(2)]
                W5 = [sb(S, "w5f_%d" % i, [128, 512], F32) for i in range(4)]
                W5 = [W5[0], W5[1], W5[0], W5[2], W5[3]]
                gz = [(W5[3], "w5_3"), (W5[4], "w5_4")]
                ub = [sb(S, "ubf%d" % i, [128, 512], BF16) for i in range(2)]
                usq = [sb(S, "usqf%d" % i, [128, 512], BF16) for i in range(2)]
                LT = ((W5[0], "w5_0"), (W5[1], "w5_1"), (W5[2], "w5_0"), [(W5[3], "w5_3"), (W5[4], "w5_4")])
                iu_box = [0]
                idn = 0
                ir_box = [0]
                pend_ln2 = [None]
                for hf in range(2):
                    def up_gen():
                        for j in range(16):
                            ws = iu_box[0] % 2
                            wk = "Wup%d" % ws
                            iu_box[0] += 1
                            DMA("sp", Wup[ws][:], WB["w_up"][l][:, j * 256:(j + 1) * 256].rearrange("(c p) n -> p c n", p=128), r=WBK[("w_up", l)], w=[wk])
                            for t4 in range(2):
                                fft = j * 2 + t4
                                for g in range(2):
                                    g4 = hf * 2 + g
                                    tok = slice(g4 * 512, (g4 + 1) * 512)
                                    Pu = pf[2 + ir_box[0] % 2]
                                    puk = "pf%d" % (2 + ir_box[0] % 2)
                                    r_ = rl[ir_box[0] % 2]
                                    rk = "rl%d" % (ir_box[0] % 2)
                                    ir_box[0] += 1
                                    for c in range(8):
                                        MM(Pu[:, 0:512], Wup[ws][:, c, t4 * 128:(t4 + 1) * 128], hT[:, c, tok], start=(c == 0),
                                           stop=(c == 7), r=[wk, ("hT", g4)], w=[puk], inc=(c == 7))
                                    ACT(r_[:], Pu[:, 0:512], AF.Relu, r=[puk], w=[rk])
                                    TT("dve" if g == 0 else "pool", aT[:, fft, g * 512:(g + 1) * 512], r_[:], r_[:], ALU.mult, r=[rk],
                                       w=[("aT", g)])
                                    yield
                    interleave2(up_gen(), pend_ln2[0], 2)
                    pendf = []
                    for fo in range(8):
                        ws = idn % 2
                        wk = "Wdn%d" % ws
                        idn += 1
                        DMA("sp", Wdn[ws][:], WB["w_down"][l][:, fo * 128:(fo + 1) * 128].rearrange("(t p) n -> p t n", p=128),
                            r=WBK[("w_down", l)], w=[wk])
                        for g in range(2):
                            g4 = hf * 2 + g
                            tok = slice(g4 * 512, (g4 + 1) * 512)
                            Pd = pf[2 + g]
                            pdk = "pf%d" % (2 + g)
                            for fft in range(32):
                                MM(Pd[:, 0:512], Wdn[ws][:, fft, :], aT[:, fft, g * 512:(g + 1) * 512], start=(fft == 0),
                                   stop=(fft == 31), r=[wk, ("aT", g)], w=[pdk], inc=(fft == 31))
                            Pm, Pe = (pf[0], pf[1]) if g == 0 else (pf[4], pf[5])
                            mk, ek = ("pf0", "pf1") if g == 0 else ("pf4", "pf5")
                            if pendf:
                                pendf.pop(0)()
                            pendf.append(residual_and_stats(l, fo, tok, Pd, pdk, 7, Pm, Pe, mk, ek, gz, ub, usq, 2 * fo + g))
                    while pendf:
                        pendf.pop(0)()
                    lngens = []
                    for g in range(2):
                        g4 = hf * 2 + g
                        tok = slice(g4 * 512, (g4 + 1) * 512)
                        Pm, Pe = (pf[0], pf[1]) if g == 0 else (pf[4], pf[5])
                        mk, ek = ("pf0", "pf1") if g == 0 else ("pf4", "pf5")
                        lngens.append(ln_finalize_gen(l, 2, tok, Pm, Pe, mk, ek, LT, l == 0))
                    if hf == 0:
                        pend_ln2[0] = itertools.chain(*lngens)
                    else:
                        for g_ in lngens:
                            for _ in g_:
                                pass
                k.barrier()

        for l in range(2):
            with contextlib.ExitStack() as SM:
                OT = sb(SM, "OT", [128, 4, 2, T], BF16)
                if STAGE in (None, "A", "L0", "MIX"):
                    branch_A(l, OT)
                if l == 0:
                    precast(0, ["w_in_g", "w_branch", "w_o"])
                else:
                    precast(1, ["w_up"])
                if STAGE in (None, "B", "L0", "MIX"):
                    branch_B(l, OT)
                if l == 0:
                    precast(0, ["w_up"])
                else:
                    precast(1, ["w_down"])
                if STAGE in (None, "C", "CD", "C1", "C2", "L0", "MIX"):
                    branch_CD(l, OT, False)
                if l == 0:
                    precast(0, ["w_down"])
                if STAGE in (None, "D", "CD", "L0", "MIX"):
                    branch_CD(l, OT, True)
                if l == 0:
                    precast(1, ["w_in_qkv"])
                if STAGE in ("A", "C", "D", "CD", "C1", "C2", "B"):
                    dump("OT", OT[:], [128, 4, 2, T], BF16)
                    k.finish()
                    print("ninst", k.ninst, "waits", k.nwaits)
                    return nc, dbg_out
                merge(l, OT)
                k.barrier()
            if STAGE == "MIX":
                dump("xT", xT[:], [128, 8, T])
                dump("hT", hT[:], [128, 8, T], BF16)
                k.finish()
                print("ninst", k.ninst, "waits", k.nwaits)
                return nc, dbg_out
            if l == 0:
                precast(1, ["w_in_g", "w_branch", "w_o"])
            ffn(l)
            if STAGE == "L0":
                dump("xT", xT[:], [128, 8, T])
                dump("hT", hT[:], [128, 8, T], BF16)
                k.finish()
                print("ninst", k.ninst, "waits", k.nwaits)
                return nc, dbg_out

        with contextlib.ExitStack() as S9:
            ys = [sb(S9, "ys%d" % i, [128, 1024], F32) for i in range(2)]
            for b in range(NB):
                yk = "ys%d" % (b % 2)
                for q in range(2):
                    bank = (2 * b + q) % 4
                    for cc in range(4):
                        c = 4 * q + cc
                        TR(pf[bank][:, cc * 128:(cc + 1) * 128], xT[:, c, b * 128:(b + 1) * 128], ident_f[:],
                           r=[("xT", b // 4), "ident_f"], w=["pf%d" % bank], inc=(cc == 3))
                    CP("act" if q == 0 else "dve", ys[b % 2][:, q * 512:(q + 1) * 512], pf[bank][:, 0:512],
                       r=["pf%d" % bank], w=[yk])
                DMA("sp", O["y"][b * 128:(b + 1) * 128, :], ys[b % 2][:], r=[yk])
        print("ninst", k.ninst, "waits", k.nwaits)
        k.finish()
    return nc, dbg_out


def _axial(t, rot_dim):
    rows = t // 64
    row = np.repeat(np.arange(rows, dtype=np.float32), 64)
    col = (np.arange(t) % 64).astype(np.float32)
    n_freq = rot_dim // 4
    inv = (np.float32(10000.0) ** (-np.arange(n_freq, dtype=np.float32) / np.float32(n_freq))).astype(np.float32)
    ang = np.concatenate([row[:, None] * inv, col[:, None] * inv], axis=-1).astype(np.float32)
    return np.cos(ang).astype(np.float32), np.sin(ang).astype(np.float32)


def _mode_tables(sample):
    bf = ml_dtypes.bfloat16
    if sample:
        ca, sa = _axial(T, 32)
        ch, sh = _axial(T, 64)
    else:
        ca, sa = np.ones((T, 16), np.float32), np.zeros((T, 16), np.float32)
        ch, sh = np.ones((T, 32), np.float32), np.zeros((T, 32), np.float32)
    ropetab = np.concatenate([ca, ca, -sa, sa, ch, ch, -sh, sh], axis=1).astype(np.float32)
    indq = np.zeros((8, T), np.float32)
    indk = np.zeros((8, T + 256), np.float32)
    indkc = np.zeros((8, 20 * 128), np.float32)
    if sample:
        indq[0, :] = 1.0
        indk[0, :] = BIG
        indkc[0, 128:128 + T] = BIG
        indkc[0, 18 * 128:] = BIG
    else:
        for s in range(8):
            indq[s, s * 256:(s + 1) * 256] = 1.0
            indk[s, s * 256:(s + 1) * 256] = BIG
            indkc[s, 128 + s * 256:128 + (s + 1) * 256] = BIG
    bandm = np.ones((2, 128, 256), np.float32)
    if sample:
        kk = np.arange(128)[:, None]
        qq = np.arange(128)[None, :]
        m0 = np.where(kk < qq, 0.0, 1.0)
        m1 = np.where(kk > qq, 0.0, 1.0)
        bandm[0] = np.concatenate([m0, m0], axis=1)
        bandm[1] = np.concatenate([m1, m1], axis=1)
    if sample:
        keepfb = np.ones((32,), np.float32)
    else:
        kf = np.array([0.0 if n % 2 == 0 else 1.0 for n in range(16)], np.float32)
        kb = np.array([0.0 if n % 2 == 1 else 1.0 for n in range(16)], np.float32)
        keepfb = np.concatenate([kf, kb])
    return dict(ropetab=ropetab, indq=indq.astype(bf), indk=indk.astype(bf), indkc=indkc.astype(bf),
                bandm=bandm.astype(bf), keepfb=keepfb)


def _rconst():
    j = np.arange(128, dtype=np.float32)[:, None]
    i = np.arange(128, dtype=np.float32)[None, :]
    pdF = np.maximum(i - j, 0.0)
    mkF = (i >= j).astype(np.float32)
    pdB = np.maximum(j - i, 0.0)
    mkB = (j > i).astype(np.float32)
    iq1 = np.broadcast_to(i + 1.0, (128, 128))
    iq2 = np.broadcast_to(128.0 - i, (128, 128))
    jc = np.concatenate([127.0 - j, j], axis=1)
    return np.ascontiguousarray(np.concatenate([pdF, mkF, pdB, mkB, iq1, iq2, jc], axis=1).astype(np.float32))


_NC_CACHE = {}


def _get_nc():
    key = (STAGE,)
    if key not in _NC_CACHE:
        _NC_CACHE[key] = build()
    return _NC_CACHE[key]


def make_in_maps(inputs):
    f32 = lambda a: np.ascontiguousarray(np.asarray(a, dtype=np.float32))
    w = {n: f32(inputs[n]) for n in W_NAMES}
    rconst = _rconst()
    tabs = {True: _mode_tables(True), False: _mode_tables(False)}
    xp = f32(inputs["x_prompt"])
    xs = f32(inputs["x_sample"])
    maps = []
    for core in range(8):
        sample = core >= 4
        m = dict(w)
        m.update(tabs[sample])
        m["rconst"] = rconst
        if sample:
            b = core - 4
            m["xin"] = np.ascontiguousarray(xs[b])
            m["cond"] = f32(inputs["c"][b])
            m["c_ckv"] = f32(inputs["cache_mla_ckv"][b])
            m["c_kpe"] = f32(inputs["cache_mla_kpe"][b])
            m["c_wk"] = f32(inputs["cache_win_k"][b]).reshape(2, 256, 128)
            m["c_wv"] = f32(inputs["cache_win_v"][b]).reshape(2, 256, 128)
            m["c_gk"] = f32(inputs["cache_gqa_k"][b]).reshape(2, 256, 128)
            m["c_gv"] = f32(inputs["cache_gqa_v"][b]).reshape(2, 256, 128)
            m["sf0"] = f32(inputs["state_ret_fwd"][b])
            m["sb0"] = f32(inputs["state_ret_bwd"][b])
        else:
            m["xin"] = np.ascontiguousarray(xp[core * 8:(core + 1) * 8].reshape(2048, 1024))
            m["cond"] = f32(inputs["c_ctx"])
            for nm, shp in [("c_ckv", (2, 256, 128)), ("c_kpe", (2, 256, 32)), ("c_wk", (2, 256, 128)),
                            ("c_wv", (2, 256, 128)), ("c_gk", (2, 256, 128)), ("c_gv", (2, 256, 128)),
                            ("sf0", (2, 4, 64, 64)), ("sb0", (2, 4, 64, 64))]:
                m[nm] = np.zeros(shp, np.float32)
        maps.append(m)
    return maps


def kernel(**inputs):
    nc, _ = _get_nc()
    maps = make_in_maps(inputs)
    res = run_bass_kernel_spmd(nc, maps, core_ids=list(range(8)))
    R = res.results
    y_prompt = np.concatenate([R[c]["y"].reshape(8, 256, 1024) for c in range(4)], axis=0)
    y_sample = np.stack([R[c]["y"] for c in range(4, 8)], axis=0)

    def cache(name, last):
        parts = []
        for c in range(4):
            a = R[c][name].reshape((2, 8, 256) + last)
            parts.append(np.moveaxis(a, 0, 1))
        return np.ascontiguousarray(np.concatenate(parts, axis=0))

    new_ckv = cache("o_ckv", (128,))
    new_kpe = cache("o_kpe", (32,))
    new_wk = cache("o_wk", (2, 64))
    new_wv = cache("o_wv", (2, 64))
    new_gk = cache("o_gk", (2, 64))
    new_gv = cache("o_gv", (2, 64))
    rf = np.ascontiguousarray(np.concatenate([np.moveaxis(R[c]["o_rf"], 0, 1) for c in range(4)], axis=0))
    rb = np.ascontiguousarray(np.concatenate([np.moveaxis(R[c]["o_rb"], 0, 1) for c in range(4)], axis=0))
    outs = (y_prompt, y_sample, new_ckv, new_kpe, new_wk, new_wv, new_gk, new_gv, rf, rb)
    return tuple(np.ascontiguousarray(o.astype(np.float32)) for o in outs)
```

```python
import contextlib
import itertools
import numpy as np
import ml_dtypes
import concourse.bass as bass
import concourse.mybir as mybir
from concourse.bass_utils import run_bass_kernel_spmd

F32 = mybir.dt.float32
BF16 = mybir.dt.bfloat16
AF = mybir.ActivationFunctionType
ALU = mybir.AluOpType
AX = mybir.AxisListType

T = 2048
NB = 16
BIG = 100.0
EPS = 1e-6
ALPHA = 4.0 ** 0.25
MLA_SCALE = 96.0 ** -0.5
ATT_SCALE = 0.125
SAME_ENGINE_SYNC = True
STAGE = None
DBG = {}


class K:
    def __init__(self, nc, es, n_dma_sems=8):
        self.nc = nc
        self.engs = {"pe": nc.tensor, "act": nc.scalar, "dve": nc.vector, "pool": nc.gpsimd, "sp": nc.sync}
        self.sem = {}
        self.cnt = {}
        for e in ("pe", "act", "dve", "pool"):
            self.sem[e] = es.enter_context(nc.semaphore("s_" + e))
            self.cnt[e] = 0
        self.dsem, self.dval, self.dnext = {}, {}, {}
        for q, ns in (("sp", n_dma_sems), ("pool", n_dma_sems), ("poolx", 24)):
            self.dsem[q] = [es.enter_context(nc.semaphore("d_%s%d" % (q, i))) for i in range(ns)]
            self.dval[q] = [0] * ns
            self.dnext[q] = 0
        self.engs["poolx"] = nc.gpsimd
        self.semobjs = {}
        for e, s in self.sem.items():
            self.semobjs[("e", e)] = s
        for q in self.dsem:
            for i, s in enumerate(self.dsem[q]):
                self.semobjs[("d", q, i)] = s
        self.waited = {e: {} for e in self.engs if e != "poolx"}
        self.waited["poolx"] = self.waited["pool"]
        self.lastw = {}
        self.reads = {}
        self.nwaits = 0
        self.ninst = {e: 0 for e in self.engs}
        self.ninst["poolx"] = 0

    def _deps(self, reads, writes, eng=None):
        deps = {}

        def add(ev):
            if ev is None:
                return
            sid, v = ev
            if deps.get(sid, 0) < v:
                deps[sid] = v
        for key in reads:
            add(self.lastw.get(key))
            if isinstance(key, str) and key[:2] in ("pf", "pb"):
                for sid, v in self.reads.get(key, {}).items():
                    if sid != ("e", eng):
                        add((sid, v))
        for key in writes:
            add(self.lastw.get(key))
            for sid, v in self.reads.get(key, {}).items():
                add((sid, v))
        return deps

    def _emit_waits(self, eng, deps):
        w = self.waited[eng]
        for sid, v in deps.items():
            if sid == ("e", eng) and (eng == "pe" or not SAME_ENGINE_SYNC):
                continue
            if sid == ("e", "pe"):
                assert self.cnt["pe"] >= v, "dependency on PE instruction without inc"
            if w.get(sid, 0) >= v:
                continue
            self.engs[eng].wait_ge(self.semobjs[sid], v)
            w[sid] = v
            self.nwaits += 1

    def _record(self, ev, reads, writes):
        sid, v = ev
        for key in reads:
            d = self.reads.setdefault(key, {})
            if d.get(sid, 0) < v:
                d[sid] = v
        for key in writes:
            self.lastw[key] = ev
            self.reads[key] = {}

    def op(self, eng, fn, reads=(), writes=(), inc=True):
        self._emit_waits(eng, self._deps(reads, writes, eng))
        ins = fn()
        self.ninst[eng] += 1
        if eng == "pe" and not inc:
            ev = (("e", "pe"), self.cnt["pe"] + 1)
        else:
            ins.then_inc(self.sem[eng], 1)
            self.cnt[eng] += 1
            ev = (("e", eng), self.cnt[eng])
        self._record(ev, reads, writes)
        return ins

    def dma(self, q, out, in_, reads=(), writes=()):
        i = self.dnext[q]
        self.dnext[q] = (i + 1) % len(self.dsem[q])
        sid = ("d", q, i)
        deps = self._deps(reads, writes, q)
        if self.dval[q][i] > 0:
            deps[sid] = max(deps.get(sid, 0), self.dval[q][i])
        self._emit_waits(q, deps)
        ins = self.engs[q].dma_start(out=out, in_=in_)
        self.dval[q][i] += 16
        ins.then_inc(self.dsem[q][i], 16)
        self._record((sid, self.dval[q][i]), reads, writes)
        self.ninst[q] += 1
        return ins

    def barrier(self, final=False):
        evs = {}
        for e in ("pe", "act", "dve", "pool"):
            if self.cnt[e] > 0:
                evs[("e", e)] = self.cnt[e]
        for q in self.dsem:
            if q == "poolx" and not final:
                continue
            for i in range(len(self.dsem[q])):
                if self.dval[q][i] > 0:
                    evs[("d", q, i)] = self.dval[q][i]
        for eng in ("pe", "act", "dve", "pool", "sp"):
            w = self.waited[eng]
            for sid, v in evs.items():
                if sid == ("e", eng):
                    continue
                if w.get(sid, 0) >= v:
                    continue
                self.engs[eng].wait_ge(self.semobjs[sid], v)
                w[sid] = v
                self.nwaits += 1
        self.lastw = {kk: v for kk, v in self.lastw.items() if isinstance(kk, tuple) and kk[0] == "wb"}
        self.reads = {}

    def finish(self):
        self.barrier(final=True)


W_NAMES = ['w_ada', 'b_ada', 'w_in', 'mla_q_norm', 'mla_w_uq', 'mla_kv_norm', 'mla_w_uk', 'mla_w_uv',
           'ret_decay_fwd', 'ret_decay_bwd', 'ret_gn_gain', 'win_sink', 'gqa_q_norm', 'gqa_k_norm',
           'w_branch', 'w_o', 'ln1_g', 'ln1_b', 'w_up', 'w_down', 'ln2_g', 'ln2_b']
W_SHAPES = dict(w_ada=(2, 1024, 6144), b_ada=(2, 6144), w_in=(2, 1024, 6560), mla_q_norm=(2, 256),
                mla_w_uq=(2, 256, 384), mla_kv_norm=(2, 128), mla_w_uk=(2, 128, 256), mla_w_uv=(2, 128, 256),
                ret_decay_fwd=(2, 4), ret_decay_bwd=(2, 4), ret_gn_gain=(2, 256), win_sink=(2, 4),
                gqa_q_norm=(2, 64), gqa_k_norm=(2, 64), w_branch=(2, 4, 256, 1024), w_o=(2, 1024, 1024),
                ln1_g=(2, 1024), ln1_b=(2, 1024), w_up=(2, 1024, 4096), w_down=(2, 4096, 1024),
                ln2_g=(2, 1024), ln2_b=(2, 1024))
IN_SPECS = [("xin", (2048, 1024), F32), ("cond", (1024,), F32), ("ropetab", (2048, 192), F32),
            ("indq", (8, 2048), BF16), ("indk", (8, 2304), BF16), ("indkc", (8, 2560), BF16),
            ("bandm", (2, 128, 256), BF16), ("keepfb", (32,), F32), ("rconst", (128, 770), F32),
            ("c_ckv", (2, 256, 128), F32), ("c_kpe", (2, 256, 32), F32), ("c_wk", (2, 256, 128), F32),
            ("c_wv", (2, 256, 128), F32), ("c_gk", (2, 256, 128), F32), ("c_gv", (2, 256, 128), F32),
            ("sf0", (2, 4, 64, 64), F32), ("sb0", (2, 4, 64, 64), F32)]
OUT_SPECS = [("y", (2048, 1024)), ("o_ckv", (2, 2048, 128)), ("o_kpe", (2, 2048, 32)), ("o_wk", (2, 2048, 128)),
             ("o_wv", (2, 2048, 128)), ("o_gk", (2, 2048, 128)), ("o_gv", (2, 2048, 128)),
             ("o_rf", (2, 8, 4, 64, 64)), ("o_rb", (2, 8, 4, 64, 64))]


def build():
    nc = bass.Bass("TRN2", target_bir_lowering=False)
    I = {}
    for name, shape, dt in IN_SPECS:
        I[name] = nc.dram_tensor(name, list(shape), dt, kind="ExternalInput").ap()
    for name in W_NAMES:
        I[name] = nc.dram_tensor(name, list(W_SHAPES[name]), F32, kind="ExternalInput").ap()
    O = {}
    for name, shape in OUT_SPECS:
        O[name] = nc.dram_tensor(name, list(shape), F32, kind="ExternalOutput").ap()
    dbg_out = {}
    WB = {}
    for name in ("w_in", "w_branch", "w_o", "w_up", "w_down"):
        WB[name] = nc.dram_tensor("wb_" + name, list(W_SHAPES[name]), BF16, kind="Internal").ap()

    with contextlib.ExitStack() as es:
        k = K(nc, es)

        sbn = [0]

        def sb(scope, name, shape, dt=F32):
            sbn[0] += 1
            return scope.enter_context(nc.sbuf_tensor("sb%d_%s" % (sbn[0], name), list(shape), dt))

        def MM(out, lhsT, rhs, start=True, stop=True, r=(), w=(), inc=False):
            return k.op("pe", lambda: nc.tensor.matmul(out, lhsT=lhsT, rhs=rhs, start=start, stop=stop),
                        reads=r, writes=w, inc=inc)

        def TR(out, in_, ident, r=(), w=(), inc=True):
            return k.op("pe", lambda: nc.tensor.transpose(out, in_, ident), reads=r, writes=w, inc=inc)

        def ACT(out, in_, func, r=(), w=(), bias=None, scale=None, accum=None):
            kw = {}
            if bias is not None:
                kw["bias"] = bias
                r = list(r) + ["cst"]
            if scale is not None:
                kw["scale"] = scale
            if accum is not None:
                kw["accum_out"] = accum
            return k.op("act", lambda: nc.scalar.activation(out=out, in_=in_, func=func, **kw), reads=r, writes=w)

        def E(eng):
            return {"dve": nc.vector, "pool": nc.gpsimd}[eng]

        def TT(eng, out, in0, in1, op, r=(), w=()):
            return k.op(eng, lambda: E(eng).tensor_tensor(out=out, in0=in0, in1=in1, op=op), reads=r, writes=w)

        def TS(eng, out, in0, s1, s2, op0, op1=None, r=(), w=()):
            if op1 is None:
                return k.op(eng, lambda: E(eng).tensor_scalar(out=out, in0=in0, scalar1=s1, scalar2=None, op0=op0),
                            reads=r, writes=w)
            return k.op(eng, lambda: E(eng).tensor_scalar(out=out, in0=in0, scalar1=s1, scalar2=s2, op0=op0, op1=op1),
                        reads=r, writes=w)

        def STT(eng, out, in0, scalar, in1, op0, op1, r=(), w=()):
            return k.op(eng, lambda: E(eng).scalar_tensor_tensor(out=out, in0=in0, scalar=scalar, in1=in1,
                                                                 op0=op0, op1=op1), reads=r, writes=w)

        def CP(eng, out, in_, r=(), w=()):
            if eng == "act":
                return k.op("act", lambda: nc.scalar.copy(out=out, in_=in_), reads=r, writes=w)
            return k.op(eng, lambda: E(eng).tensor_copy(out=out, in_=in_), reads=r, writes=w)

        def RED(out, in_, r=(), w=()):
            return k.op("dve", lambda: nc.vector.tensor_reduce(out=out, in_=in_, axis=AX.X, op=ALU.add),
                        reads=r, writes=w)

        def RCP(out, in_, r=(), w=()):
            return k.op("dve", lambda: nc.vector.reciprocal(out=out, in_=in_), reads=r, writes=w)

        def MSET(eng, ap, val, w=()):
            return k.op(eng, lambda: E(eng).memset(ap, val), writes=w)

        def DMA(q, out, in_, r=(), w=()):
            return k.dma(q, out, in_, reads=r, writes=w)

        def interleave(gens):
            gens = list(gens)
            while gens:
                for g_ in list(gens):
                    try:
                        next(g_)
                    except StopIteration:
                        gens.remove(g_)

        def interleave2(main, side, ratio):
            main_done = side is None
            side_done = side is None
            main_done = False
            while not main_done:
                try:
                    next(main)
                except StopIteration:
                    main_done = True
                if not side_done:
                    for _ in range(ratio):
                        try:
                            next(side)
                        except StopIteration:
                            side_done = True
                            break
            if not side_done:
                for _ in side:
                    pass

        def dump(name, ap, shape, dt=F32, r=()):
            d = nc.dram_tensor("dbg_" + name, list(shape), dt, kind="ExternalOutput").ap()
            dbg_out[name] = d
            DMA("sp", d, ap, r=r)

        pf = [es.enter_context(nc.psum_tensor("pf%d" % i, [128, 512], F32)) for i in range(6)]
        pb = [es.enter_context(nc.psum_tensor("pb%d" % i, [128, 1024], BF16)) for i in range(2)]

        P = es
        xT = sb(P, "xT", [128, 8, T], F32)
        hT = sb(P, "hT", [128, 8, T], BF16)
        ident_bf = sb(P, "ident_bf", [128, 128], BF16)
        ident_f = sb(P, "ident_f", [128, 128], F32)
        ones_div = sb(P, "ones_div", [128, 128], BF16)
        cst = sb(P, "cst", [128, 4], F32)
        modv = sb(P, "modv", [128, 2, 6, 8], F32)
        lnp = sb(P, "lnp", [128, 4, 2, 8], F32)
        der = sb(P, "der", [128, 2, 8, 8], F32)

        MSET("pool", ident_bf[:], 1.0, w=["ident_bf"])
        k.op("pool", lambda: nc.gpsimd.affine_select(out=ident_bf[:], in_=ident_bf[:], pattern=[[-1, 128]],
                                                     compare_op=ALU.is_equal, fill=0.0, base=0, channel_multiplier=1),
             reads=["ident_bf"], writes=["ident_bf"])
        MSET("pool", ident_f[:], 1.0, w=["ident_f"])
        k.op("pool", lambda: nc.gpsimd.affine_select(out=ident_f[:], in_=ident_f[:], pattern=[[-1, 128]],
                                                     compare_op=ALU.is_equal, fill=0.0, base=0, channel_multiplier=1),
             reads=["ident_f"], writes=["ident_f"])
        MSET("dve", ones_div[:], 1.0 / 1024.0, w=["ones_div"])
        MSET("dve", cst[:, 0:1], -BIG, w=["cst"])
        MSET("dve", cst[:, 1:2], EPS, w=["cst"])
        MSET("dve", cst[:, 2:3], 1.0, w=["cst"])
        MSET("dve", cst[:, 3:4], 0.0, w=["cst"])
        NEGBIG = cst[:, 0:1]
        EPSC = cst[:, 1:2]
        ONEC = cst[:, 2:3]
        esraw = sb(P, "esraw", [128, 8], F32)
        esink = sb(P, "esink", [128, 8], F32)
        DMA("sp", esraw[:], I["win_sink"].rearrange("a b -> (a b)").partition_broadcast(128), w=["esraw"])
        decraw = sb(P, "decraw", [128, 16], F32)
        lgall = sb(P, "lgall", [128, 16], F32)
        DMA("sp", decraw[:, 0:8], I["ret_decay_fwd"].rearrange("a b -> (a b)").partition_broadcast(128), w=["decraw"])
        DMA("sp", decraw[:, 8:16], I["ret_decay_bwd"].rearrange("a b -> (a b)").partition_broadcast(128), w=["decraw"])

        condT = sb(P, "condT", [128, 8], F32)
        ctmp = sb(P, "ctmp", [128, 8], F32)
        condS = sb(P, "condS", [128, 8], BF16)
        badaT = sb(P, "badaT", [128, 96], F32)
        SSTG = contextlib.ExitStack()
        stg1 = sb(SSTG, "stg1", [128, 128], F32)
        stg2 = sb(SSTG, "stg2", [128, 128], F32)
        MSET("dve", stg1[:], 0.0, w=["stg1"])
        MSET("dve", stg2[:], 0.0, w=["stg2"])
        for j, nm in enumerate(["ln1_g", "ln1_b", "ln2_g", "ln2_b"]):
            DMA("sp", stg1[j * 16:(j + 1) * 16, :], I[nm].rearrange("l (c p) -> (l c) p", p=128), w=["stg1"])
        DMA("sp", stg1[64:72, :], I["cond"].rearrange("(c p) -> c p", p=128), w=["stg1"])
        DMA("sp", stg2[0:96, :], I["b_ada"].rearrange("l (j p) -> (l j) p", p=128), w=["stg2"])
        TR(pf[1][:, 0:128], stg1[:], ident_f[:], r=["stg1", "ident_f"], w=["pf1"])
        CP("dve", lnp[:].rearrange("p j l c -> p (j l c)"), pf[1][:, 0:64], r=["pf1"], w=["lnp"])
        CP("dve", condT[:], pf[1][:, 64:72], r=["pf1"], w=["condT"])
        TR(pf[1][:, 128:256], stg2[:], ident_f[:], r=["stg2", "ident_f"], w=["pf1"])
        CP("dve", badaT[:], pf[1][:, 128:224], r=["pf1"], w=["badaT"])
        k.barrier()
        SSTG.close()
        ACT(ctmp[:], condT[:], AF.Exp, r=["condT"], w=["ctmp"], scale=-1.0)
        ACT(esink[:], esraw[:], AF.Exp, r=["esraw"], w=["esink"])
        ACT(lgall[:], decraw[:], AF.Exp, r=["decraw"], w=["lgall"], scale=-1.0)
        ACT(lgall[:], lgall[:], AF.Ln, r=["lgall"], w=["lgall"], bias=ONEC)
        TS("dve", lgall[:], lgall[:], -1.0, None, ALU.mult, r=["lgall"], w=["lgall"])
        TS("dve", ctmp[:], ctmp[:], 1.0, None, ALU.add, r=["ctmp"], w=["ctmp"])
        RCP(ctmp[:], ctmp[:], r=["ctmp"], w=["ctmp"])
        condSf = sb(P, "condSf", [128, 8], F32)
        TT("dve", condSf[:], condT[:], ctmp[:], ALU.mult, r=["condT", "ctmp"], w=["condS"])

        WBK = {}

        def precast(l_, names):
            for nm_ in names:
                keys = []
                if nm_ == "w_branch":
                    pieces = [(WB[nm_][l_].rearrange("i k n -> (i k) n"), I[nm_][l_].rearrange("i k n -> (i k) n"))]
                elif nm_ in ("w_in_qkv", "w_in_g"):
                    c0, c1 = (0, 2464) if nm_ == "w_in_qkv" else (2464, 6560)
                    pieces = [(WB["w_in"][l_][i_ * 512:(i_ + 1) * 512, c0:c1], I["w_in"][l_][i_ * 512:(i_ + 1) * 512, c0:c1])
                              for i_ in range(2)]
                else:
                    rows = W_SHAPES[nm_][1]
                    npc = 4 if nm_ == "w_down" else 2
                    step = rows // npc
                    pieces = [(WB[nm_][l_][i_ * step:(i_ + 1) * step, :], I[nm_][l_][i_ * step:(i_ + 1) * step, :])
                              for i_ in range(npc)]
                for i_, (dst_, src_) in enumerate(pieces):
                    key = ("wb", nm_, l_, i_)
                    keys.append(key)
                    DMA("poolx", dst_, src_, w=[key])
                WBK[(nm_, l_)] = keys

        def mod_gen(l, wada, bank):
            pm = pf[bank]
            pmk = "pf%d" % bank
            SW_ = wada[0].shape[2]
            for j in range(6144 // SW_):
                wt = wada[j % 2]
                wk = "wada%d" % (j % 2)
                DMA("sp", wt[:], I["w_ada"][l][:, j * SW_:(j + 1) * SW_].rearrange("(c p) n -> p c n", p=128), w=[wk])
                yield
                for ft in range(SW_ // 128):
                    col = j * (SW_ // 128) + ft
                    for c in range(8):
                        MM(pm[:, col:col + 1], wt[:, c, ft * 128:(ft + 1) * 128], condSf[:, c:c + 1],
                           start=(c == 0), stop=(c == 7), r=[wk, "condS"], w=[pmk], inc=(c == 7))
                    yield
            TT("dve", modv[:, l].rearrange("p j c -> p (j c)"), pm[:, 0:48], badaT[:, l * 48:(l + 1) * 48], ALU.add,
               r=[pmk, "badaT"], w=["modv"])
            TS("dve", der[:, l, 0, :], modv[:, l, 1, :], 1.0, None, ALU.add, r=["modv"], w=["der"])
            TS("dve", der[:, l, 1, :], modv[:, l, 4, :], 1.0, None, ALU.add, r=["modv"], w=["der"])
            yield
            TT("dve", der[:, l, 2, :], lnp[:, 0, l, :], der[:, l, 1, :], ALU.mult, r=["lnp", "der"], w=["der"])
            TT("dve", der[:, l, 3, :], lnp[:, 1, l, :], der[:, l, 1, :], ALU.mult, r=["lnp", "der"], w=["der"])
            TT("dve", der[:, l, 3, :], der[:, l, 3, :], modv[:, l, 3, :], ALU.add, r=["modv", "der"], w=["der"])
            yield
            TS("dve", der[:, l, 6, :], modv[:, l, 2, :], 0.5, None, ALU.mult, r=["modv"], w=["der"])
            CP("dve", der[:, l, 7, :], modv[:, l, 5, :], r=["modv"], w=["der"])
            if l == 1:
                TT("dve", der[:, 0, 4, :], lnp[:, 2, 0, :], der[:, 1, 0, :], ALU.mult, r=["lnp", "der"], w=["der"])
                TT("dve", der[:, 0, 5, :], lnp[:, 3, 0, :], der[:, 1, 0, :], ALU.mult, r=["lnp", "der"], w=["der"])
                TT("dve", der[:, 0, 5, :], der[:, 0, 5, :], modv[:, 1, 0, :], ALU.add, r=["modv", "der"], w=["der"])
            yield

        def x_gen(xs):
            for b in range(NB):
                xk = "xs%d" % (b % 2)
                DMA("sp", xs[b % 2][:], I["xin"][b * 128:(b + 1) * 128, :], w=[xk])
                for q in range(2):
                    bank = (2 * b + q) % 4
                    for cc in range(4):
                        c = 4 * q + cc
                        TR(pf[bank][:, cc * 128:(cc + 1) * 128], xs[b % 2][:, c * 128:(c + 1) * 128], ident_f[:],
                           r=[xk, "ident_f"], w=["pf%d" % bank], inc=(cc == 3))
                    CP("act" if q == 0 else "dve", xT[:, 4 * q:4 * q + 4, b * 128:(b + 1) * 128],
                       pf[bank][:].rearrange("p (c t) -> p c t", c=4), r=["pf%d" % bank], w=[("xT", b // 4)])
                    yield

        precast(0, ["w_in_qkv"])
        with contextlib.ExitStack() as S1:
            xs = [sb(S1, "xs%d" % i, [128, 1024], F32) for i in range(2)]
            wada0 = [sb(S1, "wadaS%d" % i, [128, 8, 512], F32) for i in range(2)]
            interleave([x_gen(xs), mod_gen(0, wada0, 4)])
            k.barrier()
        for g in range(4):
            for c in range(8):
                TS("dve" if c % 2 == 0 else "pool", hT[:, c, g * 512:(g + 1) * 512], xT[:, c, g * 512:(g + 1) * 512],
                   der[:, 0, 0, c:c + 1], modv[:, 0, 0, c:c + 1], ALU.mult, ALU.add,
                   r=[("xT", g), "der", "modv"], w=[("hT", g)])
        k.barrier()

        if STAGE == "x":
            dump("xT", xT[:], [128, 8, T], r=[("xT", g) for g in range(4)])
            dump("hT", hT[:], [128, 8, T], BF16, r=[("hT", g) for g in range(4)])
            dump("modv", modv[:], [128, 2, 6, 8], r=["modv"])
            k.finish()
            return nc, dbg_out

        def lastsl(X, a, b):
            if len(X.shape) == 3:
                return X[:, :, a:b]
            return X[:, :, :, a:b]

        def bcl(t_, X):
            ap = t_
            for _ in range(len(X.shape) - 2):
                ap = ap.unsqueeze(1)
            return ap.to_broadcast(list(X.shape[:-1]) + [t_.shape[-1]])

        def rope(eng, out, X, cc, ss, tmpA, tmpB, H, half, r, w, ktmp):
            D2 = 2 * half
            TT(eng, tmpA, X, bcl(cc, X), ALU.mult, r=r, w=[ktmp + "A"])
            TT(eng, lastsl(tmpB, 0, half), lastsl(X, half, D2), bcl(ss[:, 0:half], X), ALU.mult, r=r, w=[ktmp + "B"])
            TT(eng, lastsl(tmpB, half, D2), lastsl(X, 0, half), bcl(ss[:, half:D2], X), ALU.mult, r=r, w=[ktmp + "B"])
            TT(eng, out, tmpA, tmpB, ALU.add, r=[ktmp + "A", ktmp + "B"], w=w)

        def branch_A(l, OT):
            with contextlib.ExitStack() as S:
                KTA = sb(S, "KTA", [128, 4, 2304], BF16)
                VA = sb(S, "VA", [128, 18, 2, 192], BF16)
                QTA = [sb(S, "QTA%d" % i, [128, 4, 512], BF16) for i in range(2)]
                WinKV = sb(S, "WinKV", [128, 8, 160], BF16)
                WinQ = sb(S, "WinQ", [128, 8, 256], BF16)
                Wuq32 = sb(S, "Wuq32", [128, 2, 384], F32)
                Wuq = sb(S, "Wuq", [128, 2, 384], BF16)
                gq = sb(S, "gq", [128, 2], F32)
                Wukv = sb(S, "Wukv", [128, 512], BF16)
                gkv = sb(S, "gkv", [128, 128], F32)
                rt = [sb(S, "rtA%d" % i, [128, 192], F32) for i in range(2)]
                ssq = [sb(S, "ssqA%d" % i, [128, 1], F32) for i in range(2)]
                rs = [sb(S, "rsA%d" % i, [128, 2], F32) for i in range(2)]
                junk = [sb(S, "junkA%d" % i, [128, 256], BF16) for i in range(2)]
                ckv32 = [sb(S, "ckv32_%d" % i, [128, 128], F32) for i in range(2)]
                ckvb = [sb(S, "ckvb%d" % i, [128, 128], BF16) for i in range(2)]
                ckvT = [sb(S, "ckvT%d" % i, [128, 128], BF16) for i in range(2)]
                kpe32 = [sb(S, "kpe32_%d" % i, [128, 32], F32) for i in range(2)]
                kper = [sb(S, "kper%d" % i, [128, 32], F32) for i in range(2)]
                tmpA = [sb(S, "tmpAA%d" % i, [128, 4, 32], F32) for i in range(2)]
                tmpB = [sb(S, "tmpBA%d" % i, [128, 4, 32], F32) for i in range(2)]
                KA = [sb(S, "KA%d" % i, [128, 4, 96], BF16) for i in range(2)]
                qlb = [sb(S, "qlb%d" % i, [128, 256], BF16) for i in range(2)]
                qlT = [sb(S, "qlT%d" % i, [128, 2, 128], BF16) for i in range(2)]
                qs = [sb(S, "qs%d" % i, [128, 4, 96], F32) for i in range(2)]
                QA = [sb(S, "QA%d" % i, [128, 4, 96], BF16) for i in range(2)]
                PT = [sb(S, "PTA%d" % i, [128, 512], BF16) for i in range(3)]
                rcp = sb(S, "rcpA", [128, 512], F32)

                DMA("sp", WinKV[:], WB["w_in"][l][:, 256:416].rearrange("(c p) n -> p c n", p=128), r=WBK[("w_in_qkv", l)], w=["WinKV"])
                DMA("sp", WinQ[:], WB["w_in"][l][:, 0:256].rearrange("(c p) n -> p c n", p=128), r=WBK[("w_in_qkv", l)], w=["WinQ"])
                DMA("sp", rcp[:, 0:256], I["mla_w_uk"][l], w=["rcpA"])
                DMA("sp", rcp[:, 256:512], I["mla_w_uv"][l], w=["rcpA"])
                CP("dve", Wukv[:], rcp[:], r=["rcpA"], w=["Wukv"])
                DMA("sp", Wuq32[:], I["mla_w_uq"][l].rearrange("(c p) n -> p c n", p=128), w=["Wuq32"])
                with nc.allow_non_contiguous_dma(reason="tiny parameter vectors"):
                    DMA("sp", gq[:], I["mla_q_norm"][l].rearrange("(c p) -> p c", p=128), w=["gq"])
                DMA("sp", gkv[:], I["mla_kv_norm"][l].partition_broadcast(128), w=["gkv"])
                for c in range(2):
                    TS("dve", Wuq[:, c, :], Wuq32[:, c, :], gq[:, c:c + 1], None, ALU.mult, r=["Wuq32", "gq"], w=["Wuq"])
                MSET("pool", VA[:], 1.0, w=["VA"])
                for h in range(4):
                    DMA("sp", KTA[96:104, h, :], I["indk"], w=["KTA"])

                def blockA1(b):
                    s = b % 2
                    p1k, pkvk = "pf%d" % s, "pf%d" % (2 + s)
                    P1, Pkv = pf[s], pf[2 + s]
                    if b < 16:
                        DMA("sp", rt[s][:], I["ropetab"][b * 128:(b + 1) * 128, :], w=["rtA%d" % s])
                        yield
                        for c in range(8):
                            MM(P1[:, 0:160], hT[:, c, b * 128:(b + 1) * 128], WinKV[:, c, :], start=(c == 0), stop=(c == 7),
                               r=[("hT", b // 4), "WinKV"], w=[p1k], inc=(c == 7))
                        MSET("dve", ssq[s][:], 0.0, w=["ssqA%d" % s])
                        yield
                        ACT(junk[s][:, 0:128], P1[:, 0:128], AF.Square, r=[p1k], w=["junkA%d" % s, "ssqA%d" % s],
                            accum=ssq[s][:, 0:1])
                        yield
                        ACT(rs[s][:, 0:1], ssq[s][:, 0:1], AF.Ln, r=["ssqA%d" % s], w=["rsA%d" % s], scale=1.0 / 128.0,
                            bias=EPSC)
                        yield
                        ACT(rs[s][:, 0:1], rs[s][:, 0:1], AF.Exp, r=["rsA%d" % s], w=["rsA%d" % s], scale=-0.5)
                        yield
                        STT("dve", ckv32[s][:], P1[:, 0:128], rs[s][:, 0:1], gkv[:], ALU.mult, ALU.mult,
                            r=[p1k, "rsA%d" % s, "gkv"], w=["ckv32_%d" % s])
                        yield
                        DMA("sp", O["o_ckv"][l, b * 128:(b + 1) * 128, :], ckv32[s][:], r=["ckv32_%d" % s])
                        yield
                        CP("act", kpe32[s][:], P1[:, 128:160], r=[p1k], w=["kpe32_%d" % s])
                        yield
                        DMA("sp", O["o_kpe"][l, b * 128:(b + 1) * 128, :], kpe32[s][:], r=["kpe32_%d" % s])
                        yield
                        rope("dve", kper[s][:].unsqueeze(1), kpe32[s][:].unsqueeze(1), rt[s][:, 0:32], rt[s][:, 32:64],
                             tmpA[s][:, 0:1, :], tmpB[s][:, 0:1, :], 1, 16,
                             r=["kpe32_%d" % s, "rtA%d" % s], w=["kper%d" % s], ktmp="tmpA%d" % s)
                        yield
                    else:
                        j = b - 16
                        DMA("sp", ckv32[s][:], I["c_ckv"][l, j * 128:(j + 1) * 128, :], w=["ckv32_%d" % s])
                        yield
                        DMA("sp", kper[s][:], I["c_kpe"][l, j * 128:(j + 1) * 128, :], w=["kper%d" % s])
                        yield
                    CP("dve", ckvb[s][:], ckv32[s][:], r=["ckv32_%d" % s], w=["ckvb%d" % s])
                    yield
                    TR(pb[0][:, s * 128:(s + 1) * 128], ckvb[s][:], ident_bf[:], r=["ckvb%d" % s, "ident_bf"], w=["pb0"])
                    yield
                    CP("act", ckvT[s][:], pb[0][:, s * 128:(s + 1) * 128], r=["pb0"], w=["ckvT%d" % s])
                    yield
                    MM(Pkv[:, 0:512], ckvT[s][:], Wukv[:], r=["ckvT%d" % s, "Wukv"], w=[pkvk], inc=True)
                    yield
                    CP("act", KA[s][:, :, 0:64], Pkv[:, 0:256].rearrange("p (h d) -> p h d", h=4), r=[pkvk], w=["KA%d" % s])
                    yield
                    CP("dve", KA[s][:, :, 64:96], kper[s][:].unsqueeze(1).to_broadcast([128, 4, 32]),
                       r=["kper%d" % s], w=["KA%d" % s])
                    yield
                    for h in range(4):
                        TR(pb[1][0:96, s * 512 + h * 128:s * 512 + (h + 1) * 128], KA[s][:, h, :], ident_bf[:], r=["KA%d" % s, "ident_bf"],
                           w=["pb1"], inc=(h == 3))
                        yield
                    CP("dve", KTA[0:96, :, b * 128:(b + 1) * 128], pb[1][0:96, s * 512:(s + 1) * 512].rearrange("p (h t) -> p h t", h=4),
                       r=["pb1"], w=["KTA"])
                    yield
                    vv = Pkv[:, 256:512].rearrange("p (a b d) -> p a b d", a=2, b=2)
                    CP("act", VA[:, b, :, 0:64], vv[:, :, 0, :], r=[pkvk], w=["VA"])
                    yield
                    CP("dve", VA[:, b, :, 128:192], vv[:, :, 1, :], r=[pkvk], w=["VA"])
                    yield

                for b0 in range(0, 18, 2):
                    interleave([blockA1(b0), blockA1(b0 + 1)])

                ipt_box = [0]

                def prologueA(qg):
                    sl = qg % 2
                    qk = "QTA%d" % sl
                    for h in range(4):
                        DMA("sp", QTA[sl][96:104, h, :], I["indq"][:, qg * 512:(qg + 1) * 512], w=[qk])
                        yield
                    for bb in range(4):
                        b = qg * 4 + bb
                        s = b % 2
                        P1, Pq = pf[0], pf[1]
                        DMA("sp", rt[s][:], I["ropetab"][b * 128:(b + 1) * 128, :], w=["rtA%d" % s])
                        yield
                        for c in range(8):
                            MM(P1[:, 0:256], hT[:, c, b * 128:(b + 1) * 128], WinQ[:, c, :], start=(c == 0), stop=(c == 7),
                               r=[("hT", b // 4), "WinQ"], w=["pf0"], inc=(c == 7))
                            yield
                        MSET("dve", ssq[s][:], 0.0, w=["ssqA%d" % s])
                        yield
                        ACT(junk[s][:], P1[:, 0:256], AF.Square, r=["pf0"], w=["junkA%d" % s, "ssqA%d" % s],
                            accum=ssq[s][:, 0:1])
                        yield
                        ACT(rs[s][:, 0:1], ssq[s][:, 0:1], AF.Ln, r=["ssqA%d" % s], w=["rsA%d" % s], scale=1.0 / 256.0,
                            bias=EPSC)
                        yield
                        ACT(rs[s][:, 0:1], rs[s][:, 0:1], AF.Exp, r=["rsA%d" % s], w=["rsA%d" % s], scale=-0.5)
                        yield
                        TS("dve", rs[s][:, 1:2], rs[s][:, 0:1], MLA_SCALE, None, ALU.mult, r=["rsA%d" % s], w=["rsA%d" % s])
                        yield
                        CP("dve", qlb[s][:], P1[:, 0:256], r=["pf0"], w=["qlb%d" % s])
                        yield
                        for kc in range(2):
                            TR(pb[0][:, 128 + kc * 128:256 + kc * 128], qlb[s][:, kc * 128:(kc + 1) * 128], ident_bf[:],
                               r=["qlb%d" % s, "ident_bf"], w=["pb0"], inc=(kc == 1))
                            yield
                        CP("act", qlT[s][:], pb[0][:, 128:384].rearrange("p (c t) -> p c t", c=2), r=["pb0"],
                           w=["qlT%d" % s])
                        yield
                        for kc in range(2):
                            MM(Pq[:, 0:384], qlT[s][:, kc, :], Wuq[:, kc, :], start=(kc == 0), stop=(kc == 1),
                               r=["qlT%d" % s, "Wuq"], w=["pf1"], inc=(kc == 1))
                            yield
                        ACT(qs[s][:], Pq[:, 0:384].rearrange("p (h d) -> p h d", h=4), AF.Copy, r=["pf1", "rsA%d" % s],
                            w=["qs%d" % s], scale=rs[s][:, 1:2])
                        yield
                        CP("dve", QA[s][:, :, 0:64], qs[s][:, :, 0:64], r=["qs%d" % s], w=["QA%d" % s])
                        yield
                        rope("dve", QA[s][:, :, 64:96], qs[s][:, :, 64:96], rt[s][:, 0:32], rt[s][:, 32:64],
                             tmpA[s][:], tmpB[s][:], 4, 16, r=["qs%d" % s, "rtA%d" % s], w=["QA%d" % s], ktmp="tmpA%d" % s)
                        yield
                        for h in range(4):
                            TR(pb[0][0:96, 512 + h * 128:640 + h * 128], QA[s][:, h, :], ident_bf[:],
                               r=["QA%d" % s, "ident_bf"], w=["pb0"], inc=(h == 3))
                            yield
                        CP("act", QTA[sl][0:96, :, bb * 128:(bb + 1) * 128],
                           pb[0][0:96, 512:1024].rearrange("p (h t) -> p h t", h=4), r=["pb0"], w=[qk])
                        yield

                def attentionA(qg):
                    sl = qg % 2
                    qk = "QTA%d" % sl
                    tl = [(h, kt) for h in range(4) for kt in range(18)]

                    sbank = [(pf[2], "pf2"), (pf[3], "pf3"), (pb[1][:, 0:1024].bitcast(F32), "pb1")]

                    def qk_A(i):
                        h, kt = tl[i]
                        MM(sbank[i % 3][0][:, 0:512], KTA[0:104, h, kt * 128:(kt + 1) * 128], QTA[sl][0:104, h, :],
                           r=["KTA", qk], w=[sbank[i % 3][1]], inc=True)
                    qk_A(0)
                    qk_A(1)
                    for i, (h, kt) in enumerate(tl):
                        if i + 2 < len(tl):
                            qk_A(i + 2)
                        Oacc = pf[4 + (h % 2)]
                        ok = "pf%d" % (4 + (h % 2))
                        pt = PT[ipt_box[0] % 3]
                        ptk = "PTA%d" % (ipt_box[0] % 3)
                        ipt_box[0] += 1
                        ACT(pt[:], sbank[i % 3][0][:, 0:512], AF.Exp, r=[sbank[i % 3][1]], w=[ptk], bias=NEGBIG)
                        MM(Oacc[:, 0:512], VA[:, kt, h // 2, (h % 2) * 64:(h % 2) * 64 + 128], pt[:],
                           start=(kt == 0), stop=(kt == 17), r=["VA", ptk], w=[ok], inc=True)
                        yield
                        if kt == 17:
                            po = (h % 2) * 64
                            pss = 64 - po
                            RCP(rcp[pss:pss + 64, :], Oacc[pss:pss + 64, 0:512], r=[ok], w=["rcpA"])
                            TT("dve", OT[po:po + 64, 0, h // 2, qg * 512:(qg + 1) * 512], Oacc[po:po + 64, 0:512],
                               rcp[pss:pss + 64, :], ALU.mult, r=[ok, "rcpA"], w=[("OT", 0, qg)])

                for _ in prologueA(0):
                    pass
                for qg in range(4):
                    interleave2(attentionA(qg), prologueA(qg + 1) if qg + 1 < 4 else None, 3)
                k.barrier()

        def branch_CD(l, OT, isD):
            nm = "D" if isD else "C"
            br = 3 if isD else 2
            NT = 18 if isD else 20
            kcol = 2208 if isD else 1696
            qcol = 1952 if isD else 1440
            o_k, o_v = (O["o_gk"], O["o_gv"]) if isD else (O["o_wk"], O["o_wv"])
            c_k, c_v = (I["c_gk"], I["c_gv"]) if isD else (I["c_wk"], I["c_wv"])
            with contextlib.ExitStack() as S:
                KT = sb(S, "KT" + nm, [128, 2, NT * 128], BF16)
                VV = sb(S, "VV" + nm, [128, NT, 192], BF16)
                QW = 256 if isD else 128
                QT = [sb(S, "QT%s%d" % (nm, i), [128, 2, 2, QW], BF16) for i in range(2)]
                WinKV = sb(S, "WinKV" + nm, [128, 8, 256], BF16)
                WinQ = sb(S, "WinQ" + nm, [128, 8, 256], BF16)
                gains = sb(S, "gains" + nm, [128, 2, 64], F32)
                rt = [sb(S, "rt%s%d" % (nm, i), [128, 192], F32) for i in range(2)]
                sq = [sb(S, "sq%s%d" % (nm, i), [128, 256], F32) for i in range(2)]
                ssq = [sb(S, "ssq%s%d" % (nm, i), [128, 4], F32) for i in range(2)]
                rs = [sb(S, "rs%s%d" % (nm, i), [128, 4], F32) for i in range(2)]
                x32 = [sb(S, "x32%s%d" % (nm, i), [128, 4, 64], F32) for i in range(2)]
                v32 = [sb(S, "v32%s%d" % (nm, i), [128, 128], F32) for i in range(2)]
                tA = [sb(S, "tA%s%d" % (nm, i), [128, 4, 64], F32) for i in range(2)]
                tB = [sb(S, "tB%s%d" % (nm, i), [128, 4, 64], F32) for i in range(2)]
                xb = [sb(S, "xb%s%d" % (nm, i), [128, 256], BF16) for i in range(2)]
                PT = [sb(S, "PT%s%d" % (nm, i), [128, 512], BF16) for i in range(3)]
                rcp = sb(S, "rcp" + nm, [128, 512], F32)
                bandm = sb(S, "bandm" + nm, [128, 2, 256], BF16)
                K_ = lambda base, i: "%s%s%d" % (base, nm, i)
                modside = None
                if isD and l == 0:
                    wada1 = [sb(S, "wadaD%d" % i, [128, 8, 256], F32) for i in range(2)]
                    modside = mod_gen(1, wada1, 1)

                DMA("sp", WinKV[:], WB["w_in"][l][:, kcol:kcol + 256].rearrange("(c p) n -> p c n", p=128), r=WBK[("w_in_qkv", l)], w=["WinKV"])
                DMA("sp", WinQ[:], WB["w_in"][l][:, qcol:qcol + 256].rearrange("(c p) n -> p c n", p=128), r=WBK[("w_in_qkv", l)], w=["WinQ"])
                if isD:
                    DMA("sp", gains[:, 0, :], I["gqa_q_norm"][l].partition_broadcast(128), w=["gains"])
                    DMA("sp", gains[:, 1, :], I["gqa_k_norm"][l].partition_broadcast(128), w=["gains"])
                    TS("dve", gains[:, 0, :], gains[:, 0, :], ATT_SCALE, None, ALU.mult, r=["gains"], w=["gains"])
                else:
                    if STAGE != "C1":
                        for d_ in range(2):
                            DMA("sp", bandm[:, d_, :], I["bandm"][d_], w=["bandm"])
                MSET("pool", VV[:], 1.0, w=["VV"])
                MSET("pool", KT[:], 0.0, w=["KT"])
                for kp in range(2):
                    DMA("sp", KT[64:72, kp, :], I["indk"] if isD else I["indkc"], r=[], w=["KT"])

                def blockCD1(b):
                    s = b % 2
                    tile_i = b if isD else (b + 1 if b < 16 else b + 2)
                    P1 = pf[s]
                    p1k = "pf%d" % s
                    if b < 16:
                        DMA("sp", rt[s][:], I["ropetab"][b * 128:(b + 1) * 128, :], w=[K_("rt", s)])
                        yield
                        for c in range(8):
                            MM(P1[:, 0:256], hT[:, c, b * 128:(b + 1) * 128], WinKV[:, c, :], start=(c == 0), stop=(c == 7),
                               r=[("hT", b // 4), "WinKV"], w=[p1k], inc=(c == 7))
                        kview = P1[:, 0:128].rearrange("p (h d) -> p h d", h=2)
                        if isD:
                            ACT(sq[s][:, 0:128], P1[:, 0:128], AF.Square, r=[p1k], w=[K_("sq", s)])
                            yield
                            RED(ssq[s][:, 0:2], sq[s][:, 0:128].rearrange("p (h d) -> p h d", h=2), r=[K_("sq", s)],
                                w=[K_("ssq", s)])
                            yield
                            ACT(rs[s][:, 0:2], ssq[s][:, 0:2], AF.Ln, r=[K_("ssq", s)], w=[K_("rs", s)], scale=1.0 / 64.0,
                                bias=EPSC)
                            yield
                            ACT(rs[s][:, 0:2], rs[s][:, 0:2], AF.Exp, r=[K_("rs", s)], w=[K_("rs", s)], scale=-0.5)
                            yield
                            TT("dve", x32[s][:, 0:2, :], kview, rs[s][:, 0:2].unsqueeze(2).to_broadcast([128, 2, 64]), ALU.mult,
                               r=[p1k, K_("rs", s)], w=[K_("x32", s)])
                            yield
                            TT("dve", x32[s][:, 0:2, :], x32[s][:, 0:2, :],
                               gains[:, 1, :].unsqueeze(1).to_broadcast([128, 2, 64]), ALU.mult,
                               r=[K_("x32", s), "gains"], w=[K_("x32", s)])
                            yield
                        else:
                            CP("dve", x32[s][:, 0:2, :], kview, r=[p1k], w=[K_("x32", s)])
                            yield
                        DMA("sp", o_k[l, b * 128:(b + 1) * 128, :], x32[s][:, 0:2, :].rearrange("p h d -> p (h d)"),
                            r=[K_("x32", s)])
                        yield
                        CP("act", v32[s][:], P1[:, 128:256], r=[p1k], w=[K_("v32", s)])
                        yield
                        DMA("sp", o_v[l, b * 128:(b + 1) * 128, :], v32[s][:], r=[K_("v32", s)])
                        yield
                        rope("dve", xb[s][:, 0:128].rearrange("p (h d) -> p h d", h=2), x32[s][:, 0:2, :],
                             rt[s][:, 64:128], rt[s][:, 128:192], tA[s][:, 0:2, :], tB[s][:, 0:2, :], 2, 32,
                             r=[K_("x32", s), K_("rt", s)], w=[K_("xb", s)], ktmp=K_("t", s))
                        yield
                    else:
                        j = b - 16
                        DMA("sp", x32[s][:, 0:2, :].rearrange("p h d -> p (h d)"), c_k[l, j * 128:(j + 1) * 128, :],
                            w=[K_("x32", s)])
                        yield
                        DMA("sp", v32[s][:], c_v[l, j * 128:(j + 1) * 128, :], w=[K_("v32", s)])
                        yield
                        CP("dve", xb[s][:, 0:128], x32[s][:, 0:2, :].rearrange("p h d -> p (h d)"), r=[K_("x32", s)],
                           w=[K_("xb", s)])
                        yield
                    TR(pb[1][:, s * 128:(s + 1) * 128], xb[s][:, 0:128], ident_bf[:], r=[K_("xb", s), "ident_bf"], w=["pb1"])
                    yield
                    CP("act", KT[0:64, 0, tile_i * 128:(tile_i + 1) * 128], pb[1][0:64, s * 128:(s + 1) * 128], r=["pb1"], w=["KT"])
                    yield
                    CP("dve", KT[0:64, 1, tile_i * 128:(tile_i + 1) * 128], pb[1][64:128, s * 128:(s + 1) * 128], r=["pb1"], w=["KT"])
                    yield
                    CP("act", VV[:, tile_i, 0:64], v32[s][:, 0:64], r=[K_("v32", s)], w=["VV"])
                    yield
                    CP("dve", VV[:, tile_i, 128:192], v32[s][:, 64:128], r=[K_("v32", s)], w=["VV"])
                    yield

                for b0 in range(0, 18, 2):
                    interleave([blockCD1(b0), blockCD1(b0 + 1)])

                ipt = 0
                NG = 8 if isD else 16
                if STAGE == "C1":
                    NG = 0
                if STAGE == "C2":
                    NG = 2
                BPG = 2 if isD else 1
                ipt_box = [0]

                def prologueCD(qg):
                    sl = qg % 2
                    qk = K_("QT", sl)
                    for kp in range(2):
                        for g in range(2):
                            DMA("sp", QT[sl][64:72, kp, g, :], I["indq"][:, qg * QW:(qg + 1) * QW], w=[qk])
                            yield
                    for bb in range(BPG):
                        b = qg * BPG + bb
                        s = b % 2
                        P1 = pf[0]
                        DMA("sp", rt[s][:], I["ropetab"][b * 128:(b + 1) * 128, :], w=[K_("rt", s)])
                        yield
                        for c in range(8):
                            MM(P1[:, 0:256], hT[:, c, b * 128:(b + 1) * 128], WinQ[:, c, :], start=(c == 0), stop=(c == 7),
                               r=[("hT", b // 4), "WinQ"], w=["pf0"], inc=(c == 7))
                            yield
                        qview = P1[:, 0:256].rearrange("p (h d) -> p h d", h=4)
                        if isD:
                            ACT(sq[s][:], P1[:, 0:256], AF.Square, r=["pf0"], w=[K_("sq", s)])
                            yield
                            RED(ssq[s][:], sq[s][:].rearrange("p (h d) -> p h d", h=4), r=[K_("sq", s)], w=[K_("ssq", s)])
                            yield
                            ACT(rs[s][:], ssq[s][:], AF.Ln, r=[K_("ssq", s)], w=[K_("rs", s)], scale=1.0 / 64.0, bias=EPSC)
                            yield
                            ACT(rs[s][:], rs[s][:], AF.Exp, r=[K_("rs", s)], w=[K_("rs", s)], scale=-0.5)
                            yield
                            TT("dve", x32[s][:], qview, rs[s][:].unsqueeze(2).to_broadcast([128, 4, 64]), ALU.mult,
                               r=["pf0", K_("rs", s)], w=[K_("x32", s)])
                            yield
                            TT("dve", x32[s][:], x32[s][:], gains[:, 0, :].unsqueeze(1).to_broadcast([128, 4, 64]), ALU.mult,
                               r=[K_("x32", s), "gains"], w=[K_("x32", s)])
                            yield
                        else:
                            ACT(x32[s][:], qview, AF.Copy, r=["pf0"], w=[K_("x32", s)], scale=ATT_SCALE)
                            yield
                        rope("dve", xb[s][:].rearrange("p (g k d) -> p k g d", g=2, k=2),
                             x32[s][:].rearrange("p (k g) d -> p k g d", k=2), rt[s][:, 64:128], rt[s][:, 128:192],
                             tA[s][:].rearrange("p (k g) d -> p k g d", k=2), tB[s][:].rearrange("p (k g) d -> p k g d", k=2),
                             4, 32, r=[K_("x32", s), K_("rt", s)], w=[K_("xb", s)], ktmp=K_("t", s))
                        yield
                        for g in range(2):
                            TR(pb[0][:, g * 128:(g + 1) * 128], xb[s][:, g * 128:(g + 1) * 128], ident_bf[:],
                               r=[K_("xb", s), "ident_bf"], w=["pb0"], inc=(g == 1))
                            yield
                        tv = pb[0][:, 0:256].rearrange("p (g t) -> p g t", g=2)
                        CP("act", QT[sl][0:64, 0, :, bb * 128:(bb + 1) * 128], tv[0:64, :, :], r=["pb0"], w=[qk])
                        yield
                        CP("dve", QT[sl][0:64, 1, :, bb * 128:(bb + 1) * 128], tv[64:128, :, :], r=["pb0"], w=[qk])
                        yield

                def attentionCD(qg):
                    sl = qg % 2
                    qk = K_("QT", sl)
                    NW = 2 * QW
                    if isD:
                        tiles = [(kt, None) for kt in range(18)]
                    else:
                        tiles = [(qg, 0), (qg + 1, None), (qg + 2, 1), (18, None), (19, None)]
                    tl = [(kp, ti) for kp in range(2) for ti in range(len(tiles))]

                    sbank = [(pf[2], "pf2"), (pf[3], "pf3"), (pb[1][:, 0:1024].bitcast(F32), "pb1")]

                    def qk_CD(i):
                        kp, ti = tl[i]
                        kt = tiles[ti][0]
                        MM(sbank[i % 3][0][:, 0:NW], KT[0:72, kp, kt * 128:(kt + 1) * 128], QT[sl][0:72, kp, :, :],
                           start=True, stop=True, r=["KT", qk], w=[sbank[i % 3][1]], inc=True)
                    qk_CD(0)
                    qk_CD(1)
                    for i, (kp, ti) in enumerate(tl):
                        kt, bm = tiles[ti]
                        if i + 2 < len(tl):
                            qk_CD(i + 2)
                        Oacc = pf[4 + kp]
                        ok = "pf%d" % (4 + kp)
                        pk = sbank[i % 3][1]
                        pt = PT[ipt_box[0] % 3]
                        ptk = K_("PT", ipt_box[0] % 3)
                        ipt_box[0] += 1
                        ACT(pt[:, 0:NW], sbank[i % 3][0][:, 0:NW], AF.Exp, r=[pk], w=[ptk], bias=NEGBIG)
                        if bm is not None:
                            TT("dve", pt[:, 0:NW], pt[:, 0:NW], bandm[:, bm, :], ALU.mult, r=[ptk, "bandm"], w=[ptk])
                        MM(Oacc[:, 0:NW], VV[:, kt, kp * 64:kp * 64 + 128], pt[:, 0:NW],
                           start=(ti == 0), stop=(ti == len(tiles) - 1), r=["VV", ptk], w=[ok], inc=True)
                        yield
                        if ti != len(tiles) - 1:
                            continue
                        po = kp * 64
                        pss = 64 - po
                        if isD:
                            RCP(rcp[pss:pss + 64, 0:NW], Oacc[pss:pss + 64, 0:NW], r=[ok], w=["rcp"])
                        else:
                            for g in range(2):
                                h = 2 * kp + g
                                ACT(rcp[pss:pss + 64, g * QW:(g + 1) * QW], Oacc[pss:pss + 64, g * QW:(g + 1) * QW], AF.Ln,
                                    r=[ok, "esink"], w=["rcp"], bias=esink[pss:pss + 64, 4 * l + h:4 * l + h + 1])
                            ACT(rcp[pss:pss + 64, 0:NW], rcp[pss:pss + 64, 0:NW], AF.Exp, r=["rcp"], w=["rcp"], scale=-1.0)
                        for g in range(2):
                            TT("dve", OT[g * 64:(g + 1) * 64, br, kp, qg * QW:(qg + 1) * QW],
                               Oacc[po:po + 64, g * QW:(g + 1) * QW], rcp[pss:pss + 64, g * QW:(g + 1) * QW], ALU.mult,
                               r=[ok, "rcp"], w=[("OT", br, (qg * QW) // 512)])

                if NG > 0:
                    for _ in prologueCD(0):
                        pass
                def chain(*gs):
                    for g_ in gs:
                        if g_ is not None:
                            for _ in g_:
                                yield

                for qg in range(NG):
                    side = prologueCD(qg + 1) if qg + 1 < NG else None
                    if modside is not None and qg >= 1:

                        side = chain(side, itertools.islice(modside, 14))
                    interleave2(attentionCD(qg), side, 3)
                if modside is not None:
                    for _ in modside:
                        pass
                k.barrier()

        def branch_B(l, OT):
            with contextlib.ExitStack() as S:
                QTB = sb(S, "QTB", [128, 2, T], BF16)
                KTB = sb(S, "KTB", [128, 2, T], BF16)
                VB = sb(S, "VB", [128, 16, 256], BF16)
                SG = sb(S, "SG", [128, 16, 256], BF16)
                Ust = sb(S, "Ust", [128, 16, 2, 2, 64], F32)
                Dcomb = sb(S, "Dcomb", [128, 4, 128], F32)
                rc = sb(S, "rc", [128, 770], F32)
                DMA("sp", rc[:], I["rconst"], w=["rc"])
                keepfb = sb(S, "keepfb", [128, 32], F32)
                DMA("sp", keepfb[:], I["keepfb"].partition_broadcast(128), w=["keepfb"])
                qdec = sb(S, "qdec", [128, 2, 2, 128], F32)
                kdec = sb(S, "kdec", [128, 8], F32)
                cdec = sb(S, "cdec", [128, 4], F32)
                car = sb(S, "car", [128, 2, 2, 16], F32)
                lgp = sb(S, "lgp", [128, 4], F32)
                gnb = sb(S, "gnb", [128, 256], F32)
                Sst = [sb(S, "Sst%d" % i, [128, 2, 64], F32) for i in range(2)]
                tmpD = sb(S, "tmpD", [128, 128], F32)
                for h in range(4):
                    cf = l * 4 + h
                    cb = 8 + l * 4 + h
                    ACT(Dcomb[:, h, :], rc[:, 0:128], AF.Exp, r=["rc", "lgall"], w=["Dcomb"], scale=lgall[:, cf:cf + 1])
                    TT("dve", Dcomb[:, h, :], Dcomb[:, h, :], rc[:, 128:256], ALU.mult, r=["Dcomb", "rc"], w=["Dcomb"])
                    ACT(tmpD[:], rc[:, 256:384], AF.Exp, r=["rc", "lgall"], w=["tmpD"], scale=lgall[:, cb:cb + 1])
                    TT("dve", tmpD[:], tmpD[:], rc[:, 384:512], ALU.mult, r=["tmpD", "rc"], w=["tmpD"])
                    TT("dve", Dcomb[:, h, :], Dcomb[:, h, :], tmpD[:], ALU.add, r=["Dcomb", "tmpD"], w=["Dcomb"])
                for d in range(2):
                    for pr in range(2):
                        c0 = d * 8 + l * 4 + 2 * pr
                        CP("dve", lgp[0:64, d * 2 + pr:d * 2 + pr + 1], lgall[0:64, c0:c0 + 1], r=["lgall"], w=["lgp"])
                        CP("dve", lgp[64:128, d * 2 + pr:d * 2 + pr + 1], lgall[64:128, c0 + 1:c0 + 2], r=["lgall"], w=["lgp"])
                for pr in range(2):
                    ACT(qdec[:, 0, pr, :], rc[:, 512:640], AF.Exp, r=["rc", "lgp"], w=["qdec"], scale=lgp[:, pr:pr + 1])
                    ACT(qdec[:, 1, pr, :], rc[:, 640:768], AF.Exp, r=["rc", "lgp"], w=["qdec"], scale=lgp[:, 2 + pr:3 + pr])
                ACT(kdec[:, 0:4], lgall[:, l * 4:l * 4 + 4], AF.Exp, r=["rc", "lgall"], w=["kdec"], scale=rc[:, 768:769])
                ACT(kdec[:, 4:8], lgall[:, 8 + l * 4:12 + l * 4], AF.Exp, r=["rc", "lgall"], w=["kdec"], scale=rc[:, 769:770])
                TS("dve", kdec[:], kdec[:], 0.125, None, ALU.mult, r=["kdec"], w=["kdec"])
                ACT(cdec[:], lgp[:], AF.Exp, r=["lgp"], w=["cdec"], scale=128.0)
                for d in range(2):
                    for pr in range(2):
                        TS("dve", car[:, d, pr, :], keepfb[:, d * 16:(d + 1) * 16], cdec[:, d * 2 + pr:d * 2 + pr + 1], None,
                           ALU.mult, r=["keepfb", "cdec"], w=["car"])
                DMA("sp", gnb[:], I["ret_gn_gain"][l].partition_broadcast(128), w=["gnb"])
                TS("dve", gnb[:], gnb[:], 1.0, None, ALU.mult, r=["gnb"], w=["gnb"])

                with contextlib.ExitStack() as SW:
                    WinQK = sb(SW, "WinQK", [128, 8, 512], BF16)
                    DMA("sp", WinQK[:], WB["w_in"][l][:, 416:928].rearrange("(c p) n -> p c n", p=128), r=WBK[("w_in_qkv", l)], w=["WinQK"])
                    ii = 0
                    for g4 in range(4):
                        for ct in range(4):
                            Pq = pf[ii % 2]
                            pqk = "pf%d" % (ii % 2)
                            ii += 1
                            for c in range(8):
                                MM(Pq[:, 0:512], WinQK[:, c, ct * 128:(ct + 1) * 128], hT[:, c, g4 * 512:(g4 + 1) * 512],
                                   start=(c == 0), stop=(c == 7), r=["WinQK", ("hT", g4)], w=[pqk], inc=(c == 7))
                            if ct < 2:
                                CP("act", QTB[:, ct, g4 * 512:(g4 + 1) * 512], Pq[:, 0:512], r=[pqk], w=[("QTB", g4)])
                            else:
                                ACT(KTB[:, ct - 2, g4 * 512:(g4 + 1) * 512], Pq[:, 0:512], AF.Copy, r=[pqk], w=[("KTB", g4)],
                                    scale=0.125)
                    k.barrier()
                with contextlib.ExitStack() as SW:
                    WinKVG = sb(SW, "WinKVG", [128, 8, 768], BF16)
                    KdF = [sb(SW, "KdF%d" % i, [128, 256], BF16) for i in range(2)]
                    KdB = [sb(SW, "KdB%d" % i, [128, 256], BF16) for i in range(2)]
                    Eg = [sb(SW, "Eg%d" % i, [128, 256], F32) for i in range(2)]
                    DMA("sp", WinKVG[:], WB["w_in"][l][:, 672:1440].rearrange("(c p) n -> p c n", p=128), r=WBK[("w_in_qkv", l)], w=["WinKVG"])
                    def inprojB(b):
                        s = b % 2
                        for c in range(8):
                            MM(pf[s][:, 0:512], hT[:, c, b * 128:(b + 1) * 128], WinKVG[:, c, 0:512], start=(c == 0), stop=(c == 7),
                               r=["WinKVG", ("hT", b // 4)], w=["pf%d" % s], inc=(c == 7))
                        for c in range(8):
                            MM(pf[2 + s][:, 0:256], hT[:, c, b * 128:(b + 1) * 128], WinKVG[:, c, 512:768], start=(c == 0),
                               stop=(c == 7), r=["WinKVG", ("hT", b // 4)], w=["pf%d" % (2 + s)], inc=(c == 7))
                    inprojB(0)
                    for b in range(NB):
                        s = b % 2
                        Pa, Pb, PU = pf[s], pf[2 + s], pf[4 + s]
                        puk = "pf%d" % (4 + s)
                        if b + 1 < NB:
                            inprojB(b + 1)
                        kv_ = Pa[:, 0:256].rearrange("p (h d) -> p h d", h=4)
                        TT("dve", KdF[s][:].rearrange("p (h d) -> p h d", h=4), kv_,
                           kdec[:, 0:4].unsqueeze(2).to_broadcast([128, 4, 64]), ALU.mult, r=["pf%d" % s, "kdec"], w=["KdF%d" % s])
                        TT("dve", KdB[s][:].rearrange("p (h d) -> p h d", h=4), kv_,
                           kdec[:, 4:8].unsqueeze(2).to_broadcast([128, 4, 64]), ALU.mult, r=["pf%d" % s, "kdec"], w=["KdB%d" % s])
                        CP("act", VB[:, b, :], Pa[:, 256:512], r=["pf%d" % s], w=[("VB", b)])
                        ACT(Eg[s][:], Pb[:, 0:256], AF.Exp, r=["pf%d" % (2 + s)], w=["Eg%d" % s], scale=-1.0)
                        ACT(Eg[s][:], Eg[s][:], AF.Ln, r=["Eg%d" % s], w=["Eg%d" % s], bias=ONEC)
                        ACT(Eg[s][:], Eg[s][:], AF.Exp, r=["Eg%d" % s], w=["Eg%d" % s], scale=-1.0)
                        TT("dve", SG[:, b, :], Pb[:, 0:256], Eg[s][:], ALU.mult, r=["pf%d" % (2 + s), "Eg%d" % s], w=[("SG", b)])
                        for d in range(2):
                            Kd = KdF[s] if d == 0 else KdB[s]
                            for pr in range(2):
                                j = d * 2 + pr
                                MM(PU[:, j * 128:(j + 1) * 128], Kd[:, pr * 128:(pr + 1) * 128], VB[:, b, pr * 128:(pr + 1) * 128],
                                   r=["KdF%d" % s, "KdB%d" % s, ("VB", b)], w=[puk], inc=(j == 3))
                        puv = PU[:, 0:512].rearrange("p (j e) -> p j e", j=4)
                        uv = Ust[:, b].rearrange("p d r e -> p (d r) e")
                        CP("act", uv[0:64, :, :], puv[0:64, :, 0:64], r=[puk], w=[("Ust", b)])
                        CP("dve", uv[64:128, :, :], puv[64:128, :, 64:128], r=[puk], w=[("Ust", b)])
                    k.barrier()

                Sbf = sb(S, "Sbf", [128, 2, 16, 2, 64], BF16)
                for d in range(2):
                    src = I["sf0"] if d == 0 else I["sb0"]
                    dst = O["o_rf"] if d == 0 else O["o_rb"]
                    cur = 0
                    for hp in range(2):
                        DMA("sp", Sst[cur][hp * 64:(hp + 1) * 64, :, :],
                            src[l].rearrange("(pr hp) dd e -> hp dd pr e", hp=2)[hp], w=["Sst%d" % cur])
                    order = range(16) if d == 0 else range(15, -1, -1)
                    for n in order:
                        nxt = 1 - cur
                        TS("dve", Sbf[:, d, n], Sst[cur][:], keepfb[:, d * 16 + n:d * 16 + n + 1], None, ALU.mult,
                           r=["Sst%d" % cur, "keepfb"], w=[("Sbf", n)])
                        for pr in range(2):
                            STT("dve", Sst[nxt][:, pr, :], Sst[cur][:, pr, :], car[:, d, pr, n:n + 1], Ust[:, n, d, pr, :],
                                ALU.mult, ALU.add, r=["Sst%d" % cur, "car", ("Ust", n)], w=["Sst%d" % nxt])
                        if (d == 0 and n % 2 == 1) or (d == 1 and n % 2 == 0):
                            for hp in range(2):
                                DMA("sp", dst[l, n // 2].rearrange("(pr hp) dd e -> hp dd pr e", hp=2)[hp],
                                    Sst[nxt][hp * 64:(hp + 1) * 64, :, :], r=["Sst%d" % nxt])
                        cur = nxt

                with contextlib.ExitStack() as S2:
                    AT = [sb(S2, "AT%d" % i, [128, 4, 128], BF16) for i in range(2)]
                    Qd = [sb(S2, "Qd%d" % i, [128, 2, 2, 128], BF16) for i in range(2)]
                    sqo = [sb(S2, "sqo%d" % i, [128, 256], F32) for i in range(2)]
                    st = [sb(S2, "st%d" % i, [128, 16], F32) for i in range(2)]
                    tt = [sb(S2, "tt%d" % i, [128, 4, 64], F32) for i in range(2)]
                    OB = [sb(S2, "OB%d" % i, [128, 256], BF16) for i in range(2)]
                    def attB(n):
                        s = n % 2
                        cs = slice(n * 128, (n + 1) * 128)
                        for h in range(4):
                            hp, pr = h % 2, h // 2
                            MM(pf[4 * s + hp][:, pr * 128:(pr + 1) * 128], KTB[hp * 64:(hp + 1) * 64, pr, cs],
                               QTB[hp * 64:(hp + 1) * 64, pr, cs],
                               r=[("KTB", n // 4), ("QTB", n // 4)], w=["pf%d" % (4 * s + hp)], inc=(h >= 2))
                    attB(0)
                    for n in range(NB):
                        s = n % 2
                        Po = pf[2 + s]
                        pok = "pf%d" % (2 + s)
                        cs = slice(n * 128, (n + 1) * 128)
                        if n + 1 < NB:
                            attB(n + 1)
                        for hp in range(2):
                            TT("dve", AT[s][:].rearrange("p (a b) i -> p a b i", b=2)[:, :, hp, :],
                               pf[4 * s + hp][:, 0:256].rearrange("p (a i) -> p a i", a=2),
                               Dcomb[:].rearrange("p (a b) i -> p a b i", b=2)[:, :, hp, :], ALU.mult,
                               r=["pf%d" % (4 * s + hp), "Dcomb"], w=["AT%d" % s])
                        for d in range(2):
                            TT("pool", Qd[s][:, d], QTB[:, :, cs], qdec[:, d], ALU.mult, r=[("QTB", n // 4), "qdec"],
                               w=["Qd%d" % s])
                        for h in range(4):
                            hp, pr = h % 2, h // 2
                            ps_ = slice(hp * 64, (hp + 1) * 64)
                            MM(Po[:, h * 64:(h + 1) * 64], AT[s][:, h, :], VB[:, n, h * 64:(h + 1) * 64], start=True, stop=False,
                               r=["AT%d" % s, ("VB", n)], w=[pok])
                            MM(Po[:, h * 64:(h + 1) * 64], Qd[s][ps_, 0, pr, :], Sbf[ps_, 0, n, pr, :], start=False, stop=False,
                               r=["Qd%d" % s, ("Sbf", n)], w=[pok])
                            MM(Po[:, h * 64:(h + 1) * 64], Qd[s][ps_, 1, pr, :], Sbf[ps_, 1, n, pr, :], start=False, stop=True,
                               r=["Qd%d" % s, ("Sbf", n)], w=[pok], inc=(h == 3))
                        pov = Po[:, 0:256].rearrange("p (h e) -> p h e", h=4)
                        stk = "st%d" % s
                        RED(st[s][:, 0:4], pov, r=[pok], w=[stk])
                        ACT(sqo[s][:], Po[:, 0:256], AF.Square, r=[pok], w=["sqo%d" % s])
                        RED(st[s][:, 4:8], sqo[s][:].rearrange("p (h e) -> p h e", h=4), r=["sqo%d" % s], w=[stk])
                        TS("dve", st[s][:, 0:4], st[s][:, 0:4], 1.0 / 64.0, None, ALU.mult, r=[stk], w=[stk])
                        TT("dve", st[s][:, 8:12], st[s][:, 0:4], st[s][:, 0:4], ALU.mult, r=[stk], w=[stk])
                        STT("dve", st[s][:, 12:16], st[s][:, 4:8], 1.0 / 64.0, st[s][:, 8:12], ALU.mult, ALU.subtract,
                            r=[stk], w=[stk])
                        ACT(st[s][:, 12:16], st[s][:, 12:16], AF.Ln, r=[stk], w=[stk], bias=EPSC)
                        ACT(st[s][:, 12:16], st[s][:, 12:16], AF.Exp, r=[stk], w=[stk], scale=-0.5)
                        TT("dve", tt[s][:], pov, st[s][:, 0:4].unsqueeze(2).to_broadcast([128, 4, 64]), ALU.subtract,
                           r=[pok, stk], w=["tt%d" % s])
                        TT("dve", tt[s][:], tt[s][:], st[s][:, 12:16].unsqueeze(2).to_broadcast([128, 4, 64]), ALU.mult,
                           r=["tt%d" % s, stk], w=["tt%d" % s])
                        TT("dve", tt[s][:], tt[s][:], gnb[:].rearrange("p (h e) -> p h e", h=4), ALU.mult,
                           r=["tt%d" % s, "gnb"], w=["tt%d" % s])
                        TT("dve", OB[s][:], tt[s][:].rearrange("p h e -> p (h e)"), SG[:, n, :], ALU.mult,
                           r=["tt%d" % s, ("SG", n)], w=["OB%d" % s])
                        for pr in range(2):
                            TR(pb[0][:, pr * 128:(pr + 1) * 128], OB[s][:, pr * 128:(pr + 1) * 128], ident_bf[:],
                               r=["OB%d" % s, "ident_bf"], w=["pb0"], inc=(pr == 1))
                        CP("act", OT[:, 1, :, cs], pb[0][:, 0:256].rearrange("p (r t) -> p r t", r=2), r=["pb0"],
                           w=[("OT", 1, n // 4)])
                    k.barrier()

        def ln_finalize_gen(l, which, tok, Pmean, Pex2, mk, ek, LT, write_h):
            (msq, kmsq), (rstd, krstd), (nmr, knmr), tt = LT
            gi, bi = (0, 1) if which == 1 else (2, 3)
            ACT(msq[:], Pmean[:, 0:512], AF.Square, r=[mk], w=[kmsq])
            yield
            TT("dve", rstd[:], Pex2[:, 0:512], msq[:], ALU.subtract, r=[ek, kmsq], w=[krstd])
            yield
            ACT(rstd[:], rstd[:], AF.Ln, r=[krstd], w=[krstd], bias=EPSC)
            yield
            ACT(rstd[:], rstd[:], AF.Exp, r=[krstd], w=[krstd], scale=-0.5)
            yield
            STT("dve", nmr[:], Pmean[:, 0:512], -1.0, rstd[:], ALU.mult, ALU.mult, r=[mk, krstd], w=[knmr])
            yield
            g4 = tok.start // 512
            for fo in range(8):
                t_, tk = tt[fo % 2]
                TT("dve", t_[:], xT[:, fo, tok], rstd[:], ALU.mult, r=[("xT", g4), krstd], w=[tk])
                yield
                TT("pool", t_[:], t_[:], nmr[:], ALU.add, r=[tk, knmr], w=[tk])
                yield
                ACT(xT[:, fo, tok], t_[:], AF.Identity, r=[tk, "lnp"], w=[("xT", g4)],
                    scale=lnp[:, gi, l, fo:fo + 1], bias=lnp[:, bi, l, fo:fo + 1])
                yield
                if write_h:
                    sa, ba = (2, 3) if which == 1 else (4, 5)
                    ACT(hT[:, fo, tok], t_[:], AF.Identity, r=[tk, "der"], w=[("hT", g4)],
                        scale=der[:, l, sa, fo:fo + 1], bias=der[:, l, ba, fo:fo + 1])
                    yield


        def ln_finalize(*a):
            for _ in ln_finalize_gen(*a):
                pass

        def residual_and_stats(l, fo, tok, Pz, pzk, gcol, Pmean, Pex2, mk, ek, gz, ub, usq, idx, ubkey="ub%d"):
            g4 = tok.start // 512
            s2 = idx % 2
            gzb, gzk = gz[s2]
            ACT(gzb[:], Pz[:, 0:512], AF.Copy, r=[pzk, "der"], w=[gzk], scale=der[:, l, gcol, fo:fo + 1])
            STT("dve", xT[:, fo, tok], xT[:, fo, tok], ALPHA, gzb[:], ALU.mult, ALU.add,
                r=[("xT", g4), gzk], w=[("xT", g4)])
            CP("pool", ub[s2][:], xT[:, fo, tok], r=[("xT", g4)], w=[ubkey % s2])
            ACT(usq[s2][:], xT[:, fo, tok], AF.Square, r=[("xT", g4)], w=["usq%d" % s2])

            def stats():
                MM(Pmean[:, 0:512], ones_div[:], ub[s2][:], start=(fo == 0), stop=(fo == 7), r=["ones_div", ubkey % s2],
                   w=[mk], inc=True)
                MM(Pex2[:, 0:512], ones_div[:], usq[s2][:], start=(fo == 0), stop=(fo == 7), r=["ones_div", "usq%d" % s2],
                   w=[ek], inc=True)
            return stats

        def merge(l, OT):
            with contextlib.ExitStack() as S:
                Wbr = sb(S, "Wbr", [128, 4, 2, 1024], BF16)
                Wo = sb(S, "Wo", [128, 8, 1024], BF16)
                Wg = [sb(S, "Wg%d" % i, [128, 4, 8, 128], BF16) for i in range(2)]
                mT = sb(S, "mT", [128, 8, 512], BF16)
                sig = [sb(S, "sig%d" % i, [128, 512], BF16) for i in range(2)]
                W5 = [sb(S, "w5_%d" % i, [128, 512], F32) for i in range(5)]
                t32 = W5[0:3]
                gz = [(W5[3], "w5_3"), (W5[4], "w5_4")]
                ub = sig
                usq = [sb(S, "usq%d" % i, [128, 512], BF16) for i in range(2)]
                L3 = [sb(S, "l3_%d" % i, [128, 512], F32) for i in range(3)]
                LT = ((L3[0], "l3_0"), (L3[1], "l3_1"), (L3[2], "l3_2"), [(W5[3], "w5_3"), (W5[4], "w5_4")])
                pend_ln = [None]
                ig_box = [0]
                sig3 = sig + [sb(S, "sig2", [128, 512], BF16)]
                gbank = [(pf[0], "pf0"), (pf[1], "pf1"), (pb[0][:, 0:1024].bitcast(F32), "pb0")]
                ybank = [(pf[2], "pf2"), (pf[3], "pf3"), (pb[1][:, 0:1024].bitcast(F32), "pb1")]
                for i in range(4):
                    DMA("sp", Wbr[:, i], WB["w_branch"][l][i].rearrange("(kc p) n -> p kc n", p=128), r=WBK[("w_branch", l)], w=["Wbr"])
                DMA("sp", Wo[:], WB["w_o"][l].rearrange("(c p) n -> p c n", p=128), r=WBK[("w_o", l)], w=["Wo"])
                iw_box = [0]
                for g4 in range(4):
                    tok = slice(g4 * 512, (g4 + 1) * 512)
                    def gates_gen():
                        for ft in range(8):
                            ws = iw_box[0] % 2
                            wk = "Wg%d" % ws
                            iw_box[0] += 1
                            for br in range(4):
                                c0 = 2464 + br * 1024 + ft * 128
                                DMA("sp", Wg[ws][:, br], WB["w_in"][l][:, c0:c0 + 128].rearrange("(c p) n -> p c n", p=128), r=WBK[("w_in_g", l)], w=[wk])
                            for br in range(4):
                                ig = ig_box[0]
                                ig_box[0] += 1
                                Pg, pgk = gbank[ig % 3]
                                Py, pyk = ybank[ig % 3]
                                sg_, sgk = sig3[ig % 3], "sig%d" % (ig % 3)
                                for c in range(8):
                                    MM(Pg[:, 0:512], Wg[ws][:, br, c, :], hT[:, c, tok], start=(c == 0), stop=(c == 7),
                                       r=[wk, ("hT", g4)], w=[pgk], inc=(c == 7))
                                ACT(sg_[:], Pg[:, 0:512], AF.Tanh, r=[pgk], w=[sgk], scale=0.5)
                                for kc in range(2):
                                    MM(Py[:, 0:512], Wbr[:, br, kc, ft * 128:(ft + 1) * 128], OT[:, br, kc, tok], start=(kc == 0),
                                       stop=(kc == 1), r=["Wbr", ("OT", br, g4)], w=[pyk], inc=(kc == 1))
                                dst = t32[0] if br == 0 else t32[1 + br % 2]
                                dk = "w5_0" if br == 0 else "w5_%d" % (1 + br % 2)
                                STT("dve", dst[:], sg_[:], 1.0, Py[:, 0:512], ALU.add, ALU.mult,
                                    r=[sgk, pyk], w=[dk])
                                if br in (1, 2):
                                    TT("pool", t32[0][:], t32[0][:], dst[:], ALU.add, r=["w5_0", dk], w=["w5_0"])
                                if br == 3:
                                    TT("pool", mT[:, ft, :], t32[0][:], dst[:], ALU.add, r=["w5_0", dk], w=["mT"])
                                yield
                    interleave2(gates_gen(), pend_ln[0], 1)
                    pend = []
                    for fo in range(8):
                        Pz = pf[fo % 2]
                        pzk = "pf%d" % (fo % 2)
                        for ft in range(8):
                            MM(Pz[:, 0:512], Wo[:, ft, fo * 128:(fo + 1) * 128], mT[:, ft, :], start=(ft == 0), stop=(ft == 7),
                               r=["Wo", "mT"], w=[pzk], inc=(ft == 7))
                        if fo >= 1:
                            pend.pop(0)()
                        pend.append(residual_and_stats(l, fo, tok, Pz, pzk, 6, pf[4], pf[5], "pf4", "pf5", gz, ub, usq, fo, "sig%d"))
                    while pend:
                        pend.pop(0)()
                    pend_ln[0] = ln_finalize_gen(l, 1, tok, pf[4], pf[5], "pf4", "pf5", LT, True)
                for _ in pend_ln[0]:
                    pass
                k.barrier()

        def ffn(l):
            with contextlib.ExitStack() as S:
                aT = sb(S, "aT", [128, 32, 1024], BF16)
                Wup = [sb(S, "Wup%d" % i, [128, 8, 256], BF16) for i in range(2)]
                Wdn = [sb(S, "Wdn%d" % i, [128, 32, 128], BF16) for i in range(2)]
                rl = [sb(S, "rl%d" % i, [128, 512], BF16) for i in range(2)]
                W5 = [sb(S, "w5f_%d" % i, [128, 512], F32) for i in range(4)]
                W5 = [W5[0], W5[1], W5[0], W5[2], W5[3]]
                gz = [(W5[3], "w5_3"), (W5[4], "w5_4")]
                ub = [sb(S, "ubf%d" % i, [128, 512], BF16) for i in range(2)]
                usq = [sb(S, "usqf%d" % i, [128, 512], BF16) for i in range(2)]
                LT = ((W5[0], "w5_0"), (W5[1], "w5_1"), (W5[2], "w5_0"), [(W5[3], "w5_3"), (W5[4], "w5_4")])
                iu_box = [0]
                idn = 0
                ir_box = [0]
                pend_ln2 = [None]
                for hf in range(2):
                    def up_gen():
                        for j in range(16):
                            ws = iu_box[0] % 2
                            wk = "Wup%d" % ws
                            iu_box[0] += 1
                            DMA("sp", Wup[ws][:], WB["w_up"][l][:, j * 256:(j + 1) * 256].rearrange("(c p) n -> p c n", p=128), r=WBK[("w_up", l)], w=[wk])
                            for t4 in range(2):
                                fft = j * 2 + t4
                                for g in range(2):
                                    g4 = hf * 2 + g
                                    tok = slice(g4 * 512, (g4 + 1) * 512)
                                    Pu = pf[2 + ir_box[0] % 2]
                                    puk = "pf%d" % (2 + ir_box[0] % 2)
                                    r_ = rl[ir_box[0] % 2]
                                    rk = "rl%d" % (ir_box[0] % 2)
                                    ir_box[0] += 1
                                    for c in range(8):
                                        MM(Pu[:, 0:512], Wup[ws][:, c, t4 * 128:(t4 + 1) * 128], hT[:, c, tok], start=(c == 0),
                                           stop=(c == 7), r=[wk, ("hT", g4)], w=[puk], inc=(c == 7))
                                    ACT(r_[:], Pu[:, 0:512], AF.Relu, r=[puk], w=[rk])
                                    TT("dve" if g == 0 else "pool", aT[:, fft, g * 512:(g + 1) * 512], r_[:], r_[:], ALU.mult, r=[rk],
                                       w=[("aT", g)])
                                    yield
                    interleave2(up_gen(), pend_ln2[0], 1)
                    pendf = []
                    for fo in range(8):
                        ws = idn % 2
                        wk = "Wdn%d" % ws
                        idn += 1
                        DMA("sp", Wdn[ws][:], WB["w_down"][l][:, fo * 128:(fo + 1) * 128].rearrange("(t p) n -> p t n", p=128),
                            r=WBK[("w_down", l)], w=[wk])
                        for g in range(2):
                            g4 = hf * 2 + g
                            tok = slice(g4 * 512, (g4 + 1) * 512)
                            Pd = pf[2 + g]
                            pdk = "pf%d" % (2 + g)
                            for fft in range(32):
                                MM(Pd[:, 0:512], Wdn[ws][:, fft, :], aT[:, fft, g * 512:(g + 1) * 512], start=(fft == 0),
                                   stop=(fft == 31), r=[wk, ("aT", g)], w=[pdk], inc=(fft == 31))
                            Pm, Pe = (pf[0], pf[1]) if g == 0 else (pf[4], pf[5])
                            mk, ek = ("pf0", "pf1") if g == 0 else ("pf4", "pf5")
                            if pendf:
                                pendf.pop(0)()
                            pendf.append(residual_and_stats(l, fo, tok, Pd, pdk, 7, Pm, Pe, mk, ek, gz, ub, usq, 2 * fo + g))
                    while pendf:
                        pendf.pop(0)()
                    lngens = []
                    for g in range(2):
                        g4 = hf * 2 + g
                        tok = slice(g4 * 512, (g4 + 1) * 512)
                        Pm, Pe = (pf[0], pf[1]) if g == 0 else (pf[4], pf[5])
                        mk, ek = ("pf0", "pf1") if g == 0 else ("pf4", "pf5")
                        lngens.append(ln_finalize_gen(l, 2, tok, Pm, Pe, mk, ek, LT, l == 0))
                    if hf == 0:
                        pend_ln2[0] = itertools.chain(*lngens)
                    else:
                        for g_ in lngens:
                            for _ in g_:
                                pass
                k.barrier()

        for l in range(2):
            with contextlib.ExitStack() as SM:
                OT = sb(SM, "OT", [128, 4, 2, T], BF16)
                if STAGE in (None, "A", "L0", "MIX"):
                    branch_A(l, OT)
                if l == 0:
                    precast(0, ["w_in_g", "w_branch", "w_o"])
                else:
                    precast(1, ["w_up"])
                if STAGE in (None, "B", "L0", "MIX"):
                    branch_B(l, OT)
                if l == 0:
                    precast(0, ["w_up"])
                else:
                    precast(1, ["w_down"])
                if STAGE in (None, "C", "CD", "C1", "C2", "L0", "MIX"):
                    branch_CD(l, OT, False)
                if l == 0:
                    precast(0, ["w_down"])
                if STAGE in (None, "D", "CD", "L0", "MIX"):
                    branch_CD(l, OT, True)
                if l == 0:
                    precast(1, ["w_in_qkv"])
                if STAGE in ("A", "C", "D", "CD", "C1", "C2", "B"):
                    dump("OT", OT[:], [128, 4, 2, T], BF16)
                    k.finish()
                    print("ninst", k.ninst, "waits", k.nwaits)
                    return nc, dbg_out
                merge(l, OT)
                k.barrier()
            if STAGE == "MIX":
                dump("xT", xT[:], [128, 8, T])
                dump("hT", hT[:], [128, 8, T], BF16)
                k.finish()
                print("ninst", k.ninst, "waits", k.nwaits)
                return nc, dbg_out
            if l == 0:
                precast(1, ["w_in_g", "w_branch", "w_o"])
            ffn(l)
            if STAGE == "L0":
                dump("xT", xT[:], [128, 8, T])
                dump("hT", hT[:], [128, 8, T], BF16)
                k.finish()
                print("ninst", k.ninst, "waits", k.nwaits)
                return nc, dbg_out

        with contextlib.ExitStack() as S9:
            ys = [sb(S9, "ys%d" % i, [128, 1024], F32) for i in range(2)]
            for b in range(NB):
                yk = "ys%d" % (b % 2)
                for q in range(2):
                    bank = (2 * b + q) % 4
                    for cc in range(4):
                        c = 4 * q + cc
                        TR(pf[bank][:, cc * 128:(cc + 1) * 128], xT[:, c, b * 128:(b + 1) * 128], ident_f[:],
                           r=[("xT", b // 4), "ident_f"], w=["pf%d" % bank], inc=(cc == 3))
                    CP("act" if q == 0 else "dve", ys[b % 2][:, q * 512:(q + 1) * 512], pf[bank][:, 0:512],
                       r=["pf%d" % bank], w=[yk])
                DMA("sp", O["y"][b * 128:(b + 1) * 128, :], ys[b % 2][:], r=[yk])
        print("ninst", k.ninst, "waits", k.nwaits)
        k.finish()
    return nc, dbg_out


def _axial(t, rot_dim):
    rows = t // 64
    row = np.repeat(np.arange(rows, dtype=np.float32), 64)
    col = (np.arange(t) % 64).astype(np.float32)
    n_freq = rot_dim // 4
    inv = (np.float32(10000.0) ** (-np.arange(n_freq, dtype=np.float32) / np.float32(n_freq))).astype(np.float32)
    ang = np.concatenate([row[:, None] * inv, col[:, None] * inv], axis=-1).astype(np.float32)
    return np.cos(ang).astype(np.float32), np.sin(ang).astype(np.float32)


def _mode_tables(sample):
    bf = ml_dtypes.bfloat16
    if sample:
        ca, sa = _axial(T, 32)
        ch, sh = _axial(T, 64)
    else:
        ca, sa = np.ones((T, 16), np.float32), np.zeros((T, 16), np.float32)
        ch, sh = np.ones((T, 32), np.float32), np.zeros((T, 32), np.float32)
    ropetab = np.concatenate([ca, ca, -sa, sa, ch, ch, -sh, sh], axis=1).astype(np.float32)
    indq = np.zeros((8, T), np.float32)
    indk = np.zeros((8, T + 256), np.float32)
    indkc = np.zeros((8, 20 * 128), np.float32)
    if sample:
        indq[0, :] = 1.0
        indk[0, :] = BIG
        indkc[0, 128:128 + T] = BIG
        indkc[0, 18 * 128:] = BIG
    else:
        for s in range(8):
            indq[s, s * 256:(s + 1) * 256] = 1.0
            indk[s, s * 256:(s + 1) * 256] = BIG
            indkc[s, 128 + s * 256:128 + (s + 1) * 256] = BIG
    bandm = np.ones((2, 128, 256), np.float32)
    if sample:
        kk = np.arange(128)[:, None]
        qq = np.arange(128)[None, :]
        m0 = np.where(kk < qq, 0.0, 1.0)
        m1 = np.where(kk > qq, 0.0, 1.0)
        bandm[0] = np.concatenate([m0, m0], axis=1)
        bandm[1] = np.concatenate([m1, m1], axis=1)
    if sample:
        keepfb = np.ones((32,), np.float32)
    else:
        kf = np.array([0.0 if n % 2 == 0 else 1.0 for n in range(16)], np.float32)
        kb = np.array([0.0 if n % 2 == 1 else 1.0 for n in range(16)], np.float32)
        keepfb = np.concatenate([kf, kb])
    return dict(ropetab=ropetab, indq=indq.astype(bf), indk=indk.astype(bf), indkc=indkc.astype(bf),
                bandm=bandm.astype(bf), keepfb=keepfb)


def _rconst():
    j = np.arange(128, dtype=np.float32)[:, None]
    i = np.arange(128, dtype=np.float32)[None, :]
    pdF = np.maximum(i - j, 0.0)
    mkF = (i >= j).astype(np.float32)
    pdB = np.maximum(j - i, 0.0)
    mkB = (j > i).astype(np.float32)
    iq1 = np.broadcast_to(i + 1.0, (128, 128))
    iq2 = np.broadcast_to(128.0 - i, (128, 128))
    jc = np.concatenate([127.0 - j, j], axis=1)
    return np.ascontiguousarray(np.concatenate([pdF, mkF, pdB, mkB, iq1, iq2, jc], axis=1).astype(np.float32))


_NC_CACHE = {}


def _get_nc():
    key = (STAGE,)
    if key not in _NC_CACHE:
        _NC_CACHE[key] = build()
    return _NC_CACHE[key]


def make_in_maps(inputs):
    f32 = lambda a: np.ascontiguousarray(np.asarray(a, dtype=np.float32))
    w = {n: f32(inputs[n]) for n in W_NAMES}
    rconst = _rconst()
    tabs = {True: _mode_tables(True), False: _mode_tables(False)}
    xp = f32(inputs["x_prompt"])
    xs = f32(inputs["x_sample"])
    maps = []
    for core in range(8):
        sample = core >= 4
        m = dict(w)
        m.update(tabs[sample])
        m["rconst"] = rconst
        if sample:
            b = core - 4
            m["xin"] = np.ascontiguousarray(xs[b])
            m["cond"] = f32(inputs["c"][b])
            m["c_ckv"] = f32(inputs["cache_mla_ckv"][b])
            m["c_kpe"] = f32(inputs["cache_mla_kpe"][b])
            m["c_wk"] = f32(inputs["cache_win_k"][b]).reshape(2, 256, 128)
            m["c_wv"] = f32(inputs["cache_win_v"][b]).reshape(2, 256, 128)
            m["c_gk"] = f32(inputs["cache_gqa_k"][b]).reshape(2, 256, 128)
            m["c_gv"] = f32(inputs["cache_gqa_v"][b]).reshape(2, 256, 128)
            m["sf0"] = f32(inputs["state_ret_fwd"][b])
            m["sb0"] = f32(inputs["state_ret_bwd"][b])
        else:
            m["xin"] = np.ascontiguousarray(xp[core * 8:(core + 1) * 8].reshape(2048, 1024))
            m["cond"] = f32(inputs["c_ctx"])
            for nm, shp in [("c_ckv", (2, 256, 128)), ("c_kpe", (2, 256, 32)), ("c_wk", (2, 256, 128)),
                            ("c_wv", (2, 256, 128)), ("c_gk", (2, 256, 128)), ("c_gv", (2, 256, 128)),
                            ("sf0", (2, 4, 64, 64)), ("sb0", (2, 4, 64, 64))]:
                m[nm] = np.zeros(shp, np.float32)
        maps.append(m)
    return maps


def kernel(**inputs):
    nc, _ = _get_nc()
    maps = make_in_maps(inputs)
    res = run_bass_kernel_spmd(nc, maps, core_ids=list(range(8)))
    R = res.results
    y_prompt = np.concatenate([R[c]["y"].reshape(8, 256, 1024) for c in range(4)], axis=0)
    y_sample = np.stack([R[c]["y"] for c in range(4, 8)], axis=0)

    def cache(name, last):
        parts = []
        for c in range(4):
            a = R[c][name].reshape((2, 8, 256) + last)
            parts.append(np.moveaxis(a, 0, 1))
        return np.ascontiguousarray(np.concatenate(parts, axis=0))

    new_ckv = cache("o_ckv", (128,))
    new_kpe = cache("o_kpe", (32,))
    new_wk = cache("o_wk", (2, 64))
    new_wv = cache("o_wv", (2, 64))
    new_gk = cache("o_gk", (2, 64))
    new_gv = cache("o_gv", (2, 64))
    rf = np.ascontiguousarray(np.concatenate([np.moveaxis(R[c]["o_rf"], 0, 1) for c in range(4)], axis=0))
    rb = np.ascontiguousarray(np.concatenate([np.moveaxis(R[c]["o_rb"], 0, 1) for c in range(4)], axis=0))
    outs = (y_prompt, y_sample, new_ckv, new_kpe, new_wk, new_wv, new_gk, new_gv, rf, rb)
    return tuple(np.ascontiguousarray(o.astype(np.float32)) for o in outs)
```
